# Optimizing a Trainium2 kernel written in Bass

```python
import math
import jax, jax.numpy as jnp
from jax import lax
import numpy as np

D_MODEL = 1024
BATCH = 8
SEQ = 2048
DEPTH = 2
DEC_BATCH = 128
DEC_SEQ = 1
PAST_LEN = 16384
PAGE_SIZE = 128

HG_HEADS = 4
HG_DK = 128
HG_DV = 128
HG_WIDTH = HG_HEADS * HG_DK
GLA_CHUNK = 64
GM_GROUPS = 4
GM_CHUNK = 128
GM_WIDTH = 512
GM_GC = GM_WIDTH // GM_GROUPS
S5_WIDTH = 512
S5_GROUP = 16
S5_GROUPS = S5_WIDTH // S5_GROUP
S5_STATE = 64
N_BRANCH = 3
BRANCH_W = 512
D_FF = 2816
N_IN = 4 * HG_WIDTH + 2 * GM_WIDTH + S5_WIDTH + N_BRANCH * D_MODEL
EPS = 1e-6

kernel_name = 'hybrid_hgrn2_gmlp_s5_decoder_step'

F32 = jnp.float32


def rms_norm(x, g):
    xf = x.astype(F32)
    y = xf * lax.rsqrt(jnp.mean(xf * xf, axis=-1, keepdims=True) + EPS)
    return (y * g.astype(F32)).astype(x.dtype)


def layer_norm(x, g, b):
    xf = x.astype(F32)
    mu = jnp.mean(xf, axis=-1, keepdims=True)
    var = jnp.mean(jnp.square(xf - mu), axis=-1, keepdims=True)
    y = (xf - mu) * lax.rsqrt(var + EPS) * g.astype(F32) + b.astype(F32)
    return y.astype(x.dtype)


def swiglu(x, wg, wu, wd):
    return (jax.nn.silu(x @ wg) * (x @ wu)) @ wd


def gla_chunked(q, k, v, logf, s0):
    B, L, H, K = q.shape
    V = v.shape[-1]
    c = math.gcd(L, GLA_CHUNK)
    n = L // c

    def to_chunks(a):
        return a.reshape(B, n, c, H, a.shape[-1]).swapaxes(0, 1)

    mask = jnp.tril(jnp.ones((c, c), dtype=bool))[None, :, :, None, None]

    def step(S, inp):
        qc, kc, vc, gc = inp
        b = jnp.cumsum(gc, axis=1)
        b_last = b[:, -1]
        o_inter = jnp.einsum('bthk,bhkv->bthv', qc * jnp.exp(b), S)
        rel = jnp.where(mask, b[:, :, None] - b[:, None, :], -jnp.inf)
        scores = jnp.einsum('bthk,btshk,bshk->btsh', qc, jnp.exp(rel), kc)
        o_intra = jnp.einsum('btsh,bshv->bthv', scores, vc)
        S_new = jnp.exp(b_last)[..., None] * S + jnp.einsum(
            'bshk,bshv->bhkv', kc * jnp.exp(b_last[:, None] - b), vc)
        return S_new, o_inter + o_intra

    S_fin, o = lax.scan(step, s0, (to_chunks(q), to_chunks(k), to_chunks(v), to_chunks(logf)))
    return o.swapaxes(0, 1).reshape(B, L, H, V), S_fin


def hgrn2_mixer(xq, xf, xi, xg, s0, lb, norm_g):
    B, L, _ = xq.shape

    def heads(a):
        return a.astype(F32).reshape(B, L, HG_HEADS, -1)

    zf = heads(xf)
    lbh = lb.astype(F32).reshape(HG_HEADS, HG_DK)
    logf = jnp.logaddexp(jnp.log(lbh), jnp.log1p(-lbh) + jax.nn.log_sigmoid(zf))
    k = (1.0 - lbh) * jax.nn.sigmoid(-zf)
    o, S = gla_chunked(heads(xq), k, heads(xi), logf, s0.astype(F32))
    o = o * lax.rsqrt(jnp.mean(o * o, axis=-1, keepdims=True) + EPS)
    o = o * norm_g.astype(F32).reshape(HG_HEADS, HG_DV)
    o = o.reshape(B, L, HG_WIDTH) * jax.nn.silu(xg.astype(F32))
    return o.astype(xq.dtype), S


def gmlp_mixer(xu, xv, ws, bs, ng, nb):
    B, L, _ = xu.shape
    u = jax.nn.gelu(xu)
    v = layer_norm(jax.nn.gelu(xv), ng, nb)
    n = -(-L // GM_CHUNK)
    pad = n * GM_CHUNK - L
    vp = jnp.pad(v, ((0, 0), (0, pad), (0, 0))).reshape(B, n, GM_CHUNK, GM_GROUPS, GM_GC)
    wm = ws * jnp.tril(jnp.ones((GM_CHUNK, GM_CHUNK), dtype=ws.dtype))
    mix = jnp.einsum('gts,bnsgc->bntgc', wm, vp) + bs.T[None, None, :, :, None]
    mix = mix.reshape(B, n * GM_CHUNK, GM_WIDTH)[:, :L]
    return u * mix, v


def _ssm_combine(c1, c2):
    a1, b1 = c1
    a2, b2 = c2
    return a1 * a2, a2 * b1 + b2


def s5_mixer(u, h0_re, h0_im, lam_re, lam_im, log_dt, b_re, b_im, c_re, c_im, d_skip, glu_w, glu_b):
    Bsz, L, _ = u.shape
    uf = u.astype(F32).reshape(Bsz, L, S5_GROUPS, S5_GROUP)
    lam = lax.complex(lam_re.astype(F32), lam_im.astype(F32))
    dt = jnp.exp(log_dt.astype(F32))[:, None]
    abar = jnp.exp(lam * dt)
    bbar = ((abar - 1.0) / lam)[..., None] * lax.complex(b_re.astype(F32), b_im.astype(F32))
    bu = lax.complex(jnp.einsum('gph,blgh->blgp', jnp.real(bbar), uf),
                     jnp.einsum('gph,blgh->blgp', jnp.imag(bbar), uf))
    h0 = lax.complex(h0_re.astype(F32), h0_im.astype(F32))
    bu = bu.at[:, 0].add(abar * h0)
    a = jnp.broadcast_to(abar, bu.shape)
    _, h = lax.associative_scan(_ssm_combine, (a, bu), axis=1)
    y = (jnp.einsum('ghp,blgp->blgh', c_re.astype(F32), jnp.real(h))
         - jnp.einsum('ghp,blgp->blgh', c_im.astype(F32), jnp.imag(h)))
    y = y.reshape(Bsz, L, S5_WIDTH) + d_skip.astype(F32) * u.astype(F32)
    z = jax.nn.gelu(y).astype(u.dtype)
    zz = z @ glu_w + glu_b
    out = zz[..., :S5_WIDTH] * jax.nn.sigmoid(zz[..., S5_WIDTH:])
    h_last = h[:, -1]
    return out, jnp.real(h_last), jnp.imag(h_last)


def decoder_layer(x, s_hg, h_re, h_im, lb, norm_g, ffn_w_gate, ffn_w_up, ffn_w_down, w_in,
                  hgrn_norm_g, gmlp_ws, gmlp_bs, gmlp_norm_g, gmlp_norm_b,
                  s5_lam_re, s5_lam_im, s5_log_dt, s5_b_re, s5_b_im, s5_c_re, s5_c_im, s5_d,
                  s5_glu_w, s5_glu_b, w_branch, w_out):
    B, L, _ = x.shape
    h = rms_norm(x, norm_g[0])
    x = x + 0.5 * rms_norm(swiglu(h, ffn_w_gate[0], ffn_w_up[0], ffn_w_down[0]), norm_g[1])
    h = rms_norm(x, norm_g[2])
    proj = h @ w_in
    sizes = [HG_WIDTH] * 4 + [GM_WIDTH] * 2 + [S5_WIDTH] + [N_BRANCH * D_MODEL]
    xq, xf, xi, xg, gu, gv, su, gate_cols = jnp.split(proj, np.cumsum(sizes)[:-1].tolist(), axis=-1)
    a_out, s_hg_new = hgrn2_mixer(xq, xf, xi, xg, s_hg, lb, hgrn_norm_g)
    b_out, v_rows = gmlp_mixer(gu, gv, gmlp_ws, gmlp_bs, gmlp_norm_g, gmlp_norm_b)
    c_out, h_re_new, h_im_new = s5_mixer(su, h_re, h_im, s5_lam_re, s5_lam_im, s5_log_dt,
                                         s5_b_re, s5_b_im, s5_c_re, s5_c_im, s5_d, s5_glu_w, s5_glu_b)
    branches = jnp.stack([a_out, b_out, c_out], axis=2)
    bproj = jnp.einsum('blnw,nwd->blnd', branches, w_branch)
    gates = jax.nn.sigmoid(gate_cols).reshape(B, L, N_BRANCH, D_MODEL)
    merged = jnp.einsum('blnd,blnd->bld', gates, bproj)
    x = x + rms_norm(merged @ w_out, norm_g[3])
    h = rms_norm(x, norm_g[4])
    x = x + 0.5 * rms_norm(swiglu(h, ffn_w_gate[1], ffn_w_up[1], ffn_w_down[1]), norm_g[5])
    return x, s_hg_new, h_re_new, h_im_new, v_rows


def setup_inputs(seed: int = 0) -> dict:
    key = jax.random.key(seed)
    ks = jax.random.split(key, 32)

    def nrm(k, shape, scale):
        return scale * jax.random.normal(k, shape, dtype=F32)

    P = S5_STATE
    G = S5_GROUPS
    return {
        'x_prompt': nrm(ks[0], (BATCH, SEQ, D_MODEL), 1.0),
        'x_sample': nrm(ks[1], (DEC_BATCH, DEC_SEQ, D_MODEL), 1.0),
        'state_hgrn': nrm(ks[2], (DEPTH, DEC_BATCH, HG_HEADS, HG_DK, HG_DV), 0.3),
        'state_s5_re': nrm(ks[3], (DEPTH, DEC_BATCH, G, P), 0.1),
        'state_s5_im': nrm(ks[4], (DEPTH, DEC_BATCH, G, P), 0.1),
        'norm_g': 1.0 + nrm(ks[5], (DEPTH, 6, D_MODEL), 0.05),
        'ffn_w_gate': nrm(ks[6], (DEPTH, 2, D_MODEL, D_FF), D_MODEL ** -0.5),
        'ffn_w_up': nrm(ks[7], (DEPTH, 2, D_MODEL, D_FF), D_MODEL ** -0.5),
        'ffn_w_down': nrm(ks[8], (DEPTH, 2, D_FF, D_MODEL), D_FF ** -0.5),
        'w_in': nrm(ks[9], (DEPTH, D_MODEL, N_IN), D_MODEL ** -0.5),
        'hgrn_lb_logits': 1.0 + nrm(ks[10], (DEPTH, HG_WIDTH), 0.1),
        'hgrn_norm_g': 1.0 + nrm(ks[11], (DEPTH, HG_WIDTH), 0.05),
        'gmlp_ws': nrm(ks[12], (DEPTH, GM_GROUPS, GM_CHUNK, GM_CHUNK), GM_CHUNK ** -0.5),
        'gmlp_bs': 1.0 + nrm(ks[13], (DEPTH, GM_GROUPS, GM_CHUNK), 0.01),
        'gmlp_norm_g': 1.0 + nrm(ks[14], (DEPTH, GM_WIDTH), 0.05),
        'gmlp_norm_b': nrm(ks[15], (DEPTH, GM_WIDTH), 0.01),
        's5_lam_re': -0.5 + nrm(ks[16], (DEPTH, G, P), 0.01),
        's5_lam_im': jnp.pi * jnp.arange(P, dtype=F32)[None, None, :] + nrm(ks[17], (DEPTH, G, P), 0.01),
        's5_log_dt': jax.random.uniform(ks[18], (DEPTH, G), dtype=F32,
                                        minval=math.log(1e-3), maxval=math.log(1e-1)),
        's5_b_re': nrm(ks[19], (DEPTH, G, P, S5_GROUP), (2 * S5_GROUP) ** -0.5),
        's5_b_im': nrm(ks[20], (DEPTH, G, P, S5_GROUP), (2 * S5_GROUP) ** -0.5),
        's5_c_re': nrm(ks[21], (DEPTH, G, S5_GROUP, P), (2 * P) ** -0.5),
        's5_c_im': nrm(ks[22], (DEPTH, G, S5_GROUP, P), (2 * P) ** -0.5),
        's5_d': nrm(ks[23], (DEPTH, S5_WIDTH), 0.5),
        's5_glu_w': nrm(ks[24], (DEPTH, S5_WIDTH, 2 * S5_WIDTH), S5_WIDTH ** -0.5),
        's5_glu_b': nrm(ks[25], (DEPTH, 2 * S5_WIDTH), 0.01),
        'w_branch': nrm(ks[26], (DEPTH, N_BRANCH, BRANCH_W, D_MODEL), BRANCH_W ** -0.5),
        'w_out': nrm(ks[27], (DEPTH, D_MODEL, D_MODEL), D_MODEL ** -0.5),
    }


def reference(x_prompt, x_sample, state_hgrn, state_s5_re, state_s5_im, norm_g, ffn_w_gate, ffn_w_up,
              ffn_w_down, w_in, hgrn_lb_logits, hgrn_norm_g, gmlp_ws, gmlp_bs, gmlp_norm_g, gmlp_norm_b,
              s5_lam_re, s5_lam_im, s5_log_dt, s5_b_re, s5_b_im, s5_c_re, s5_c_im, s5_d, s5_glu_w,
              s5_glu_b, w_branch, w_out):
    lb_all = jnp.cumsum(jax.nn.softmax(hgrn_lb_logits.astype(F32), axis=0), axis=0)
    lb_all = lb_all - lb_all[:1]
    yp = x_prompt
    ys = x_sample
    hg_p, re_p, im_p, hg_s, re_s, im_s, v_s = [], [], [], [], [], [], []
    for l in range(DEPTH):
        p = (lb_all[l], norm_g[l], ffn_w_gate[l], ffn_w_up[l], ffn_w_down[l], w_in[l],
             hgrn_norm_g[l], gmlp_ws[l], gmlp_bs[l], gmlp_norm_g[l], gmlp_norm_b[l],
             s5_lam_re[l], s5_lam_im[l], s5_log_dt[l], s5_b_re[l], s5_b_im[l], s5_c_re[l], s5_c_im[l],
             s5_d[l], s5_glu_w[l], s5_glu_b[l], w_branch[l], w_out[l])
        zero_hg = jnp.zeros((BATCH, HG_HEADS, HG_DK, HG_DV), F32)
        zero_s5 = jnp.zeros((BATCH, S5_GROUPS, S5_STATE), F32)
        yp, s1, r1, i1, _ = decoder_layer(yp, zero_hg, zero_s5, zero_s5, *p)
        ys, s2, r2, i2, v2 = decoder_layer(ys, state_hgrn[l], state_s5_re[l], state_s5_im[l], *p)
        hg_p.append(s1)
        re_p.append(r1)
        im_p.append(i1)
        hg_s.append(s2)
        re_s.append(r2)
        im_s.append(i2)
        v_s.append(v2)
    return (yp, ys, jnp.stack(hg_p), jnp.stack(re_p), jnp.stack(im_p),
            jnp.stack(hg_s), jnp.stack(re_s), jnp.stack(im_s), jnp.stack(v_s))
```

```python
import contextlib
import numpy as np
import concourse.bass as bass
import concourse.mybir as mybir
from concourse.bass_utils import run_bass_kernel_spmd

F32 = mybir.dt.float32
BF16 = mybir.dt.bfloat16
AF = mybir.ActivationFunctionType
ALU = mybir.AluOpType

P = 128
D = 1024
DC = 8
DFF = 2816
FC = 22
NIN = 6656
SEQ = 2048
NPH = 1024
NS = 16
DEPTH = 2
EPS = 1e-6
NCORES = 8


class T:
    __slots__ = ("w", "r", "name")

    def __init__(self, name=""):
        self.w = None
        self.r = []
        self.name = name


class Sem:
    __slots__ = ("h", "count", "name")

    def __init__(self, h, name):
        self.h = h
        self.count = 0
        self.name = name


class Eng:
    def __init__(self, name, sem):
        self.name = name
        self.sem = sem
        self.seen = {}
        self.ops = []


class K:
    def __init__(self, nc, stack):
        self.nc = nc
        self.stack = stack
        self.engs = {}
        for n in ("pe", "act", "dve", "pool", "sp"):
            s = Sem(stack.enter_context(nc.semaphore("s_" + n)), n)
            self.engs[n] = Eng(n, s)
        self.maxwait = {}
        self.dsems = []

    def dsem(self, name):
        s = Sem(self.stack.enter_context(self.nc.semaphore("d_" + name)), name)
        self.dsems.append(s)
        return s

    def _need(self, eng, waits, ev):
        s, v = ev
        if eng.seen.get(id(s), 0) >= v:
            return
        eng.seen[id(s)] = v
        waits[id(s)] = (s, v)
        if v > self.maxwait.get(id(s), (s, 0))[1]:
            self.maxwait[id(s)] = (s, v)

    def _deps(self, eng, reads, writes, own_sem):
        waits = {}
        for t in reads:
            if t.w is not None:
                self._need(eng, waits, t.w)
        for t in writes:
            if t.w is not None and t.w[0] is not own_sem:
                self._need(eng, waits, t.w)
            for ev in t.r:
                if ev[0] is not own_sem:
                    self._need(eng, waits, ev)
        return list(waits.values())

    def op(self, en, fn, reads=(), writes=(), inc=True):
        eng = self.engs[en]
        waits = self._deps(eng, reads, writes, eng.sem if en == "pe" else None)
        ev = (eng.sem, eng.sem.count + 1)
        if inc:
            eng.sem.count += 1
        for t in reads:
            t.r.append(ev)
        for t in writes:
            t.w = ev
            t.r = []
        eng.ops.append((fn, waits, (eng.sem, 1) if inc else None))

    def dma(self, en, sem, fn, reads=(), writes=()):
        eng = self.engs[en]
        waits = self._deps(eng, reads, writes, None)
        sem.count += 16
        ev = (sem, sem.count)
        for t in reads:
            t.r.append(ev)
        for t in writes:
            t.w = ev
            t.r = []
        eng.ops.append((fn, waits, (sem, 16)))

    def barrier(self, exclude=()):
        evs = [(e.sem, e.sem.count) for e in self.engs.values() if e.sem.count > 0]
        evs += [(s, s.count) for s in self.dsems if s.count > 0]
        for en_, eng in self.engs.items():
            if en_ in exclude:
                continue
            waits = {}
            for ev in evs:
                if ev[0] is not eng.sem:
                    self._need(eng, waits, ev)
            if waits:
                eng.ops.append((None, list(waits.values()), None))

    def emit(self):
        nc = self.nc
        for s, v in self.maxwait.values():
            assert v <= s.count, f"wait on {s.name} for {v} > {s.count}"
        hmap = {"pe": "tensor", "act": "scalar", "dve": "vector", "pool": "gpsimd", "sp": "sync"}
        with nc.Block() as block:
            for en, eng in self.engs.items():
                def body(e, eng=eng):
                    for fn, waits, inc in eng.ops:
                        if fn is None:
                            for s, v in waits:
                                e.wait_ge(s.h, v)
                            continue
                        for s, v in waits[1:]:
                            e.wait_ge(s.h, v)
                        ins = fn(e)
                        if waits:
                            ins._wait_ge(waits[0][0].h, waits[0][1])
                        if inc is not None:
                            ins.then_inc(inc[0].h, inc[1])
                getattr(block, hmap[en])(body)


class Arena:
    def __init__(self, ap, nwords):
        self.ap = ap
        self.n = nwords
        self.off = 0
        self.peak = 0

    def reset(self):
        self.off = 0

    def f32(self, shape):
        n = int(np.prod(shape))
        a = self.ap[:, self.off:self.off + n]
        self.off += n
        self.peak = max(self.peak, self.off)
        assert self.off <= self.n, f"arena overflow {self.off} > {self.n}"
        if len(shape) == 2:
            return a.rearrange("p (a b) -> p a b", a=shape[0])
        if len(shape) == 3:
            return a.rearrange("p (a b c) -> p a b c", a=shape[0], b=shape[1])
        return a

    def bf16(self, shape):
        n = int(np.prod(shape))
        nw = (n + 1) // 2
        a = self.ap[:, self.off:self.off + nw].bitcast(BF16)[:, 0:n]
        self.off += nw
        self.peak = max(self.peak, self.off)
        assert self.off <= self.n, f"arena overflow {self.off} > {self.n}"
        if len(shape) == 2:
            return a.rearrange("p (a b) -> p a b", a=shape[0])
        if len(shape) == 3:
            return a.rearrange("p (a b c) -> p a b c", a=shape[0], b=shape[1])
        return a


def host_consts():
    c = {}
    c["ident"] = np.eye(128, dtype=np.float32)
    i = np.arange(128)
    c["mask128"] = (i[:, None] <= i[None, :]).astype(np.float32)
    sm = np.ones((128, 528), np.float32)
    sm[:, 0:512:64] = 0.0
    sm[:, 512:] = 0.0
    c["scanmask"] = sm
    pm = np.zeros((128, 4), np.float32)
    pm[:, 0] = ((i // 16) % 2 == 0)
    pm[:, 1] = ((i // 16) % 2 == 1)
    pm[:, 2] = np.where(i < 64, -1.0, 1.0)
    pm[:, 3] = -pm[:, 2]
    c["pmask"] = pm
    c["bdmask"] = ((i[:, None] // 16) == (i[None, :] // 16)).astype(np.float32)
    c["tri2"] = ((i[:, None] % 64) <= np.arange(64)[None, :]).astype(np.float32)
    return c


class Prog:
    def __init__(self, stop=None, dbg=None):
        self.stop = stop
        self.dbg = dbg
        nc = bass.Bass("TRN2", target_bir_lowering=False)
        self.nc = nc
        with contextlib.ExitStack() as st:
            self.st = st
            self.k = K(nc, st)
            self.declare_io()
            self.alloc()
            self.body()
            self.k.emit()

    def declare_io(self):
        nc = self.nc

        def din(name, shape):
            return nc.dram_tensor(name, list(shape), F32, kind="ExternalInput").ap()

        def dout(name, shape):
            return nc.dram_tensor(name, list(shape), F32, kind="ExternalOutput").ap()

        self.xp = din("xp", [SEQ, D])
        self.xs = din("xs", [NS, D])
        self.norm_g = din("norm_g", [DEPTH, 6, D])
        self.w_gate = din("ffn_w_gate", [DEPTH, 2, D, DFF])
        self.w_up = din("ffn_w_up", [DEPTH, 2, D, DFF])
        self.w_down = din("ffn_w_down", [DEPTH, 2, DFF, D])
        self.w_in = din("w_in", [DEPTH, D, NIN])
        self.gmlp_ws = din("gmlp_ws", [DEPTH, 4, 128, 128])
        self.gmlp_bs = din("gmlp_bs", [DEPTH, 4, 128])
        self.gmlp_ng = din("gmlp_norm_g", [DEPTH, 512])
        self.gmlp_nb = din("gmlp_norm_b", [DEPTH, 512])
        self.lb_logits = din("hgrn_lb_logits", [DEPTH, 512])
        self.hgrn_ng = din("hgrn_norm_g", [DEPTH, 512])
        self.shg = din("shg", [DEPTH, NS, 4, 128, 128])
        self.c_scanmask = din("c_scanmask", [128, 528])
        self.c_tri2 = din("c_tri2", [128, 64])
        self.lam_re = din("s5_lam_re", [DEPTH, 32, 64])
        self.lam_im = din("s5_lam_im", [DEPTH, 32, 64])
        self.log_dt = din("s5_log_dt", [DEPTH, 32])
        self.b_re = din("s5_b_re", [DEPTH, 32, 64, 16])
        self.b_im = din("s5_b_im", [DEPTH, 32, 64, 16])
        self.c_re = din("s5_c_re", [DEPTH, 32, 16, 64])
        self.c_im = din("s5_c_im", [DEPTH, 32, 16, 64])
        self.s5_d = din("s5_d", [DEPTH, 512])
        self.glu_w = din("s5_glu_w", [DEPTH, 512, 1024])
        self.glu_b = din("s5_glu_b", [DEPTH, 1024])
        self.w_branch = din("w_branch", [DEPTH, 3, 512, D])
        self.w_out = din("w_out", [DEPTH, D, D])
        self.sre = din("sre", [DEPTH, NS, 32, 64])
        self.sim = din("sim", [DEPTH, NS, 32, 64])
        self.s5scr = nc.dram_tensor("s5scr", [DEPTH, 4, 128, 5120], BF16, kind="Internal").ap()
        self.t_scr = [[T(f"scr{l}{c}") for c in range(4)] for l in range(DEPTH)]
        self.s5rot = nc.dram_tensor("s5rot", [DEPTH, 128, 3, 32, 128], F32, kind="Internal").ap()
        self.t_rot = [T(f"rot{l}") for l in range(DEPTH)]
        self.c_pmask = din("c_pmask", [128, 4])
        self.c_bdmask = din("c_bdmask", [128, 128])
        self.c_ident = din("c_ident", [128, 128])
        self.c_mask128 = din("c_mask128", [128, 128])
        self.yp = dout("yp", [SEQ, D])
        self.ys = dout("ys", [NS, D])
        self.vs_out = dout("vs", [DEPTH, NS, 512])
        self.rep_out = dout("rep", [DEPTH, 32, 64])
        self.imp_out = dout("imp", [DEPTH, 32, 64])
        self.res_out = dout("res", [DEPTH, NS, 32, 64])
        self.ims_out = dout("ims", [DEPTH, NS, 32, 64])
        self.hgp_out = dout("hgp", [DEPTH, 4, 128, 128])
        self.hgs_out = dout("hgs", [DEPTH, NS, 4, 128, 128])
        if self.dbg:
            self.dbg_out = dout("dbg", [128, self.dbg])
            self.dbg_off = 0
            self.dbg_map = {}

    def alloc(self):
        nc, st, k = self.nc, self.st, self.k

        self.sb_bytes = {}

        def sb(name, shape, dt):
            self.sb_bytes[name] = int(np.prod(shape[1:])) * (2 if dt == BF16 else 4)
            return st.enter_context(nc.sbuf_tensor(name, list(shape), dt))

        NCM = NPH + NS
        self.xT = sb("xT", [P, DC, NCM], F32)
        self.hT = sb("hT", [P, DC, NCM], BF16)
        self.t_x = [T(f"x{c}") for c in range(3)]
        self.t_h = [T(f"h{c}") for c in range(3)]
        self.NSLOT = 3
        self.SLOTW = 4096
        self.ring = [sb(f"ring{i}", [P, self.SLOTW], BF16) for i in range(self.NSLOT)]
        self.t_ring = [T(f"ring{i}") for i in range(self.NSLOT)]
        self.s_ring = [k.dsem(f"ring{i}") for i in range(self.NSLOT)]
        self.ring_i = 0
        self.ARENA_W = 22 * 1024
        ar = sb("arena", [P, self.ARENA_W], F32)
        self.arena = Arena(ar, self.ARENA_W)
        self.ident = sb("ident", [P, P], F32)
        self.identb = sb("identb", [P, P], BF16)
        self.onesb = sb("onesb", [P, P], BF16)
        self.gstage = sb("gstage", [P, P], F32)
        self.gcol = sb("gcol", [P, 96], F32)
        self.sq = [sb(f"sq{i}", [P, 512], BF16) for i in range(3)]
        self.t_sq = [T(f"sq{i}") for i in range(3)]
        self.sq_i = 0
        self.rs = [sb(f"rs{i}", [P, 512], F32) for i in range(3)]
        self.t_rs = [T(f"rs{i}") for i in range(3)]
        self.rs_i = 0
        self.sg = [sb(f"sg{i}", [P, 512], BF16) for i in range(3)]
        self.t_sg = [T(f"sg{i}") for i in range(3)]
        self.sg_i = 0
        self.t_const = T("const")
        self.s_const = k.dsem("const")
        self.s_vs = k.dsem("vs")
        self.s_bsr = k.dsem("bsr")
        self.t_stage_in = [T(f"stage{i}") for i in range(4)]
        self.s_stage_in = [k.dsem(f"stin{i}") for i in range(4)]
        self.t_stage_out = [T(f"ostage{i}") for i in range(2)]
        self.s_stage_out = [k.dsem(f"stout{i}") for i in range(2)]
        self.out_sems = list(self.s_stage_out) + [self.s_vs]
        self.wmT = sb("wmT", [P, DEPTH, 4, 128], BF16)
        self.ngB = sb("ngB", [P, DEPTH, 512], F32)
        self.nbB = sb("nbB", [P, DEPTH, 512], F32)
        self.onesf = sb("onesf", [NS, 128], F32)
        self.S32 = sb("S32", [P, DEPTH, 4, 128], F32)
        self.Sb = sb("Sb", [P, DEPTH, 4, 128], BF16)
        self.t_S32 = [[T(f"S32_{l}{hh}") for hh in range(4)] for l in range(DEPTH)]
        self.t_Sb = [[T(f"Sb_{l}{hh}") for hh in range(4)] for l in range(DEPTH)]
        self.hcol = sb("hcol", [P, 16], F32)
        self.lbc = sb("lbc", [P, DEPTH, 4], F32)
        self.omlc = sb("omlc", [P, DEPTH, 4], F32)
        self.scanmask = sb("scanmask", [P, 528], F32)
        self.tri2 = sb("tri2", [P, 64], F32)
        self.s5tab = sb("s5tab", [P, DEPTH, 8, 2, 32], F32)
        self.s5coef = sb("s5coef", [P, DEPTH, 2, 32], F32)
        self.Hprev = sb("Hprev", [P, DEPTH, 2, 32], F32)
        self.Hnew = sb("Hnew", [P, 2, 32], F32)
        self.t_Hnew = T("Hnew")
        self.pmask = sb("pmask", [P, 4], F32)
        self.bdmask = sb("bdmask", [P, P], F32)
        self.halfpi = sb("halfpi", [P, 1], F32)
        self.scol = sb("scol", [P, 24], F32)
        self.s_s5in = [k.dsem(f"s5in{i}") for i in range(3)]
        self.s_s5x = [k.dsem(f"s5x{i}") for i in range(2)]
        self.s_s5ck = k.dsem("s5ck")
        self.s_s5p = k.dsem("s5p")
        self.s_s5s = k.dsem("s5s")
        self.out_sems += [self.s_s5p, self.s_s5s]
        self.s_hg = [k.dsem(f"hg{i}") for i in range(2)]
        self.s_hgp = k.dsem("hgp")
        self.out_sems += self.s_hg + [self.s_hgp]
        self.wsc = sb("wsc", [P, DEPTH, 4], F32)
        self.bsc = sb("bsc", [P, DEPTH, 4], F32)
        self.mask128 = sb("mask128", [P, P], F32)
        self.st6 = [sb(f"st6_{i}", [P, 8], F32) for i in range(2)]
        self.t_st6 = [T(f"st6_{i}") for i in range(2)]
        self.st6_i = 0
        if self.dbg:
            self.dbgst = sb("dbgst", [P, 1040], F32)
            self.t_dbgst = T("dbgst")
            self.s_dbg = k.dsem("dbg")
            self.out_sems.append(self.s_dbg)
        self.ps = [st.enter_context(nc.psum_tensor(f"ps{i}", [P, 512], F32)) for i in range(8)]
        self.t_ps = [T(f"ps{i}") for i in range(8)]
        self.ps_i = 0

    def psum(self, lo=0, hi=8):
        if not (lo <= self.ps_i < hi):
            self.ps_i = lo
        i = self.ps_i
        self.ps_i = lo + (i + 1 - lo) % (hi - lo)
        return self.ps[i], self.t_ps[i]

    def nxt(self, what):
        lst = getattr(self, what)
        tl = getattr(self, "t_" + what)
        i = getattr(self, what + "_i")
        setattr(self, what + "_i", (i + 1) % len(lst))
        return lst[i], tl[i]

    def cts(self, h):
        c = [(0, 0, 512), (1, 512, 512)]
        if h == 0:
            c.append((2, 1024, NS))
        return c

    def wslot(self):
        i = self.ring_i
        self.ring_i = (i + 1) % self.NSLOT
        return self.ring[i], self.t_ring[i], self.s_ring[i]

    def setup_consts(self):
        k = self.k
        tc = self.t_const
        k.dma("sp", self.s_const, lambda e: e.dma_start(out=self.ident[:], in_=self.c_ident[:, :]), writes=[tc])
        k.dma("sp", self.s_const, lambda e: e.dma_start(
            out=self.gstage[0:96, :], in_=self.norm_g.rearrange("l j (c p) -> (l j c) p", p=P)), writes=[tc])
        k.op("dve", lambda e: e.tensor_copy(out=self.identb[:], in_=self.ident[:]), reads=[tc], writes=[tc])
        k.op("dve", lambda e: e.memset(self.onesb[:], 1.0), writes=[tc])
        ps, tp = self.psum()
        k.op("pe", lambda e: e.transpose(ps[:, 0:96], self.gstage[0:96, :], self.ident[0:96, 0:96]),
             reads=[tc], writes=[tp])
        k.op("dve", lambda e: e.tensor_copy(out=self.gcol[:], in_=ps[:, 0:96]), reads=[tp], writes=[tc])
        for l in range(DEPTH):
            for j in (1, 5):
                o = (l * 6 + j) * 8
                k.op("dve", lambda e, o=o: e.tensor_scalar(out=self.gcol[:, o:o + 8], in0=self.gcol[:, o:o + 8],
                                                           scalar1=0.5, scalar2=None, op0=ALU.mult),
                     reads=[tc], writes=[tc])

    def setup_gmlp(self):
        k = self.k
        tc = self.t_const
        sc = self.s_const
        k.dma("act", sc, lambda e: e.dma_start(out=self.mask128[:], in_=self.c_mask128[:, :]), writes=[tc])
        k.op("dve", lambda e: e.memset(self.onesf[:], 1.0), writes=[tc])
        for l in range(DEPTH):
            def one_layer(l):
                k.dma("act", sc, lambda e: e.dma_start(out=self.ngB[:, l, :], in_=self.gmlp_ng[l:l + 1, :].partition_broadcast(P)),
                      writes=[tc])
                k.dma("act", sc, lambda e: e.dma_start(out=self.nbB[:, l, :], in_=self.gmlp_nb[l:l + 1, :].partition_broadcast(P)),
                      writes=[tc])
                for g in range(4):
                    def one_g(g):
                        k.dma("act", sc, lambda e: e.dma_start(
                            out=self.wsc[:, l, g:g + 1], in_=self.gmlp_ws[l, g, 0:1, 0:1].partition_broadcast(P)), writes=[tc])
                        k.dma("act", sc, lambda e: e.dma_start(
                            out=self.bsc[:, l, g:g + 1], in_=self.gmlp_bs[l, g:g + 1, 0:1].partition_broadcast(P)), writes=[tc])
                        tg = T("wstage")
                        sg_ = k.dsem(f"wst{l}{g}")
                        k.dma("act", sg_, lambda e: e.dma_start(out=self.gstage[:, :], in_=self.gmlp_ws[l, g, :, :]),
                              reads=[tc], writes=[tc, tg])
                        ps, tp = self.psum()
                        k.op("pe", lambda e: e.transpose(ps[:, 0:128], self.gstage[:, :], self.ident[:]),
                             reads=[tc, tg], writes=[tp])
                        k.op("dve", lambda e: e.tensor_tensor(out=self.wmT[:, l, g, :], in0=ps[:, 0:128], in1=self.mask128[:],
                                                              op=ALU.mult), reads=[tp, tc], writes=[tc])
                    one_g(g)
            one_layer(l)

    def setup_hgrn(self):
        k = self.k
        tc, sc = self.t_const, self.s_const
        k.dma("act", sc, lambda e: e.dma_start(out=self.scanmask[:], in_=self.c_scanmask[:, :]), writes=[tc])
        k.dma("act", sc, lambda e: e.dma_start(out=self.tri2[:], in_=self.c_tri2[:, :]), writes=[tc])
        tg = T("hstage")
        sg_ = k.dsem("hstage")
        k.dma("act", sg_, lambda e: e.dma_start(out=self.gstage[0:8, :], in_=self.lb_logits.rearrange("l (h p) -> (l h) p", p=P)),
              reads=[tc], writes=[tc, tg])
        k.dma("act", sg_, lambda e: e.dma_start(out=self.gstage[8:16, :], in_=self.hgrn_ng.rearrange("l (h p) -> (l h) p", p=P)),
              writes=[tg])
        ps, tp = self.psum()
        k.op("pe", lambda e: e.transpose(ps[:, 0:16], self.gstage[0:16, :], self.ident[0:16, 0:16]), reads=[tc, tg], writes=[tp])
        k.op("dve", lambda e: e.tensor_copy(out=self.hcol[:], in_=ps[:, 0:16]), reads=[tp], writes=[tc])
        k.op("dve", lambda e: e.memset(self.lbc[:, 0, :], 0.0), writes=[tc])
        k.op("dve", lambda e: e.tensor_tensor(out=self.lbc[:, 1, :], in0=self.hcol[:, 4:8], in1=self.hcol[:, 0:4], op=ALU.subtract),
             reads=[tc], writes=[tc])
        k.op("act", lambda e: e.activation(out=self.lbc[:, 1, :], in_=self.lbc[:, 1, :], func=AF.Sigmoid), reads=[tc], writes=[tc])
        k.op("dve", lambda e: e.tensor_scalar(out=self.omlc[:], in0=self.lbc[:], scalar1=-1.0, scalar2=1.0, op0=ALU.mult, op1=ALU.add),
             reads=[tc], writes=[tc])
        k.op("dve", lambda e: e.memset(self.S32[:], 0.0), writes=[t for l in range(DEPTH) for t in self.t_S32[l]])
        k.op("dve", lambda e: e.memset(self.Sb[:], 0.0), writes=[t for l in range(DEPTH) for t in self.t_Sb[l]])

    def hgrn_tile(self, l, h, ci, aout, t_a):
        k = self.k
        ar = self.arena
        c0 = ci * 512
        samp = (h == 0 and ci == 1)
        NW = 512 + (NS if samp else 0)
        NWA = 512 + NS
        ABC = [(ar.f32([NWA])[:, 0:NW], ar.f32([NWA])[:, 0:NW], ar.f32([NWA])[:, 0:NW]) for _ in range(2)]
        tABC = [(self.pt(f"A{i}"), self.pt(f"Bk{i}"), self.pt(f"C{i}")) for i in range(2)]
        QB = ar.bf16([4, 512]); KBh = ar.bf16([4, 512]); KD = ar.bf16([4, 512])
        tQB = [self.pt(f"QB{i}") for i in range(4)]; tKB = [self.pt(f"KB{i}") for i in range(4)]; tKD = [self.pt(f"KD{i}") for i in range(4)]
        ebl = ar.f32([4, 8]); t_ebl = [self.pt(f"ebl{i}") for i in range(4)]
        vtok = ar.bf16([4, 512]); t_vtok = [self.pt(f"hv{i}") for i in range(4)]
        kdtok = ar.bf16([4, 512]); t_kdtok = [self.pt(f"kdt{i}") for i in range(4)]
        AmT = ar.bf16([4, 4 * 64]); t_AmT = [self.pt(f"AmT{i}") for i in range(4)]
        sgh = ar.bf16([4, NWA]); t_sgh = [self.pt(f"sgh{i}") for i in range(4)]
        O = [ar.f32([512]) for _ in range(2)]; tO = [self.pt("O0"), self.pt("O1")]
        if samp:
            f_s = ar.f32([4, NS]); kk_s = ar.f32([4, NS]); q_s = ar.f32([4, NS])
            t_fs, t_ks, t_qs = self.pt("f_s"), self.pt("kk_s"), self.pt("q_s")
            v_s = ar.f32([512]); t_vs = self.pt("hv_s")
            vmask = [ar.f32([512]) for _ in range(2)]; t_vm = [self.pt("vm0"), self.pt("vm1")]
            Sj = [ar.f32([4, 128]) for _ in range(2)]; t_Sj = [self.pt("Sj0"), self.pt("Sj1")]
        svf, tsf = self.wblock(l, 512)
        svq, tsq_ = self.wblock(l, 0)

        def head(hh):
            A, Bk, C = ABC[hh % 2]
            tA, tB, tC = tABC[hh % 2]
            ps, tp = self.psum(0, 8)
            self.proj_fm(svf, tsf, hh, ci, c0, 512, ps, tp)
            k.op("act", lambda e: e.activation(out=A[:, 0:512], in_=ps[:, 0:512], func=AF.Sigmoid), reads=[tp], writes=[tA])
            if samp:
                ps2, tp2 = self.psum(0, 8)
                self.proj_fm(svf, tsf, hh, 2, NPH, NS, ps2, tp2)
                k.op("act", lambda e: e.activation(out=A[:, 512:NW], in_=ps2[:, 0:NS], func=AF.Sigmoid), reads=[tp2], writes=[tA])
            k.op("dve", lambda e: e.tensor_scalar(out=A[:, :], in0=A[:, :], scalar1=self.omlc[:, l, hh:hh + 1],
                                                  scalar2=self.lbc[:, l, hh:hh + 1], op0=ALU.mult, op1=ALU.add),
                 reads=[tA, self.t_const], writes=[tA])
            k.op("dve", lambda e: e.tensor_scalar(out=Bk[:, :], in0=A[:, :], scalar1=-1.0, scalar2=1.0, op0=ALU.mult, op1=ALU.add),
                 reads=[tA], writes=[tB])
            if samp:
                k.op("dve", lambda e: e.tensor_copy(out=f_s[:, hh, :], in_=A[:, 512:NW]), reads=[tA], writes=[t_fs])
                k.op("dve", lambda e: e.tensor_copy(out=kk_s[:, hh, :], in_=Bk[:, 512:NW]), reads=[tB], writes=[t_ks])
            k.op("act", lambda e: e.activation(out=C[:, :], in_=A[:, :], func=AF.Ln), reads=[tA], writes=[tC])
            k.op("dve", lambda e: e.tensor_tensor_scan(out=A[:, :], data0=self.scanmask[:, 0:NW], data1=C[:, :], initial=0.0,
                                                       op0=ALU.mult, op1=ALU.add), reads=[tC, self.t_const], writes=[tA])
            k.op("act", lambda e: e.activation(out=C[:, :], in_=A[:, :], func=AF.Exp), reads=[tA], writes=[tC])
            k.op("act", lambda e: e.activation(out=A[:, :], in_=A[:, :], func=AF.Exp, scale=-1.0), reads=[tA], writes=[tA])
            k.op("dve", lambda e: e.tensor_tensor(out=Bk[:, 0:512], in0=Bk[:, 0:512], in1=A[:, 0:512], op=ALU.mult),
                 reads=[tB, tA], writes=[tB])
            k.op("act", lambda e: e.activation(out=KBh[:, hh, :], in_=Bk[:, 0:512], func=AF.Copy), reads=[tB], writes=[tKB[hh]])
            k.op("dve", lambda e: e.tensor_tensor(
                out=KD[:, hh, :].rearrange("p (c t) -> p c t", t=64), in0=Bk[:, 0:512].rearrange("p (c t) -> p c t", t=64),
                in1=C[:, 0:512].rearrange("p (c t) -> p c t", t=64)[:, :, 63:64].to_broadcast([P, 8, 64]), op=ALU.mult),
                reads=[tB, tC], writes=[tKD[hh]])
            k.op("dve", lambda e: e.tensor_copy(out=ebl[:, hh, :], in_=C[:, 0:512].rearrange("p (c t) -> p c t", t=64)[:, :, 63]),
                 reads=[tC], writes=[t_ebl[hh]])
            psq, tpq = self.psum(0, 8)
            self.proj_fm(svq, tsq_, hh, ci, c0, 512, psq, tpq)
            k.op("dve", lambda e: e.tensor_tensor(out=QB[:, hh, :], in0=psq[:, 0:512], in1=C[:, 0:512], op=ALU.mult),
                 reads=[tpq, tC], writes=[tQB[hh]])
            if samp:
                psq2, tpq2 = self.psum(0, 8)
                self.proj_fm(svq, tsq_, hh, 2, NPH, NS, psq2, tpq2)
                k.op("dve", lambda e: e.tensor_copy(out=q_s[:, hh, :], in_=psq2[:, 0:NS]), reads=[tpq2], writes=[t_qs])
        for hh in range(4):
            head(hh)
        svi, tsi = self.wblock(l, 1024)

        def vslice(s4):
            ps, tp = self.psum(0, 8)
            self.proj_tm(svi, tsi, ci, c0 + s4 * 128, 128, ps, tp)
            k.op("act", lambda e: e.activation(out=vtok[:, s4, :], in_=ps[:, :], func=AF.Copy), reads=[tp], writes=[t_vtok[s4]])
        for s4 in range(4):
            vslice(s4)
        if samp:
            ps, tp = self.psum(0, 8)
            self.proj_tm(svi, tsi, 2, NPH, NS, ps, tp)
            k.op("act", lambda e: e.activation(out=v_s[0:NS, :], in_=ps[0:NS, :], func=AF.Copy), reads=[tp], writes=[t_vs])
        svg, tsg_ = self.wblock(l, 1536)

        def ghead(hh):
            ps, tp = self.psum(0, 8)
            self.proj_fm(svg, tsg_, hh, ci, c0, 512, ps, tp)
            k.op("act", lambda e: e.activation(out=sgh[:, hh, 0:512], in_=ps[:, 0:512], func=AF.Silu), reads=[tp], writes=[t_sgh[hh]])
            if samp:
                ps2, tp2 = self.psum(0, 8)
                self.proj_fm(svg, tsg_, hh, 2, NPH, NS, ps2, tp2)
                k.op("act", lambda e: e.activation(out=sgh[:, hh, 512:NW], in_=ps2[:, 0:NS], func=AF.Silu), reads=[tp2], writes=[t_sgh[hh]])
        for hh in range(4):
            ghead(hh)

        def kdT(s4):
            ps, tp = self.psum(0, 8)
            psb = ps[:, 0:256].bitcast(BF16)
            for hh in range(4):
                k.op("pe", lambda e, hh=hh: e.transpose(psb[:, hh * 128:(hh + 1) * 128], KD[:, hh, s4 * 128:(s4 + 1) * 128],
                                                        self.identb[:]),
                     reads=[tKD[hh], self.t_const], writes=[tp], inc=(hh == 3))
            k.op("act", lambda e: e.activation(out=kdtok[:, s4, :], in_=psb[:, :], func=AF.Copy), reads=[tp], writes=[t_kdtok[s4]])
        for s4 in range(4):
            kdT(s4)

        def scores(hh):
            ps, tp = self.psum(0, 8)
            for c in range(8):
                b0 = (c % 2) * 64
                pr = c // 2
                k.op("pe", lambda e, c=c, b0=b0, pr=pr: e.matmul(ps[b0:b0 + 64, pr * 64:(pr + 1) * 64],
                                                                 lhsT=KBh[:, hh, c * 64:(c + 1) * 64], rhs=QB[:, hh, c * 64:(c + 1) * 64],
                                                                 start=True, stop=True),
                     reads=[tKB[hh], tQB[hh]], writes=[tp], inc=(c == 7))
            k.op("dve", lambda e: e.tensor_tensor(out=AmT[:, hh, :].rearrange("p (a t) -> p a t", t=64),
                                                  in0=ps[:, 0:256].rearrange("p (a t) -> p a t", t=64),
                                                  in1=self.tri2[:, :].unsqueeze(1).to_broadcast([P, 4, 64]), op=ALU.mult),
                 reads=[tp, self.t_const], writes=[t_AmT[hh]])
        for hh in range(4):
            scores(hh)
        po = [(self.ps[hh], self.t_ps[hh]) for hh in range(4)]
        psu_i = [0]

        def chunk(c, hh):
            b0 = (c % 2) * 64
            s4 = c // 2
            pO, tpO = po[hh]
            k.op("pe", lambda e: e.matmul(pO[:, c * 64:(c + 1) * 64], lhsT=self.Sb[:, l, hh, :], rhs=QB[:, hh, c * 64:(c + 1) * 64],
                                          start=True, stop=False), reads=[self.t_Sb[l][hh], tQB[hh]], writes=[tpO], inc=False)
            k.op("pe", lambda e: e.matmul(pO[:, c * 64:(c + 1) * 64], lhsT=vtok[b0:b0 + 64, s4, hh * 128:(hh + 1) * 128],
                                          rhs=AmT[b0:b0 + 64, hh, s4 * 64:(s4 + 1) * 64], start=False, stop=True),
                 reads=[t_vtok[s4], t_AmT[hh]], writes=[tpO], inc=True)
            bi = 4 + (psu_i[0] % 2)
            psu_i[0] += 1
            pU, tpU = self.ps[bi], self.t_ps[bi]
            k.op("pe", lambda e: e.matmul(pU[:, 0:128], lhsT=kdtok[b0:b0 + 64, s4, hh * 128:(hh + 1) * 128],
                                          rhs=vtok[b0:b0 + 64, s4, hh * 128:(hh + 1) * 128], start=True, stop=True),
                 reads=[t_kdtok[s4], t_vtok[s4]], writes=[tpU], inc=True)
            k.op("dve", lambda e: e.scalar_tensor_tensor(out=self.S32[:, l, hh, :], in0=self.S32[:, l, hh, :], scalar=ebl[:, hh, c:c + 1],
                                                         in1=pU[:, 0:128], op0=ALU.mult, op1=ALU.add),
                 reads=[self.t_S32[l][hh], t_ebl[hh], tpU], writes=[self.t_S32[l][hh]])
            k.op("act", lambda e: e.activation(out=self.Sb[:, l, hh, :], in_=self.S32[:, l, hh, :], func=AF.Copy),
                 reads=[self.t_S32[l][hh]], writes=[self.t_Sb[l][hh]])
        for c in range(8):
            for hh in range(4):
                chunk(c, hh)

        def onorm(hh, pO, tpO, n, acol0, scol0):
            o, to = O[hh % 2], tO[hh % 2]
            k.op("dve", lambda e: e.tensor_copy(out=o[:, 0:n], in_=pO[:, 0:n]), reads=[tpO], writes=[to])
            sq, tsq = self.nxt("sq")
            k.op("act", lambda e: e.activation(out=sq[:, 0:n], in_=o[:, 0:n], func=AF.Square), reads=[to], writes=[tsq])
            pn, tpn = self.psum(6, 8)
            k.op("pe", lambda e: e.matmul(pn[:, 0:n], lhsT=self.onesb[:], rhs=sq[:, 0:n], start=True, stop=True),
                 reads=[tsq, self.t_const], writes=[tpn])
            rs, trs = self.rstd_from(pn, tpn, n, 128)
            k.op("dve", lambda e: e.tensor_tensor(out=o[:, 0:n], in0=o[:, 0:n], in1=rs[:, 0:n], op=ALU.mult), reads=[to, trs], writes=[to])
            k.op("dve", lambda e: e.scalar_tensor_tensor(out=aout[:, hh, acol0:acol0 + n], in0=o[:, 0:n],
                                                         scalar=self.hcol[:, 8 + l * 4 + hh:9 + l * 4 + hh],
                                                         in1=sgh[:, hh, scol0:scol0 + n], op0=ALU.mult, op1=ALU.mult),
                 reads=[to, self.t_const, t_sgh[hh]], writes=[t_a[2 if acol0 >= NPH else ci]])
        for hh in range(4):
            onorm(hh, po[hh][0], po[hh][1], 512, c0, 0)
        if samp:
            pso, tpso = self.ps[5], self.t_ps[5]

            def sample(j):
                sj, tsj, ssj = Sj[j % 2], t_Sj[j % 2], self.s_hg[j % 2]
                vm, tvm = vmask[j % 2], t_vm[j % 2]
                k.dma("sp", ssj, lambda e: e.dma_start(out=sj[:, :, :], in_=self.shg[l, j].rearrange("h k v -> k h v")), writes=[tsj])
                k.op("dve", lambda e: e.tensor_scalar(out=vm[0:NS, :], in0=v_s[0:NS, :], scalar1=self.ident[0:NS, j:j + 1], scalar2=None,
                                                      op0=ALU.mult), reads=[t_vs, self.t_const], writes=[tvm])
                pv, tpv = self.psum(6, 8)
                k.op("pe", lambda e: e.matmul(pv[:, :], lhsT=self.onesf[0:NS, :], rhs=vm[0:NS, :], start=True, stop=True),
                     reads=[tvm, self.t_const], writes=[tpv])
                for hh in range(4):
                    k.op("dve", lambda e, hh=hh: e.tensor_scalar(out=sj[:, hh, :], in0=sj[:, hh, :], scalar1=f_s[:, hh, j:j + 1],
                                                                 scalar2=None, op0=ALU.mult), reads=[tsj, t_fs], writes=[tsj])
                    k.op("dve", lambda e, hh=hh: e.scalar_tensor_tensor(out=sj[:, hh, :], in0=pv[:, hh * 128:(hh + 1) * 128],
                                                                        scalar=kk_s[:, hh, j:j + 1], in1=sj[:, hh, :],
                                                                        op0=ALU.mult, op1=ALU.add),
                         reads=[tpv, t_ks, tsj], writes=[tsj])
                k.dma("sp", ssj, lambda e: e.dma_start(out=self.hgs_out[l, j].rearrange("h k v -> k h v"), in_=sj[:, :, :]), reads=[tsj])
                for hh in range(4):
                    k.op("pe", lambda e, hh=hh: e.matmul(pso[:, hh * NS + j:hh * NS + j + 1], lhsT=sj[:, hh, :], rhs=q_s[:, hh, j:j + 1],
                                                         start=True, stop=True), reads=[tsj, t_qs], writes=[tpso], inc=(hh == 3))
            for j in range(NS):
                sample(j)
            for hh in range(4):
                onorm(hh, self.ps[5][:, hh * NS:(hh + 1) * NS], tpso, NS, NPH, 512)

    PW = [1, 2, 3, 4, 5, 6, 7, 8]

    def tab(self, l, m, which):
        return self.s5tab[:, l, self.PW.index(m), which, :]

    def setup_s5(self):
        k = self.k
        tc, sc = self.t_const, self.s_const
        nc, st = self.nc, self.st

        def sbt(name, shape):
            return st.enter_context(nc.sbuf_tensor(name, list(shape), F32))
        k.dma("sp", sc, lambda e: e.dma_start(out=self.pmask[:], in_=self.c_pmask[:, :]), writes=[tc])
        k.dma("sp", sc, lambda e: e.dma_start(out=self.bdmask[:], in_=self.c_bdmask[:, :]), writes=[tc])
        k.op("dve", lambda e: e.memset(self.halfpi[:], float(np.pi / 2)), writes=[tc])
        k.op("dve", lambda e: e.memset(self.Hprev[:], 0.0), writes=[tc])
        tg = T("s5stage")
        sg_ = k.dsem("s5stage")
        k.dma("sp", sg_, lambda e: e.dma_start(out=self.gstage[0:8, :], in_=self.s5_d.rearrange("l (c p) -> (l c) p", p=P)),
              reads=[tc], writes=[tc, tg])
        k.dma("sp", sg_, lambda e: e.dma_start(out=self.gstage[8:24, :], in_=self.glu_b.rearrange("l (c p) -> (l c) p", p=P)),
              writes=[tg])
        ps, tp = self.psum()
        k.op("pe", lambda e: e.transpose(ps[:, 0:24], self.gstage[0:24, :], self.ident[0:24, 0:24]), reads=[tc, tg], writes=[tp])
        k.op("dve", lambda e: e.tensor_copy(out=self.scol[:], in_=ps[:, 0:24]), reads=[tp], writes=[tc])
        self.arena.reset()
        W = [self.arena.f32([32]) for i in range(12)]
        lam_st = self.arena.f32([256])
        self.s5raw = self.arena.f32([DEPTH * 8 * 2, 32])
        self.s5unit = self.arena.f32([DEPTH * 7 * 2, 32])
        self.s5r8 = self.arena.f32([DEPTH, 32])

        def dv(fn):
            k.op("dve", fn, reads=[tc], writes=[tc])

        def av(fn):
            k.op("act", fn, reads=[tc], writes=[tc])

        def cmul(orr, oi, ar_, ai, br, bi, t0, t1):
            dv(lambda e: e.tensor_tensor(out=t0[:], in0=ar_, in1=br, op=ALU.mult))
            dv(lambda e: e.tensor_tensor(out=t1[:], in0=ai, in1=bi, op=ALU.mult))
            dv(lambda e: e.tensor_tensor(out=orr, in0=t0[:], in1=t1[:], op=ALU.subtract))
            dv(lambda e: e.tensor_tensor(out=t0[:], in0=ar_, in1=bi, op=ALU.mult))
            dv(lambda e: e.tensor_tensor(out=t1[:], in0=ai, in1=br, op=ALU.mult))
            dv(lambda e: e.tensor_tensor(out=oi, in0=t0[:], in1=t1[:], op=ALU.add))
        for l in range(DEPTH):
            def layer(l):
                LR, LI, DT, ER, CS, SN, T0, T1, T2, T3, PRn, PIn = W
                tl = T("lamst")
                sl_ = k.dsem(f"lamst{l}")
                for i, src in enumerate((self.lam_re, self.lam_re, self.lam_im, self.lam_im)):
                    k.dma("sp", sl_, lambda e, i=i, src=src: e.dma_start(out=lam_st[0:32, i * 64:(i + 1) * 64], in_=src[l, :, :]),
                          reads=[tc] if i == 0 else [], writes=[tc, tl] if i == 0 else [tl])
                k.dma("sp", sl_, lambda e: e.dma_start(out=DT[:], in_=self.log_dt[l:l + 1, :].partition_broadcast(P)), writes=[tl])
                ps, tp = self.psum()
                k.op("pe", lambda e: e.transpose(ps[:, 0:32], lam_st[0:32, 0:128], self.ident[0:32, 0:32]), reads=[tc, tl], writes=[tp], inc=False)
                k.op("pe", lambda e: e.transpose(ps[:, 32:64], lam_st[0:32, 128:256], self.ident[0:32, 0:32]), reads=[tc, tl], writes=[tp])
                k.op("dve", lambda e: e.tensor_copy(out=LR[:], in_=ps[:, 0:32]), reads=[tp], writes=[tc])
                k.op("dve", lambda e: e.tensor_copy(out=LI[:], in_=ps[:, 32:64]), reads=[tp], writes=[tc])
                k.op("act", lambda e: e.activation(out=DT[:], in_=DT[:], func=AF.Exp), reads=[tc, tl], writes=[tc])
                dv(lambda e: e.tensor_tensor(out=T0[:], in0=LR[:], in1=DT[:], op=ALU.mult))
                av(lambda e: e.activation(out=ER[:], in_=T0[:], func=AF.Exp))
                dv(lambda e: e.tensor_tensor(out=T0[:], in0=LI[:], in1=DT[:], op=ALU.mult))
                av(lambda e: e.activation(out=SN[:], in_=T0[:], func=AF.Sin, scale=1.0 / 16))
                av(lambda e: e.activation(out=CS[:], in_=T0[:], func=AF.Sin, scale=1.0 / 16, bias=self.halfpi[:, 0:1]))
                for _ in range(4):
                    dv(lambda e: e.tensor_tensor(out=T0[:], in0=CS[:], in1=CS[:], op=ALU.mult))
                    dv(lambda e: e.tensor_tensor(out=T1[:], in0=SN[:], in1=SN[:], op=ALU.mult))
                    dv(lambda e: e.scalar_tensor_tensor(out=SN[:], in0=CS[:], scalar=2.0, in1=SN[:], op0=ALU.mult, op1=ALU.mult))
                    dv(lambda e: e.tensor_tensor(out=CS[:], in0=T0[:], in1=T1[:], op=ALU.subtract))
                pr = {}
                pi = {}

                def store(m, prt, pit):
                    dv(lambda e: e.tensor_copy(out=self.tab(l, m, 0), in_=prt))
                    dv(lambda e: e.tensor_scalar(out=self.tab(l, m, 1), in0=pit, scalar1=self.pmask[:, 2:3], scalar2=None, op0=ALU.mult))
                def raw(m, c):
                    return self.s5raw[:, (l * 8 + self.PW.index(m)) * 2 + c, :]
                dv(lambda e: e.tensor_tensor(out=raw(1, 0), in0=ER[:], in1=CS[:], op=ALU.mult))
                dv(lambda e: e.tensor_tensor(out=raw(1, 1), in0=ER[:], in1=SN[:], op=ALU.mult))
                for m in (2, 3, 4, 5, 6, 7, 8):
                    cmul(raw(m, 0), raw(m, 1), raw(m - 1, 0), raw(m - 1, 1), raw(1, 0), raw(1, 1), T0, T1)
                for m in self.PW:
                    store(m, raw(m, 0), raw(m, 1))
                def unit(kk, c):
                    return self.s5unit[:, (l * 7 + kk) * 2 + c, :]
                dv(lambda e: e.tensor_tensor(out=T0[:], in0=LR[:], in1=DT[:], op=ALU.mult))
                av(lambda e: e.activation(out=self.s5r8[:, l, :], in_=T0[:], func=AF.Exp, scale=8.0))
                dv(lambda e: e.reciprocal(out=T1[:], in_=self.s5r8[:, l, :]))
                dv(lambda e: e.tensor_tensor(out=unit(0, 0), in0=raw(8, 0), in1=T1[:], op=ALU.mult))
                dv(lambda e: e.tensor_tensor(out=unit(0, 1), in0=raw(8, 1), in1=T1[:], op=ALU.mult))
                for kk in range(1, 7):
                    dv(lambda e, kk=kk: e.tensor_tensor(out=T0[:], in0=unit(kk - 1, 0), in1=unit(kk - 1, 0), op=ALU.mult))
                    dv(lambda e, kk=kk: e.tensor_tensor(out=T1[:], in0=unit(kk - 1, 1), in1=unit(kk - 1, 1), op=ALU.mult))
                    dv(lambda e, kk=kk: e.tensor_tensor(out=unit(kk, 0), in0=T0[:], in1=T1[:], op=ALU.subtract))
                    dv(lambda e, kk=kk: e.scalar_tensor_tensor(out=unit(kk, 1), in0=unit(kk - 1, 0), scalar=2.0, in1=unit(kk - 1, 1),
                                                               op0=ALU.mult, op1=ALU.mult))
                dv(lambda e: e.tensor_scalar(out=T2[:], in0=raw(1, 0), scalar1=-1.0, scalar2=None, op0=ALU.add))
                dv(lambda e: e.tensor_tensor(out=T0[:], in0=LR[:], in1=LR[:], op=ALU.mult))
                dv(lambda e: e.tensor_tensor(out=T1[:], in0=LI[:], in1=LI[:], op=ALU.mult))
                dv(lambda e: e.tensor_tensor(out=T3[:], in0=T0[:], in1=T1[:], op=ALU.add))
                dv(lambda e: e.reciprocal(out=T3[:], in_=T3[:]))
                dv(lambda e: e.tensor_tensor(out=T0[:], in0=T2[:], in1=LR[:], op=ALU.mult))
                dv(lambda e: e.tensor_tensor(out=T1[:], in0=raw(1, 1), in1=LI[:], op=ALU.mult))
                dv(lambda e: e.tensor_tensor(out=T0[:], in0=T0[:], in1=T1[:], op=ALU.add))
                dv(lambda e: e.tensor_tensor(out=self.s5coef[:, l, 0, :], in0=T0[:], in1=T3[:], op=ALU.mult))
                dv(lambda e: e.tensor_tensor(out=T0[:], in0=raw(1, 1), in1=LR[:], op=ALU.mult))
                dv(lambda e: e.tensor_tensor(out=T1[:], in0=T2[:], in1=LI[:], op=ALU.mult))
                dv(lambda e: e.tensor_tensor(out=T0[:], in0=T0[:], in1=T1[:], op=ALU.subtract))
                dv(lambda e: e.tensor_tensor(out=T0[:], in0=T0[:], in1=T3[:], op=ALU.mult))
                dv(lambda e: e.tensor_scalar(out=self.s5coef[:, l, 1, :], in0=T0[:], scalar1=self.pmask[:, 2:3], scalar2=None, op0=ALU.mult))
            layer(l)
        self.s5_pre = []
        for l in range(DEPTH):
            def preload(l):
                f32 = self.arena.f32
                bS = f32([32, 16]); bX = f32([32, 16]); Cn1 = f32([4, 128]); Cn2 = f32([4, 128])
                t_b = T(f"bSX{l}"); t_Cn = T(f"Cn{l}")
                sb_, sc_ = k.dsem(f"s5b{l}"), k.dsem(f"s5c{l}")
                bre = self.b_re[l].rearrange("g p h -> p g h")
                bim = self.b_im[l].rearrange("g p h -> p g h")
                for (dst, top, bot) in ((bS, bre, bim), (bX, bim, bre)):
                    k.dma("sp", sb_, lambda e, dst=dst, top=top: e.dma_start(out=dst[0:64, :, :], in_=top), writes=[t_b])
                    k.dma("sp", sb_, lambda e, dst=dst, bot=bot: e.dma_start(out=dst[64:128, :, :], in_=bot), writes=[t_b])
                cre = self.c_re[l].rearrange("(c g) h p -> (g h) c p", c=4)
                cim = self.c_im[l].rearrange("(c g) h p -> (g h) c p", c=4)
                for (dst, left, right) in ((Cn1, cre, cim), (Cn2, cim, cre)):
                    k.dma("sp", sc_, lambda e, dst=dst, left=left: e.dma_start(out=dst[:, :, 0:64], in_=left), writes=[t_Cn])
                    k.dma("sp", sc_, lambda e, dst=dst, right=right: e.dma_start(out=dst[:, :, 64:128], in_=right), writes=[t_Cn])
                self.s5_pre.append((bS, bX, Cn1, Cn2, t_b, t_Cn))
            preload(l)
        self.s5_setup_mark = self.arena.off
        for l in range(DEPTH):
            def consts_layer(l):
                k.barrier()
                self.arena.off = self.s5_setup_mark
                CT4 = self.arena.bf16([4, 5120]); t_CT = T("CTs")
                self.s5_consts(l, CT4, t_CT, self.s5_pre[l])
                k.dma("sp", self.s_s5in[2], lambda e: e.dma_start(out=self.s5scr[l].rearrange("c p f -> p c f"), in_=CT4), reads=[t_CT],
                      writes=self.t_scr[l])
            consts_layer(l)
        for l in range(DEPTH):
            def rot_layer(l):
                k.barrier()
                self.arena.off = self.s5_setup_mark
                TAB = self.arena.f32([3, 32 * 128])
                tA = self.arena.f32([32, 64])
                t_T = T("rotT")
                Tc = TAB[:, 0, :].rearrange("p (g n) -> p g n", g=32)
                Ts = TAB[:, 1, :].rearrange("p (g n) -> p g n", g=32)
                Rr = TAB[:, 2, :].rearrange("p (g n) -> p g n", g=32)

                def dv(fn):
                    k.op("dve", fn, reads=[tc, t_T], writes=[t_T])
                dv(lambda e: e.memset(Tc[:, :, 0:1], 1.0))
                dv(lambda e: e.memset(Ts[:, :, 0:1], 0.0))
                for kk in range(7):
                    def lvl(kk):
                        L = 1 << kk
                        ck = self.s5unit[:, (l * 7 + kk) * 2 + 0, :].unsqueeze(2).to_broadcast([P, 32, L])
                        sk = self.s5unit[:, (l * 7 + kk) * 2 + 1, :].unsqueeze(2).to_broadcast([P, 32, L])
                        dv(lambda e: e.tensor_tensor(out=Tc[:, :, L:2 * L], in0=Tc[:, :, 0:L], in1=ck, op=ALU.mult))
                        dv(lambda e: e.tensor_tensor(out=tA[:, :, 0:L], in0=Ts[:, :, 0:L], in1=sk, op=ALU.mult))
                        dv(lambda e: e.tensor_tensor(out=Tc[:, :, L:2 * L], in0=Tc[:, :, L:2 * L], in1=tA[:, :, 0:L], op=ALU.subtract))
                        dv(lambda e: e.tensor_tensor(out=Ts[:, :, L:2 * L], in0=Ts[:, :, 0:L], in1=ck, op=ALU.mult))
                        dv(lambda e: e.tensor_tensor(out=tA[:, :, 0:L], in0=Tc[:, :, 0:L], in1=sk, op=ALU.mult))
                        dv(lambda e: e.tensor_tensor(out=Ts[:, :, L:2 * L], in0=Ts[:, :, L:2 * L], in1=tA[:, :, 0:L], op=ALU.add))
                    lvl(kk)
                dv(lambda e: e.tensor_scalar(out=TAB[:, 1, :], in0=TAB[:, 1, :], scalar1=self.pmask[:, 3:4], scalar2=None, op0=ALU.mult))
                dv(lambda e: e.tensor_copy(out=Rr, in_=self.s5r8[:, l, :].unsqueeze(2).to_broadcast([P, 32, 128])))
                dv(lambda e: e.memset(Rr[:, :, 0:1], 0.0))
                k.dma("sp", self.s_s5in[2], lambda e: e.dma_start(out=self.s5rot[l].rearrange("p a g n -> p a (g n)"), in_=TAB), reads=[t_T],
                      writes=[self.t_rot[l]])
            rot_layer(l)

    def s5(self, l, h, cout, t_c):
        k = self.k
        ar = self.arena
        tc = self.t_const
        NCM = NPH + NS
        NB = 128
        samp = (h == 0)
        uP = ar.bf16([4, 8 * NB])
        us = ar.bf16([4, NS])
        z = ar.bf16([4, NCM])
        t_uP = [T(f"uP{i}") for i in range(4)]
        t_us = T("us")
        t_z = [[T(f"z{c4}_{ci}") for ci in range(3)] for c4 in range(4)]
        svu, tsu = self.wblock(l, 3072)

        def uproj(c4, ci):
            ps, tp = self.psum()
            self.proj_fm(svu, tsu, c4, ci, ci * 512, 512, ps, tp)
            k.op("act", lambda e: e.activation(
                out=uP[:, c4, :].rearrange("p (t n) -> p t n", t=8)[:, :, ci * 64:(ci + 1) * 64],
                in_=ps[:, 0:512].rearrange("p (n t) -> p t n", t=8), func=AF.Copy), reads=[tp], writes=[t_uP[c4]])
        for c4 in range(4):
            for ci in range(2):
                uproj(c4, ci)
        if samp:
            psU, tpsU = self.psum()
            for c4 in range(4):
                self.proj_fm(svu, tsu, c4, 2, NPH, NS, psU[:, c4 * NS:(c4 + 1) * NS], tpsU)
            k.op("act", lambda e: e.activation(out=us[:, :, :], in_=psU[:, 0:4 * NS].rearrange("p (a b) -> p a b", a=4), func=AF.Copy),
                 reads=[tpsU], writes=[t_us])
        hp_s = self.Hprev[:, l, 0, :]
        hp_x = self.Hprev[:, l, 1, :]
        hcs = ar.f32([32]); hcx = ar.f32([32]); htmp = ar.f32([32])
        t_hc = T("hc")
        if h == 1:
            k.op("dve", lambda e: e.tensor_tensor(out=hcs, in0=hp_s, in1=self.tab(l, 8, 0), op=ALU.mult), reads=[tc], writes=[t_hc])
            k.op("dve", lambda e: e.tensor_tensor(out=htmp, in0=hp_x, in1=self.tab(l, 8, 1), op=ALU.mult), reads=[tc], writes=[t_hc])
            k.op("dve", lambda e: e.tensor_tensor(out=hcs, in0=hcs, in1=htmp, op=ALU.add), reads=[t_hc], writes=[t_hc])
        H0s_all = H0x_all = Hns_all = None
        t_H0 = T("H0")
        if samp:
            H0s_all = ar.f32([NS, 32]); H0x_all = ar.f32([NS, 32]); Hns_all = ar.f32([NS, 32])
            Hst = [ar.f32([4, 128]) for _ in range(2)]
            t_Hst = T("Hst")
            for (dst, left, right) in ((Hst[0], self.sre, self.sim), (Hst[1], self.sim, self.sre)):
                k.dma("sp", self.s_s5in[2], lambda e, dst=dst, left=left: e.dma_start(
                    out=dst[:, :, 0:64], in_=left[l].rearrange("(a j4) g p -> (j4 g) a p", j4=4)), writes=[t_Hst])
                k.dma("sp", self.s_s5in[2], lambda e, dst=dst, right=right: e.dma_start(
                    out=dst[:, :, 64:128], in_=right[l].rearrange("(a j4) g p -> (j4 g) a p", j4=4)), writes=[t_Hst])
            for si, dsta in ((0, H0s_all), (1, H0x_all)):
                ps, tp = self.psum()
                for a in range(4):
                    k.op("pe", lambda e, a=a, si=si, ps=ps: e.transpose(ps[:, a * 128:(a + 1) * 128], Hst[si][:, a, :], self.ident[:]),
                         reads=[t_Hst, tc], writes=[tp], inc=(a == 3))
                k.op("dve", lambda e, ps=ps, dsta=dsta: e.tensor_copy(out=dsta.rearrange("p j g -> p (j g)"), in_=ps[:, :]),
                     reads=[tp], writes=[t_H0])
        mark = ar.off
        self._pt = {}
        k.barrier()
        args = (uP, us, z, t_uP, t_us, t_z, hcs, hcx, t_hc, H0s_all, H0x_all, Hns_all, t_H0)
        ar.off = mark
        self.s5_pass("A", l, h, 0, *args)
        for c4 in range(4):
            if c4 + 1 < 4:
                ar.off = mark
                self.s5_pass("A", l, h, c4 + 1, *args)
            ar.off = mark
            self.s5_pass("B", l, h, c4, *args)
        k.barrier()
        ar.off = mark
        if samp:
            psN, tpsN = self.psum()
            for a in range(4):
                k.op("pe", lambda e, a=a: e.transpose(psN[:, a * 128:(a + 1) * 128],
                                                      Hns_all.rearrange("p j g -> p (j g)")[:, a * 128:(a + 1) * 128], self.ident[:]),
                     reads=[t_H0, tc], writes=[tpsN], inc=(a == 3))
            k.op("dve", lambda e: e.tensor_copy(out=Hst[0].rearrange("p a c -> p (a c)"), in_=psN[:, :]), reads=[tpsN, t_Hst], writes=[t_Hst])
            so = self.s_s5s
            k.dma("sp", so, lambda e: e.dma_start(out=self.res_out[l].rearrange("(a j4) g p -> (j4 g) a p", j4=4), in_=Hst[0][:, :, 0:64]),
                  reads=[t_Hst])
            k.dma("sp", so, lambda e: e.dma_start(out=self.ims_out[l].rearrange("(a j4) g p -> (j4 g) a p", j4=4), in_=Hst[0][:, :, 64:128]),
                  reads=[t_Hst])
        k.op("dve", lambda e: e.tensor_copy(out=self.Hprev[:, l, :, :], in_=self.Hnew[:, :, :]), reads=[tc, self.t_Hnew], writes=[tc])
        if h == 1:
            psF, tpsF = self.psum()
            k.op("pe", lambda e: e.transpose(psF[0:32, 0:128], self.Hnew[:, 0, :], self.ident[:]), reads=[self.t_Hnew, tc], writes=[tpsF])
            hst = ar.f32([128]); t_hst = T("hst")
            k.op("dve", lambda e: e.tensor_copy(out=hst[0:32, :], in_=psF[0:32, 0:128]), reads=[tpsF], writes=[t_hst])
            k.dma("sp", self.s_s5p, lambda e: e.dma_start(out=self.rep_out[l, :, :], in_=hst[0:32, 0:64]), reads=[t_hst])
            k.dma("sp", self.s_s5p, lambda e: e.dma_start(out=self.imp_out[l, :, :], in_=hst[0:32, 64:128]), reads=[t_hst])
        gw = self.glu_w[l].rearrange("(kc kp) f -> kp kc f", kp=P)
        slA, tsA, ssA = self.wslot()
        svA = slA[:, 0:4 * 512].rearrange("p (kc f) -> p kc f", kc=4)
        k.dma("pool", ssA, lambda e: e.dma_start(out=svA, in_=gw[:, :, 0:512]), writes=[tsA])
        slB, tsB, ssB = self.wslot()
        svB = slB[:, 0:4 * 512].rearrange("p (kc f) -> p kc f", kc=4)
        k.dma("pool", ssB, lambda e: e.dma_start(out=svB, in_=gw[:, :, 512:1024]), writes=[tsB])

        def glu(oc, ci, c0, n):
            pA, tpA = self.psum()
            pB, tpB = self.psum()
            for kc in range(4):
                k.op("pe", lambda e, kc=kc: e.matmul(pA[:, 0:n], lhsT=svA[:, kc, oc * 128:(oc + 1) * 128], rhs=z[:, kc, c0:c0 + n],
                                                     start=(kc == 0), stop=(kc == 3)), reads=[tsA, t_z[kc][ci]], writes=[tpA], inc=(kc == 3))
            for kc in range(4):
                k.op("pe", lambda e, kc=kc: e.matmul(pB[:, 0:n], lhsT=svB[:, kc, oc * 128:(oc + 1) * 128], rhs=z[:, kc, c0:c0 + n],
                                                     start=(kc == 0), stop=(kc == 3)), reads=[tsB, t_z[kc][ci]], writes=[tpB], inc=(kc == 3))
            sg, tsg = self.nxt("sg")
            k.op("act", lambda e: e.activation(out=sg[:, 0:n], in_=pB[:, 0:n], func=AF.Sigmoid,
                                               bias=self.scol[:, 8 + l * 8 + 4 + oc:8 + l * 8 + 5 + oc]),
                 reads=[tpB, tc], writes=[tsg])
            k.op("dve", lambda e: e.scalar_tensor_tensor(out=cout[:, oc, c0:c0 + n], in0=pA[:, 0:n],
                                                         scalar=self.scol[:, 8 + l * 8 + oc:8 + l * 8 + 1 + oc],
                                                         in1=sg[:, 0:n], op0=ALU.add, op1=ALU.mult),
                 reads=[tpA, tsg, tc], writes=[t_c[ci]])
        for oc in range(4):
            for (ci, c0, n) in self.cts(h):
                glu(oc, ci, c0, n)

    def pt(self, name):
        t = self._pt.get(name)
        if t is None:
            t = self._pt[name] = T(name)
        return t

    def ct_views(self, CT):
        LSe = CT[:, 0:1024].rearrange("p (a b) -> p a b", a=8)
        LSo = CT[:, 1024:2048].rearrange("p (a b) -> p a b", a=8)
        CAb = CT[:, 2048:4096].rearrange("p (a b) -> p a b", a=8)
        KBb = CT[:, 4096:5120].rearrange("p (a b) -> p a b", a=8)
        return LSe, LSo, CAb, KBb

    def s5_consts(self, l, CT4, t_CT, pre):
        k = self.k
        ar = self.arena
        tc = self.t_const
        f32 = ar.f32
        bS, bX, Cn1, Cn2, t_b, t_Cn = pre
        Bs = f32([32, 16]); Bx = f32([32, 16]); S0 = f32([32, 16]); X0 = f32([32, 16])
        Xs = [f32([32, 16]) for _ in range(2)]; t_Xs = [T("Xs0"), T("Xs1")]
        t1 = f32([32, 16]); t2 = f32([32, 16])
        t_B = T("BsBx"); t_SX = T("S0X0"); t_t12 = T("t12")

        def bc(ap32):
            return ap32.unsqueeze(2).to_broadcast([P, 32, 16])
        C1 = bc(self.s5coef[:, l, 0, :])
        C2 = bc(self.s5coef[:, l, 1, :])

        def dv(fn, reads, writes):
            k.op("dve", fn, reads=reads, writes=writes)

        def flat(a):
            return a.rearrange("p g h -> p (g h)")
        dv(lambda e: e.tensor_tensor(out=Bs[:, :, :], in0=bS[:, :, :], in1=C1, op=ALU.mult), [t_b, tc], [t_B])
        dv(lambda e: e.tensor_tensor(out=t1[:, :, :], in0=bX[:, :, :], in1=C2, op=ALU.mult), [t_b, tc], [t_t12])
        dv(lambda e: e.tensor_tensor(out=Bs[:, :, :], in0=Bs[:, :, :], in1=t1[:, :, :], op=ALU.add), [t_B, t_t12], [t_B])
        dv(lambda e: e.tensor_tensor(out=Bx[:, :, :], in0=bX[:, :, :], in1=C1, op=ALU.mult), [t_b, tc], [t_B])
        dv(lambda e: e.tensor_tensor(out=t1[:, :, :], in0=bS[:, :, :], in1=C2, op=ALU.mult), [t_b, tc], [t_t12])
        dv(lambda e: e.tensor_tensor(out=Bx[:, :, :], in0=Bx[:, :, :], in1=t1[:, :, :], op=ALU.subtract), [t_B, t_t12], [t_B])
        for (src, dst, col) in ((Cn1, S0, 3), (Cn2, X0, 2)):
            def cstack(src, dst, col):
                psC, tpsC = self.psum()
                for c4 in range(4):
                    k.op("pe", lambda e, c4=c4: e.transpose(psC[:, c4 * 128:(c4 + 1) * 128], src[:, c4, :], self.ident[:]),
                         reads=[t_Cn, tc], writes=[tpsC], inc=(c4 == 3))
                dv(lambda e: e.tensor_scalar(out=flat(dst), in0=psC[:, :], scalar1=self.pmask[:, col:col + 1], scalar2=None, op0=ALU.mult),
                   [tpsC, tc], [t_SX])
            cstack(src, dst, col)
        for m in range(8):
            def power(m):
                xs, txs = Xs[m % 2], t_Xs[m % 2]
                if m == 0:
                    dv(lambda e: e.tensor_copy(out=xs[:, :, :], in_=Bs[:, :, :]), [t_B], [txs])
                else:
                    dv(lambda e: e.tensor_tensor(out=xs[:, :, :], in0=Bs[:, :, :], in1=bc(self.tab(l, m, 0)), op=ALU.mult), [t_B, tc], [txs])
                    dv(lambda e: e.tensor_tensor(out=t1[:, :, :], in0=Bx[:, :, :], in1=bc(self.tab(l, m, 1)), op=ALU.mult), [t_B, tc], [t_t12])
                    dv(lambda e: e.tensor_tensor(out=xs[:, :, :], in0=xs[:, :, :], in1=t1[:, :, :], op=ALU.add), [txs, t_t12], [txs])
                xs2 = flat(xs)
                s02 = flat(S0)
                psT, tpT = self.psum()
                psK, tpK = self.psum()
                for c4 in range(4):
                    k.op("pe", lambda e, c4=c4: e.transpose(psT[:, c4 * 128:(c4 + 1) * 128], xs2[:, c4 * 128:(c4 + 1) * 128], self.ident[:]),
                         reads=[txs, tc], writes=[tpT], inc=(c4 == 3))
                for c4 in range(4):
                    k.op("pe", lambda e, c4=c4: e.matmul(psK[:, c4 * 128:(c4 + 1) * 128], lhsT=xs2[:, c4 * 128:(c4 + 1) * 128],
                                                         rhs=s02[:, c4 * 128:(c4 + 1) * 128], start=True, stop=True),
                         reads=[txs, t_SX], writes=[tpK], inc=(c4 == 3))
                s8 = 7 - m
                psT3 = psT[:, :].rearrange("p (c f) -> p c f", c=4)
                psK3 = psK[:, :].rearrange("p (c f) -> p c f", c=4)
                dv(lambda e: e.tensor_scalar(out=CT4[:, :, s8 * 128:(s8 + 1) * 128], in0=psT3, scalar1=self.pmask[:, 0:1], scalar2=None,
                                             op0=ALU.mult), [tpT, tc], [t_CT])
                dv(lambda e: e.tensor_scalar(out=CT4[:, :, 1024 + s8 * 128:1024 + (s8 + 1) * 128], in0=psT3, scalar1=self.pmask[:, 1:2],
                                             scalar2=None, op0=ALU.mult), [tpT, tc], [t_CT])
                bd3 = self.bdmask[:, :].unsqueeze(1).to_broadcast([P, 4, 128])
                if m == 0:
                    t2v = flat(t2).rearrange("p (c f) -> p c f", c=4)
                    dv(lambda e: e.tensor_tensor(out=t2v, in0=psK3, in1=bd3, op=ALU.mult), [tpK, tc], [t_t12])
                    for c4 in range(4):
                        dv(lambda e, c4=c4: e.scalar_tensor_tensor(out=CT4[:, c4, 4096:4096 + 128], in0=self.ident[:, :],
                                                                   scalar=self.scol[:, l * 4 + c4:l * 4 + c4 + 1], in1=t2v[:, c4, :],
                                                                   op0=ALU.mult, op1=ALU.add), [t_t12, tc], [t_CT])
                else:
                    dv(lambda e: e.tensor_tensor(out=CT4[:, :, 4096 + m * 128:4096 + (m + 1) * 128], in0=psK3, in1=bd3, op=ALU.mult),
                       [tpK, tc], [t_CT])
            power(m)
        k.op("dve", lambda e: e.memset(CT4[:, :, 2048:4096], 0.0), writes=[t_CT])
        for m in range(1, 9):
            def capow(m):
                r = m - 1
                dv(lambda e: e.tensor_tensor(out=t1[:, :, :], in0=S0[:, :, :], in1=bc(self.tab(l, m, 0)), op=ALU.mult), [t_SX, tc], [t_t12])
                dv(lambda e: e.tensor_tensor(out=t2[:, :, :], in0=X0[:, :, :], in1=bc(self.tab(l, m, 1)), op=ALU.mult), [t_SX, tc], [t_t12])
                cav = CT4[:, :, 2048 + r * 256:2048 + (r + 1) * 256].rearrange("p c (gp two f) -> p c gp two f", two=2, f=32)
                t1v = t1.rearrange("p (c gp two) h -> p c gp two h", c=4, two=2)
                t2v_ = t2.rearrange("p (c gp two) h -> p c gp two h", c=4, two=2)
                dv(lambda e: e.tensor_tensor(out=cav[:, :, :, 0, 0:16], in0=t1v[:, :, :, 0, :], in1=t2v_[:, :, :, 0, :], op=ALU.subtract),
                   [t_t12], [t_CT])
                dv(lambda e: e.tensor_tensor(out=cav[:, :, :, 1, 16:32], in0=t1v[:, :, :, 1, :], in1=t2v_[:, :, :, 1, :], op=ALU.subtract),
                   [t_t12], [t_CT])
            capow(m)

    def s5_pass(self, part, l, h, c4, uP, us, z, t_uP, t_us, t_z, hcs, hcx, t_hc, H0s_all, H0x_all, Hns_all, t_H0):
        k = self.k
        ar = self.arena
        tc = self.t_const
        NB = 128
        samp = (h == 0)
        g0 = c4 * 8
        f32 = ar.f32
        par = c4 % 2
        LSb = ar.bf16([2048]); CKb = ar.bf16([3072])
        LSe = LSb[:, 0:1024].rearrange("p (a b) -> p a b", a=8)
        LSo = LSb[:, 1024:2048].rearrange("p (a b) -> p a b", a=8)
        CAb = CKb[:, 0:2048].rearrange("p (a b) -> p a b", a=8)
        KBb = CKb[:, 2048:3072].rearrange("p (a b) -> p a b", a=8)
        t_LS = self.pt("LSb"); t_CA = t_KB = self.pt("CKb")
        Hb2 = [ar.bf16([8, NB + 2]) for _ in range(2)]
        Hb = Hb2[par]; t_Hb = [self.pt(f"Hb{par}_{i}") for i in range(8)]
        HS = ar.f32([4, 128]); HX = ar.f32([4, 128]); WS = ar.f32([4, 128]); WX = ar.f32([4, 128]); TMP = ar.f32([4, 128])
        TBb = ar.f32([3, 512])
        t_HS, t_HX, t_HX2, t_WS, t_WX, t_TMP, t_TB = (self.pt(n_) for n_ in ("HS", "HX", "HX2", "WS", "WX", "TMPr", "TBb"))
        hl = ar.f32([4, 4]); t_hl = self.pt("hl")
        if samp:
            H0s = H0s_all[:, :, g0:g0 + 8]; H0x = H0x_all[:, :, g0:g0 + 8]; Hns = Hns_all[:, :, g0:g0 + 8]
            H0b2 = [ar.bf16([8, NS]) for _ in range(2)]
            H0b = H0b2[par]; t_H0b = self.pt(f"H0b{par}")
            tmpx = f32([NS, 8]); t_tmpx = self.pt("tmpx")
            hls = f32([8, NS]); t_hls = self.pt("hls")

        def dv(fn, reads, writes):
            k.op("dve", fn, reads=reads, writes=writes)
        if part == "A":
            k.dma("sp", self.s_s5in[0], lambda e: e.dma_start(out=LSb, in_=self.s5scr[l, c4][:, 0:2048]), reads=[self.t_scr[l][c4]],
                  writes=[t_LS])
            if samp:
                k.op("act", lambda e: e.activation(out=H0b[:, :, :], in_=H0s.rearrange("p j g -> p g j"), func=AF.Copy),
                     reads=[t_H0], writes=[t_H0b])
        else:
            k.dma("pool", self.s_s5ck, lambda e: e.dma_start(out=CKb, in_=self.s5scr[l, c4][:, 2048:5120]), reads=[self.t_scr[l][c4]],
                  writes=[t_CA])

        def lsw(g8, s8):
            q = g8 // 2
            src = LSe if g8 % 2 == 0 else LSo
            return src[32 * q:32 * q + 32, s8, :]
        if part == "A":
            for gb in range(2):
                def gbatch(gb):
                    pss_ = [self.psum(), self.psum()]
                    for gl in range(4):
                        g8 = gb * 4 + gl
                        q = g8 // 2
                        ps, tp = pss_[gl // 2]
                        for s8 in range(8):
                            k.op("pe", lambda e, gl=gl, g8=g8, q=q, s8=s8, ps=ps: e.matmul(
                                ps[:, (gl % 2) * 128:(gl % 2 + 1) * 128], lhsT=lsw(g8, s8), rhs=uP[32 * q:32 * q + 32, c4, s8 * NB:(s8 + 1) * NB],
                                start=(s8 == 0), stop=(s8 == 7), tile_position=(32 * q, 0)),
                                reads=[t_LS, t_uP[c4]], writes=[tp], inc=(s8 == 7))
                    gq = g0 + gb * 4
                    k.dma("sp", self.s_s5in[1], lambda e: e.dma_start(out=TBb.rearrange("p a (g n) -> p a g n", g=4),
                                                                      in_=self.s5rot[l][:, :, gq:gq + 4, :]),
                          reads=[self.t_rot[l]], writes=[t_TB])
                    for half_ in range(2):
                        ps, tp = pss_[half_]
                        k.op("dve", lambda e, ps=ps, half_=half_: e.tensor_copy(out=HS[:, half_ * 2:half_ * 2 + 2, :],
                                                                                  in_=ps[:, 0:256].rearrange("p (g n) -> p g n", g=2)),
                             reads=[tp], writes=[t_HS])
                    if h == 1:
                        k.op("dve", lambda e: e.tensor_tensor(out=HS[:, :, 0:1], in0=HS[:, :, 0:1],
                                                              in1=hcs[:, gq:gq + 4].unsqueeze(2), op=ALU.add),
                             reads=[t_HS, t_hc], writes=[t_HS])
                    k.dma("sp", self.s_s5x[0], lambda e: e.dma_start(out=HX[0:64, :, :], in_=HS[64:128, :, :]), reads=[t_HS], writes=[t_HX])
                    k.dma("sp", self.s_s5x[1], lambda e: e.dma_start(out=HX[64:128, :, :], in_=HS[0:64, :, :]), reads=[t_HS], writes=[t_HX2])
                    fl = lambda a: a.rearrange("p g n -> p (g n)")
                    Tc, TsS, Rr = TBb[:, 0, :], TBb[:, 1, :], TBb[:, 2, :]
                    tHX = [t_HX, t_HX2]

                    def dvb(fn, reads, writes):
                        k.op("dve", fn, reads=reads, writes=writes)
                    dvb(lambda e: e.tensor_tensor(out=fl(WS), in0=fl(HS), in1=Tc, op=ALU.mult), [t_HS, t_TB], [t_WS])
                    dvb(lambda e: e.tensor_tensor(out=fl(TMP), in0=fl(HX), in1=TsS, op=ALU.mult), tHX + [t_TB], [t_TMP])
                    dvb(lambda e: e.tensor_tensor(out=fl(WS), in0=fl(WS), in1=fl(TMP), op=ALU.add), [t_WS, t_TMP], [t_WS])
                    dvb(lambda e: e.tensor_tensor(out=fl(WX), in0=fl(HX), in1=Tc, op=ALU.mult), tHX + [t_TB], [t_WX])
                    dvb(lambda e: e.tensor_tensor(out=fl(TMP), in0=fl(HS), in1=TsS, op=ALU.mult), [t_HS, t_TB], [t_TMP])
                    dvb(lambda e: e.tensor_tensor(out=fl(WX), in0=fl(WX), in1=fl(TMP), op=ALU.subtract), [t_WX, t_TMP], [t_WX])
                    dvb(lambda e: e.tensor_tensor_scan(out=fl(HS), data0=Rr, data1=fl(WS), initial=0.0, op0=ALU.mult, op1=ALU.add),
                        [t_WS, t_TB, t_HS], [t_HS])
                    dvb(lambda e: e.tensor_tensor_scan(out=fl(HX), data0=Rr, data1=fl(WX), initial=0.0, op0=ALU.mult, op1=ALU.add),
                        [t_WX, t_TB] + tHX, tHX)
                    dvb(lambda e: e.tensor_tensor(out=fl(WS), in0=fl(HS), in1=Tc, op=ALU.mult), [t_HS, t_TB], [t_WS])
                    dvb(lambda e: e.tensor_tensor(out=fl(TMP), in0=fl(HX), in1=TsS, op=ALU.mult), tHX + [t_TB], [t_TMP])
                    g8a = gb * 4
                    dvb(lambda e: e.tensor_tensor(out=Hb[:, g8a:g8a + 4, 1:NB + 1], in0=WS[:, :, :], in1=TMP[:, :, :], op=ALU.subtract),
                        [t_WS, t_TMP], t_Hb[g8a:g8a + 4])
                    dvb(lambda e: e.tensor_tensor(out=self.Hnew[:, 0, gq:gq + 4], in0=WS[:, :, NB - 1], in1=TMP[:, :, NB - 1], op=ALU.subtract),
                        [t_WS, t_TMP], [self.t_Hnew])
                    Tc3 = Tc.rearrange("p (g n) -> p g n", g=4)
                    Ts3 = TsS.rearrange("p (g n) -> p g n", g=4)
                    dvb(lambda e: e.tensor_tensor(out=hl[:, 0, :], in0=HX[:, :, NB - 1], in1=Tc3[:, :, NB - 1], op=ALU.mult), tHX + [t_TB], [t_hl])
                    dvb(lambda e: e.tensor_tensor(out=hl[:, 1, :], in0=HS[:, :, NB - 1], in1=Ts3[:, :, NB - 1], op=ALU.mult), [t_HS, t_TB], [t_hl])
                    dvb(lambda e: e.tensor_tensor(out=self.Hnew[:, 1, gq:gq + 4], in0=hl[:, 0, :], in1=hl[:, 1, :], op=ALU.add),
                        [t_hl], [self.t_Hnew])
                    k.op("act", lambda e: e.activation(out=Hb[:, g8a:g8a + 4, 0:1], in_=self.Hprev[:, l, 0, gq:gq + 4].unsqueeze(2), func=AF.Copy),
                         reads=[tc], writes=t_Hb[g8a:g8a + 4])
                gbatch(gb)
            if samp:
                for q in range(4):
                    def sq_(q):
                        psH, tpsH = self.psum()
                        for e2 in range(2):
                            g8 = 2 * q + e2
                            k.op("pe", lambda e, g8=g8, e2=e2: e.matmul(psH[:, e2 * NS:(e2 + 1) * NS], lhsT=lsw(g8, 7), rhs=us[32 * q:32 * q + 32, c4, :],
                                                                      start=True, stop=True, tile_position=(32 * q, 0)),
                                 reads=[t_LS, t_us], writes=[tpsH], inc=(e2 == 1))
                        dv(lambda e: e.tensor_copy(out=hls[:, 2 * q:2 * q + 2, :], in_=psH[:, 0:2 * NS].rearrange("p (g j) -> p g j", g=2)),
                           [tpsH], [t_hls])
                    sq_(q)
                A1b = self.tab(l, 1, 0)[:, g0:g0 + 8].unsqueeze(1).to_broadcast([P, NS, 8])
                A2b = self.tab(l, 1, 1)[:, g0:g0 + 8].unsqueeze(1).to_broadcast([P, NS, 8])
                dv(lambda e: e.tensor_tensor(out=Hns, in0=H0s, in1=A1b, op=ALU.mult), [t_H0, tc], [t_H0])
                dv(lambda e: e.tensor_tensor(out=tmpx[:, :, :], in0=H0x, in1=A2b, op=ALU.mult), [t_H0, tc], [t_tmpx])
                dv(lambda e: e.tensor_tensor(out=Hns, in0=Hns, in1=tmpx[:, :, :], op=ALU.add), [t_H0, t_tmpx], [t_H0])
                dv(lambda e: e.tensor_tensor(out=Hns, in0=Hns, in1=hls.rearrange("p g j -> p j g"), op=ALU.add),
                   [t_H0, t_hls], [t_H0])
            return
        if self.stop and self.stop.get('s5_stop') == 2:
            return
        t_Hb_all = list(t_Hb)

        def inter(psr, r, rhs_of, last_stop=True):
            for g8 in range(8):
                q = g8 // 2
                if g8 % 2 == 0:
                    o = psr[32 * q:32 * q + 16]
                    w = CAb[:, r, g8 * 32:g8 * 32 + 16]
                else:
                    o = psr[32 * q:32 * q + 32]
                    w = CAb[:, r, g8 * 32:g8 * 32 + 32]
                yield g8, o, w, q
        for bank in range(2):
            def ybank(bank):
                ps, tp = self.psum()
                for rr in range(4):
                    r = bank * 4 + rr
                    psr = ps[:, rr * NB:(rr + 1) * NB]
                    for j in range(r + 1):
                        k.op("pe", lambda e, j=j, r=r, psr=psr: e.matmul(psr, lhsT=KBb[:, j, :], rhs=uP[:, c4, (r - j) * NB:(r - j + 1) * NB],
                                                                         start=(j == 0), stop=False),
                             reads=[t_KB, t_uP[c4]], writes=[tp], inc=False)
                    for g8, o, w, q in inter(psr, r, None):
                        k.op("pe", lambda e, g8=g8, o=o, w=w, q=q: e.matmul(o, lhsT=w, rhs=Hb[:, g8, 0:NB], start=False, stop=(g8 % 2 == 1),
                                                                            tile_position=(0, 32 * q)),
                             reads=[t_CA, t_Hb[g8]], writes=[tp], inc=(g8 == 7))
                k.op("act", lambda e: e.activation(
                    out=z[:, c4, 0:NPH].rearrange("p (n r) -> p r n", r=8)[:, bank * 4:bank * 4 + 4, :],
                    in_=ps[:, :].rearrange("p (r n) -> p r n", r=4), func=AF.Gelu_apprx_tanh),
                    reads=[tp], writes=[t_z[c4][0], t_z[c4][1]])
            ybank(bank)
        if samp:
            ps, tp = self.psum()
            psr = ps[:, 0:NS]
            k.op("pe", lambda e: e.matmul(psr, lhsT=KBb[:, 0, :], rhs=us[:, c4, :], start=True, stop=False),
                 reads=[t_KB, t_us], writes=[tp], inc=False)
            for g8, o, w, q in inter(psr, 0, None):
                k.op("pe", lambda e, g8=g8, o=o, w=w, q=q: e.matmul(o, lhsT=w, rhs=H0b[:, g8, :], start=False, stop=(g8 % 2 == 1),
                                                                    tile_position=(0, 32 * q)),
                     reads=[t_CA, t_H0b], writes=[tp], inc=(g8 == 7))
            k.op("act", lambda e: e.activation(out=z[:, c4, NPH:NPH + NS], in_=ps[:, 0:NS], func=AF.Gelu_apprx_tanh),
                 reads=[tp], writes=[t_z[c4][2]])

    def merge(self, l, h, branches, nxt=None):
        k = self.k
        ar = self.arena
        NCM = NPH + NS
        cts = self.cts(h)
        merged = ar.bf16([DC, NCM])
        t_m = [[T(f"m{dc}_{c}") for c in range(3)] for dc in range(DC)]
        ybuf = ar.f32([DC, NCM])
        t_y = [T(f"my{c}") for c in range(3)]
        acc = [ar.f32([512]) for _ in range(2)]; t_acc = [T("acc0"), T("acc1")]
        sgf = [ar.f32([512]) for _ in range(2)]; t_sgf = [T("sgf0"), T("sgf1")]
        cnt = [0, 0]
        wv = self.w_in[l].rearrange("(kc kp) f -> kp kc f", kp=P)

        def one_dc(dc):
            slA, tsA, ssA = self.wslot()
            gA = slA[:, 0:8 * 3 * 128].rearrange("p (kc n f) -> p kc n f", kc=8, n=3)
            for n in range(3):
                col = 3584 + n * D + dc * 128
                k.dma("pool", ssA, lambda e, n=n, col=col: e.dma_start(out=gA[:, :, n, :], in_=wv[:, :, col:col + 128]),
                      writes=[tsA] if n == 0 else [])
            tsA.w = (ssA, ssA.count)
            slB, tsB, ssB = self.wslot()
            wB = slB[:, 0:4 * 3 * 128].rearrange("p (kc n f) -> p kc n f", kc=4, n=3)
            for n in range(3):
                src = self.w_branch[l, n].rearrange("(kc kp) d -> kp kc d", kp=P)
                k.dma("pool", ssB, lambda e, n=n, src=src: e.dma_start(out=wB[:, :, n, :], in_=src[:, :, dc * 128:(dc + 1) * 128]),
                      writes=[tsB] if n == 0 else [])
            tsB.w = (ssB, ssB.count)

            def one_ct(ci, c0, nn):
                a, ta = acc[cnt[0] % 2], t_acc[cnt[0] % 2]
                cnt[0] += 1
                for n in range(3):
                    def one_n(n):
                        br, t_br = branches[n]
                        pg, tpg = self.psum()
                        pb, tpb = self.psum()
                        for kc in range(8):
                            k.op("pe", lambda e, kc=kc: e.matmul(pg[:, 0:nn], lhsT=gA[:, kc, n, :], rhs=self.hT[:, kc, c0:c0 + nn],
                                                                 start=(kc == 0), stop=(kc == 7)),
                                 reads=[tsA, self.t_h[ci]], writes=[tpg], inc=(kc == 7))
                        for kc in range(4):
                            k.op("pe", lambda e, kc=kc: e.matmul(pb[:, 0:nn], lhsT=wB[:, kc, n, :], rhs=br[:, kc, c0:c0 + nn],
                                                                 start=(kc == 0), stop=(kc == 3)),
                                 reads=[tsB, t_br[ci]], writes=[tpb], inc=(kc == 3))
                        sg_, tsg = sgf[cnt[1] % 2], t_sgf[cnt[1] % 2]
                        cnt[1] += 1
                        k.op("act", lambda e: e.activation(out=sg_[:, 0:nn], in_=pg[:, 0:nn], func=AF.Sigmoid), reads=[tpg], writes=[tsg])
                        if n == 0:
                            k.op("dve", lambda e: e.tensor_tensor(out=a[:, 0:nn], in0=pb[:, 0:nn], in1=sg_[:, 0:nn], op=ALU.mult),
                                 reads=[tpb, tsg], writes=[ta])
                        else:
                            k.op("dve", lambda e: e.tensor_tensor(out=sg_[:, 0:nn], in0=pb[:, 0:nn], in1=sg_[:, 0:nn], op=ALU.mult),
                                 reads=[tpb, tsg], writes=[tsg])
                            if n == 1:
                                k.op("dve", lambda e: e.tensor_tensor(out=a[:, 0:nn], in0=a[:, 0:nn], in1=sg_[:, 0:nn], op=ALU.add),
                                     reads=[ta, tsg], writes=[ta])
                            else:
                                k.op("dve", lambda e: e.tensor_tensor(out=merged[:, dc, c0:c0 + nn], in0=a[:, 0:nn], in1=sg_[:, 0:nn], op=ALU.add),
                                     reads=[ta, tsg], writes=[t_m[dc][ci]])
                    one_n(n)
            for (ci, c0, nn) in cts:
                one_ct(ci, c0, nn)
        for dc in range(DC):
            one_dc(dc)
        pss = {ci: (self.ps[5 + ci], self.t_ps[5 + ci]) for (ci, _, _) in cts}
        wo = self.w_out[l].rearrange("(kc kp) d -> kp kc d", kp=P)
        def load_wo(ob):
            slot, tsl, ssl = self.wslot()
            sv = slot[:, 0:8 * 512].rearrange("p (kc f) -> p kc f", kc=8)
            k.dma("pool", ssl, lambda e: e.dma_start(out=sv, in_=wo[:, :, ob * 512:(ob + 1) * 512]), writes=[tsl])
            return sv, tsl

        def out_tile(sv, tsl, ob, j, ci, c0, nn):
            oc = ob * 4 + j
            po, tpo = self.psum(0, 5)
            for kc in range(8):
                k.op("pe", lambda e, kc=kc: e.matmul(po[:, 0:nn], lhsT=sv[:, kc, j * 128:(j + 1) * 128], rhs=merged[:, kc, c0:c0 + nn],
                                                     start=(kc == 0), stop=(kc == 7)),
                     reads=[tsl, t_m[kc][ci]], writes=[tpo], inc=(kc == 7))
            k.op("dve", lambda e: e.tensor_copy(out=ybuf[:, oc, c0:c0 + nn], in_=po[:, 0:nn]), reads=[tpo], writes=[t_y[ci]])
            sq, tsq = self.nxt("sq")
            k.op("act", lambda e: e.activation(out=sq[:, 0:nn], in_=ybuf[:, oc, c0:c0 + nn], func=AF.Square), reads=[t_y[ci]], writes=[tsq])
            pS, tpS = pss[ci]
            k.op("pe", lambda e: e.matmul(pS[:, 0:nn], lhsT=self.onesb[:], rhs=sq[:, 0:nn], start=(oc == 0), stop=(oc == DC - 1)),
                 reads=[tsq, self.t_const], writes=[tpS], inc=True)
        sv0, tsl0 = load_wo(0)
        for j in range(4):
            for (ci, c0, nn) in cts:
                out_tile(sv0, tsl0, 0, j, ci, c0, nn)
        sv1, tsl1 = load_wo(1)
        for ti, (ci, c0, nn) in enumerate(cts):
            for j in range(4):
                out_tile(sv1, tsl1, 1, j, ci, c0, nn)
            self.post_norm(l, 3, ybuf, t_y, pss, [(ci, c0, nn)], nxt, do_barrier=(ti == len(cts) - 1))

    def dump(self, name, ap, t, rows=P):
        if not self.dbg:
            return
        k = self.k
        n = int(np.prod(ap.shape[1:]))
        off = self.dbg_off
        self.dbg_off += n
        assert self.dbg_off <= self.dbg
        self.dbg_map[name] = (off, rows, tuple(ap.shape[1:]))
        dst = self.dbgst[0:rows, 0:n]
        if len(ap.shape) == 3:
            dst = dst.rearrange("p (a b) -> p a b", a=ap.shape[1])
        k.op("dve", lambda e: e.tensor_copy(out=dst, in_=ap), reads=[t], writes=[self.t_dbgst])
        k.dma("sp", self.s_dbg, lambda e: e.dma_start(out=self.dbg_out[0:rows, off:off + n], in_=self.dbgst[0:rows, 0:n]),
              reads=[self.t_dbgst])

    def wblock(self, l, col0, ncols=512):
        k = self.k
        slot, tsl, ssl = self.wslot()
        sv = slot[:, 0:8 * ncols].rearrange("p (kc f) -> p kc f", kc=8)
        wv = self.w_in[l].rearrange("(kc kp) f -> kp kc f", kp=P)
        k.dma("pool", ssl, lambda e: e.dma_start(out=sv, in_=wv[:, :, col0:col0 + ncols]), writes=[tsl])
        return sv, tsl

    def proj_fm(self, sv, tsl, j, ci, c0, n, ps, tp):
        k = self.k
        for kc in range(8):
            k.op("pe", lambda e, kc=kc: e.matmul(ps[:, 0:n], lhsT=sv[:, kc, j * 128:(j + 1) * 128], rhs=self.hT[:, kc, c0:c0 + n],
                                                 start=(kc == 0), stop=(kc == 7)),
                 reads=[tsl, self.t_h[ci]], writes=[tp], inc=(kc == 7))

    def proj_tm(self, sv, tsl, ci, t0, m, ps, tp):
        k = self.k
        for kc in range(8):
            k.op("pe", lambda e, kc=kc: e.matmul(ps[0:m, :], lhsT=self.hT[:, kc, t0:t0 + m], rhs=sv[:, kc, :],
                                                 start=(kc == 0), stop=(kc == 7)),
                 reads=[tsl, self.t_h[ci]], writes=[tp], inc=(kc == 7))

    def gmlp(self, l, h, bout, t_bout):
        k = self.k
        ar = self.arena
        NCM = NPH + NS
        vtok = ar.bf16([8, 512])
        u = ar.bf16([4, NCM])
        gt = [ar.f32([512]) for _ in range(2)]
        t_gt = [T("gt0"), T("gt1")]
        vs32 = ar.f32([512])
        t_vs = T("vs32")
        tmps = ar.f32([4, NS])
        t_tmps = T("tmps")
        t_vtok = [T(f"vtok{i}") for i in range(8)]
        t_u = [[T(f"u{j}_{c}") for c in range(3)] for j in range(4)]
        cts = self.cts(h)
        bsr = ar.f32([512])
        t_bsr = T("bsr")
        k.dma("sp", self.s_bsr, lambda e: e.dma_start(out=bsr[0:1, :].rearrange("p (g t) -> p g t", g=4), in_=self.gmlp_bs[l:l + 1, :, :]),
              writes=[t_bsr])
        sv, tsl = self.wblock(l, 2560)
        slices = [(s, s * 128, 128) for s in range(8)] + ([(8, 1024, NS)] if h == 0 else [])

        def do_slice(s, t0, m):
            ci = 2 if s == 8 else s // 4
            ps, tp = self.psum()
            self.proj_tm(sv, tsl, ci, t0, m, ps, tp)
            g, tg = gt[s % 2], t_gt[s % 2]
            st6, tst = self.nxt("st6")
            k.op("act", lambda e: e.activation(out=g[0:m, :], in_=ps[0:m, :], func=AF.Gelu_apprx_tanh), reads=[tp], writes=[tg])
            k.op("dve", lambda e: e.bn_stats(out=st6[0:m, 0:6], in_=g[0:m, :]), reads=[tg], writes=[tst])
            k.op("dve", lambda e: e.bn_aggr(out=st6[0:m, 6:8], in_=st6[0:m, 0:6]), reads=[tst], writes=[tst])
            k.op("act", lambda e: e.activation(out=st6[0:m, 7:8], in_=st6[0:m, 7:8], func=AF.Sqrt, bias=self.epsc[0:m, 0:1]),
                 reads=[tst, self.t_const], writes=[tst])
            k.op("dve", lambda e: e.reciprocal(out=st6[0:m, 7:8], in_=st6[0:m, 7:8]), reads=[tst], writes=[tst])
            k.op("dve", lambda e: e.tensor_scalar(out=g[0:m, :], in0=g[0:m, :], scalar1=st6[0:m, 6:7], scalar2=st6[0:m, 7:8],
                                                  op0=ALU.subtract, op1=ALU.mult), reads=[tg, tst], writes=[tg])
            k.op("dve", lambda e: e.tensor_tensor(out=g[0:m, :], in0=g[0:m, :], in1=self.ngB[0:m, l, :], op=ALU.mult),
                 reads=[tg, self.t_const], writes=[tg])
            if s < 8:
                k.op("dve", lambda e: e.tensor_tensor(out=vtok[0:m, s, :], in0=g[0:m, :], in1=self.nbB[0:m, l, :], op=ALU.add),
                     reads=[tg, self.t_const], writes=[t_vtok[s]])
            else:
                k.op("dve", lambda e: e.tensor_tensor(out=vs32[0:m, :], in0=g[0:m, :], in1=self.nbB[0:m, l, :], op=ALU.add),
                     reads=[tg, self.t_const], writes=[t_vs])
                k.dma("sp", self.s_vs, lambda e: e.dma_start(out=self.vs_out[l, :, :], in_=vs32[0:NS, :]), reads=[t_vs])
        for (s, t0, m) in slices:
            do_slice(s, t0, m)
        sv2, tsl2 = self.wblock(l, 2048)

        def do_u(j, ci, c0, n):
            ps, tp = self.psum()
            self.proj_fm(sv2, tsl2, j, ci, c0, n, ps, tp)
            k.op("act", lambda e: e.activation(out=u[:, j, c0:c0 + n], in_=ps[:, 0:n], func=AF.Gelu_apprx_tanh),
                 reads=[tp], writes=[t_u[j][ci]])
        for j in range(4):
            for (ci, c0, n) in cts:
                do_u(j, ci, c0, n)

        def do_mix(g, ci, c0):
            ps, tp = self.psum()
            for s4 in range(4):
                sl = ci * 4 + s4
                k.op("pe", lambda e, s4=s4, sl=sl: e.matmul(ps[:, s4 * 128:(s4 + 1) * 128], lhsT=vtok[:, sl, g * 128:(g + 1) * 128],
                                                            rhs=self.wmT[:, l, g, :], start=True, stop=False),
                     reads=[t_vtok[sl], self.t_const], writes=[tp], inc=False)
                k.op("pe", lambda e, s4=s4: e.matmul(ps[:, s4 * 128:(s4 + 1) * 128], lhsT=self.onesf[0:1, :],
                                                     rhs=bsr[0:1, g * 128:(g + 1) * 128], start=False, stop=True),
                     reads=[self.t_const, t_bsr], writes=[tp], inc=(s4 == 3))
            k.op("dve", lambda e: e.tensor_tensor(out=bout[:, g, c0:c0 + 512], in0=ps[:, :], in1=u[:, g, c0:c0 + 512], op=ALU.mult),
                 reads=[tp, t_u[g][ci]], writes=[t_bout[ci]])
        for g in range(4):
            for ci in range(2):
                do_mix(g, ci, ci * 512)
        if h == 0:
            ps, tp = self.psum()
            for g in range(4):
                k.op("pe", lambda e, g=g: e.transpose(ps[:, g * NS:(g + 1) * NS], vs32[0:NS, g * 128:(g + 1) * 128],
                                                      self.ident[0:NS, 0:NS]),
                     reads=[t_vs, self.t_const], writes=[tp], inc=(g == 3))
            for g in range(4):
                k.op("dve", lambda e, g=g: e.tensor_scalar(out=tmps[:, g, :], in0=ps[:, g * NS:(g + 1) * NS],
                                                           scalar1=self.wsc[:, l, g:g + 1], scalar2=self.bsc[:, l, g:g + 1],
                                                           op0=ALU.mult, op1=ALU.add),
                     reads=[tp, self.t_const], writes=[t_tmps])
            k.op("dve", lambda e: e.tensor_tensor(out=bout[:, :, NPH:NPH + NS], in0=tmps[:, :, :], in1=u[:, :, NPH:NPH + NS],
                                                  op=ALU.mult),
                 reads=[t_tmps] + [t_u[j][2] for j in range(4)], writes=[t_bout[2]])

    def mixer(self, l, h, pre=True, nxt=None):
        k = self.k
        if pre:
            k.barrier()
        ar = self.arena
        ar.reset()
        NCM = NPH + NS
        aout = ar.bf16([4, NCM])
        bout = ar.bf16([4, NCM])
        cout = ar.bf16([4, NCM])
        t_a = [T(f"aout{c}") for c in range(3)]
        t_b = [T(f"bout{c}") for c in range(3)]
        t_c = [T(f"cout{c}") for c in range(3)]
        if pre:
            self.prenorm(l, 2, h)
        mark = ar.off
        self.gmlp(l, h, bout, t_b)
        if self.dbg and self.stop and self.stop.get("phase") == "gmlp":
            for j in range(4):
                self.dump(f"bout{j}", bout[:, j, :], t_b[0] if False else t_b[2] if h == 0 else t_b[1])
            return
        k.barrier()
        ar.off = mark
        if not (self.stop and self.stop.get("skip_s5")):
            self.s5(l, h, cout, t_c)
            if self.dbg and self.stop and self.stop.get("phase") == "s5":
                for j in range(4):
                    self.dump(f"cout{j}", cout[:, j, :], t_c[2 if h == 0 else 1])
                return
            k.barrier()
            ar.off = mark
        self._pt = {}
        for ci in range(2):
            ar.off = mark
            self.hgrn_tile(l, h, ci, aout, t_a)
        k.barrier()
        ar.off = mark
        if h == 1:
            k.dma("sp", self.s_hgp, lambda e: e.dma_start(out=self.hgp_out[l].rearrange("h k v -> k h v"), in_=self.S32[:, l, :, :]),
                  reads=self.t_S32[l])
        if self.dbg and self.stop and self.stop.get("phase") == "hgrn":
            for j in range(4):
                self.dump(f"aout{j}", aout[:, j, :], t_a[2 if h == 0 else 1])
            return
        self.merge(l, h, [(aout, t_a), (bout, t_b), (cout, t_c)], nxt=nxt)

    def g_ap(self, l, j, dc):
        o = (l * 6 + j) * 8 + dc
        return self.gcol[:, o:o + 1]

    def load_x(self, h, off=0):
        k = self.k
        self.arena.off = off
        stage = [self.arena.f32([D]) for _ in range(4)]
        t_st = self.t_stage_in
        s_st = self.s_stage_in
        for (ci, c0, n) in self.cts(h):
            if n == 512:
                for s in range(4):
                    r0 = h * NPH + c0 + s * 128
                    k.dma("sp", s_st[s], lambda e, s=s, r0=r0: e.dma_start(out=stage[s], in_=self.xp[r0:r0 + 128, :]),
                          writes=[t_st[s]])
                for dc in range(DC):
                    ps, tp = self.psum()
                    for s in range(4):
                        k.op("pe", lambda e, s=s, dc=dc, ps=ps: e.transpose(
                            ps[:, s * 128:(s + 1) * 128], stage[s][:, dc * 128:(dc + 1) * 128], self.ident[:]),
                            reads=[t_st[s], self.t_const], writes=[tp], inc=(s == 3))
                    en = "act" if dc % 2 == 0 else "dve"
                    if en == "act":
                        k.op("act", lambda e, dc=dc, ps=ps, c0=c0: e.activation(
                            out=self.xT[:, dc, c0:c0 + 512], in_=ps[:], func=AF.Copy), reads=[tp], writes=[self.t_x[ci]])
                    else:
                        k.op("dve", lambda e, dc=dc, ps=ps, c0=c0: e.tensor_copy(
                            out=self.xT[:, dc, c0:c0 + 512], in_=ps[:]), reads=[tp], writes=[self.t_x[ci]])
            else:
                k.dma("sp", s_st[0], lambda e: e.dma_start(out=stage[0][0:NS, :], in_=self.xs[:, :]), writes=[t_st[0]])
                ps, tp = self.psum()
                for dc in range(DC):
                    k.op("pe", lambda e, dc=dc, ps=ps: e.transpose(
                        ps[:, dc * NS:(dc + 1) * NS], stage[0][0:NS, dc * 128:(dc + 1) * 128], self.ident[0:NS, 0:NS]),
                        reads=[t_st[0], self.t_const], writes=[tp], inc=(dc == DC - 1))
                k.op("dve", lambda e, ps=ps, c0=c0: e.tensor_copy(
                    out=self.xT[:, :, c0:c0 + NS], in_=ps[:, 0:DC * NS].rearrange("p (a b) -> p a b", a=DC)),
                    reads=[tp], writes=[self.t_x[ci]])

    def store_x(self, h):
        k = self.k
        self.arena.reset()
        stage = [self.arena.f32([D]) for _ in range(2)]
        t_st = self.t_stage_out
        s_st = self.s_stage_out
        si = 0
        for (ci, c0, n) in self.cts(h):
            nsl = 4 if n == 512 else 1
            rows = 128 if n == 512 else NS
            for s in range(nsl):
                stg, tst, sst = stage[si % 2], t_st[si % 2], s_st[si % 2]
                si += 1
                for half in range(2):
                    ps, tp = self.psum()
                    for q in range(4):
                        dc = half * 4 + q
                        k.op("pe", lambda e, ps=ps, q=q, dc=dc, s=s, c0=c0, rows=rows: e.transpose(
                            ps[0:rows, q * 128:(q + 1) * 128], self.xT[:, dc, c0 + s * 128:c0 + s * 128 + rows], self.ident[:]),
                            reads=[self.t_x[ci], self.t_const], writes=[tp], inc=(q == 3))
                    if half == 0:
                        k.op("act", lambda e, ps=ps, stg=stg, rows=rows: e.activation(
                            out=stg[0:rows, 0:512], in_=ps[0:rows, :], func=AF.Copy), reads=[tp], writes=[tst])
                    else:
                        k.op("dve", lambda e, ps=ps, stg=stg, rows=rows: e.tensor_copy(
                            out=stg[0:rows, 512:1024], in_=ps[0:rows, :]), reads=[tp], writes=[tst])
                if n == 512:
                    r0 = h * NPH + c0 + s * 128
                    k.dma("sp", sst, lambda e, stg=stg, r0=r0: e.dma_start(out=self.yp[r0:r0 + 128, :], in_=stg[:, :]),
                          reads=[tst])
                else:
                    k.dma("sp", sst, lambda e, stg=stg: e.dma_start(out=self.ys[:, :], in_=stg[0:NS, :]), reads=[tst])

    def rstd_from(self, ps, tp, n, width):
        k = self.k
        rs, trs = self.nxt("rs")
        k.op("act", lambda e: e.activation(out=rs[:, 0:n], in_=ps[:, 0:n], func=AF.Ln, scale=1.0 / width, bias=self.epsc[:, 0:1]),
             reads=[tp, self.t_const], writes=[trs])
        k.op("act", lambda e: e.activation(out=rs[:, 0:n], in_=rs[:, 0:n], func=AF.Exp, scale=-0.5), reads=[trs], writes=[trs])
        return rs, trs

    def prenorm(self, l, j, h):
        for (ci, c0, n) in self.cts(h):
            self.prenorm_tile(l, j, ci, c0, n)

    def prenorm_tile(self, l, j, ci, c0, n):
        k = self.k
        ps, tp = self.ps[5 + ci], self.t_ps[5 + ci]
        for dc in range(DC):
            sq, tsq = self.nxt("sq")
            k.op("act", lambda e, sq=sq, dc=dc: e.activation(out=sq[:, 0:n], in_=self.xT[:, dc, c0:c0 + n], func=AF.Square),
                 reads=[self.t_x[ci]], writes=[tsq])
            k.op("pe", lambda e, sq=sq, dc=dc: e.matmul(ps[:, 0:n], lhsT=self.onesb[:], rhs=sq[:, 0:n], start=(dc == 0), stop=(dc == DC - 1)),
                 reads=[tsq, self.t_const], writes=[tp], inc=True)
        rs, trs = self.rstd_from(ps, tp, n, D)
        for dc in range(DC):
            k.op("dve", lambda e, dc=dc: e.scalar_tensor_tensor(
                out=self.hT[:, dc, c0:c0 + n], in0=self.xT[:, dc, c0:c0 + n], scalar=self.g_ap(l, j, dc),
                in1=rs[:, 0:n], op0=ALU.mult, op1=ALU.mult),
                reads=[self.t_x[ci], trs, self.t_const], writes=[self.t_h[ci]])

    def post_norm(self, l, jpost, ybuf, t_y, pss, cts, nxt, do_barrier=True):
        k = self.k
        for (ci, c0, n) in cts:
            def post(ci, c0, n):
                pS, tpS = pss[ci]
                rs, trs = self.rstd_from(pS, tpS, n, D)
                for dc in range(DC):
                    k.op("dve", lambda e, dc=dc: e.tensor_tensor(out=ybuf[:, dc, c0:c0 + n], in0=ybuf[:, dc, c0:c0 + n], in1=rs[:, 0:n],
                                                                 op=ALU.mult), reads=[t_y[ci], trs], writes=[t_y[ci]])
                    k.op("dve", lambda e, dc=dc: e.scalar_tensor_tensor(out=self.xT[:, dc, c0:c0 + n], in0=ybuf[:, dc, c0:c0 + n],
                                                                        scalar=self.g_ap(l, jpost, dc), in1=self.xT[:, dc, c0:c0 + n],
                                                                        op0=ALU.mult, op1=ALU.add),
                         reads=[t_y[ci], self.t_const, self.t_x[ci]], writes=[self.t_x[ci]])
                if nxt is not None:
                    self.prenorm_tile(nxt[0], nxt[1], ci, c0, n)
            post(ci, c0, n)
        if not do_barrier:
            return
        if nxt is not None:
            k.barrier(exclude=("pe", "pool"))
        else:
            k.barrier()

    def ffn(self, l, i, h, pre=True, nxt=None):
        k = self.k
        cts = self.cts(h)
        NCM = NPH + NS
        self.arena.reset()
        act = self.arena.bf16([FC, NCM])
        ybuf = self.arena.f32([DC, NCM])
        t_act = [[T(f"act{fc}_{c}") for c in range(3)] for fc in range(FC)]
        t_y = [T(f"y{c}") for c in range(3)]
        jpre, jpost = (0, 1) if i == 0 else (4, 5)
        if pre:
            self.prenorm(l, jpre, h)
        wg = self.w_gate[l, i].rearrange("(kc kp) f -> kp kc f", kp=P)
        wu = self.w_up[l, i].rearrange("(kc kp) f -> kp kc f", kp=P)
        wd = self.w_down[l, i].rearrange("(fc fp) d -> fp fc d", fp=P)
        for b in range(11 if self.stop is None else self.stop.get('gu', 11)):
            slot, tsl, ssl = self.wslot()
            sv = slot[:, 0:8 * 2 * 256].rearrange("p (kc g f) -> p kc g f", kc=8, g=2)
            f0 = b * 256
            k.dma("pool", ssl, lambda e, sv=sv, f0=f0: e.dma_start(out=sv[:, :, 0, :], in_=wg[:, :, f0:f0 + 256]), writes=[tsl])
            k.dma("pool", ssl, lambda e, sv=sv, f0=f0: e.dma_start(out=sv[:, :, 1, :], in_=wu[:, :, f0:f0 + 256]), writes=[])
            tsl.w = (ssl, ssl.count)
            for jf in range(2):
                fc = 2 * b + jf
                for (ci, c0, n) in cts:
                    psg, tpg = self.psum(0, 6)
                    psu, tpu = self.psum(0, 6)
                    for kc in range(8):
                        k.op("pe", lambda e, sv=sv, kc=kc, jf=jf, psg=psg, c0=c0, n=n: e.matmul(
                            psg[:, 0:n], lhsT=sv[:, kc, 0, jf * 128:(jf + 1) * 128], rhs=self.hT[:, kc, c0:c0 + n],
                            start=(kc == 0), stop=(kc == 7)), reads=[tsl, self.t_h[ci]], writes=[tpg], inc=(kc == 7))
                    for kc in range(8):
                        k.op("pe", lambda e, sv=sv, kc=kc, jf=jf, psu=psu, c0=c0, n=n: e.matmul(
                            psu[:, 0:n], lhsT=sv[:, kc, 1, jf * 128:(jf + 1) * 128], rhs=self.hT[:, kc, c0:c0 + n],
                            start=(kc == 0), stop=(kc == 7)), reads=[tsl, self.t_h[ci]], writes=[tpu], inc=(kc == 7))
                    sg, tsg = self.nxt("sg")
                    k.op("act", lambda e, sg=sg, psg=psg, n=n: e.activation(out=sg[:, 0:n], in_=psg[:, 0:n], func=AF.Silu),
                         reads=[tpg], writes=[tsg])
                    k.op("dve", lambda e, sg=sg, psu=psu, fc=fc, c0=c0, n=n: e.tensor_tensor(
                        out=act[:, fc, c0:c0 + n], in0=psu[:, 0:n], in1=sg[:, 0:n], op=ALU.mult),
                        reads=[tpu, tsg], writes=[t_act[fc][ci]])
        pss = {ci: (self.ps[5 + ci], self.t_ps[5 + ci]) for (ci, _, _) in cts}

        def load_down(dc):
            slot, tsl, ssl = self.wslot()
            sv = slot[:, 0:22 * 128].rearrange("p (fc d) -> p fc d", fc=22)
            k.dma("pool", ssl, lambda e: e.dma_start(out=sv[:, :, :], in_=wd[:, :, dc * 128:(dc + 1) * 128]), writes=[tsl])
            return sv, tsl

        def down_tile(sv, tsl, dc, ci, c0, n):
            psd, tpd = self.psum(0, 5)
            for fc in range(FC):
                k.op("pe", lambda e, fc=fc: e.matmul(psd[:, 0:n], lhsT=sv[:, fc, :], rhs=act[:, fc, c0:c0 + n],
                                                     start=(fc == 0), stop=(fc == FC - 1)),
                     reads=[tsl, t_act[fc][ci]], writes=[tpd], inc=(fc == FC - 1))
            k.op("dve", lambda e: e.tensor_copy(out=ybuf[:, dc, c0:c0 + n], in_=psd[:, 0:n]), reads=[tpd], writes=[t_y[ci]])
            sq, tsq = self.nxt("sq")
            k.op("act", lambda e: e.activation(out=sq[:, 0:n], in_=ybuf[:, dc, c0:c0 + n], func=AF.Square), reads=[t_y[ci]], writes=[tsq])
            pS, tpS = pss[ci]
            k.op("pe", lambda e: e.matmul(pS[:, 0:n], lhsT=self.onesb[:], rhs=sq[:, 0:n], start=(dc == 0), stop=(dc == DC - 1)),
                 reads=[tsq, self.t_const], writes=[tpS], inc=True)
        for dc in range(DC - 2):
            sv, tsl = load_down(dc)
            for (ci, c0, n) in cts:
                down_tile(sv, tsl, dc, ci, c0, n)
        last = [(dc,) + load_down(dc) for dc in (DC - 2, DC - 1)]
        for ti, (ci, c0, n) in enumerate(cts):
            for (dc, sv, tsl) in last:
                down_tile(sv, tsl, dc, ci, c0, n)
            self.post_norm(l, jpost, ybuf, t_y, pss, [(ci, c0, n)], nxt, do_barrier=(ti == len(cts) - 1))

    def body(self):
        k = self.k
        nc, st = self.nc, self.st
        self.epsc = st.enter_context(nc.sbuf_tensor("epsc", [P, 1], F32))
        k.op("dve", lambda e: e.memset(self.epsc[:], EPS), writes=[self.t_const])
        self.setup_consts()
        self.setup_gmlp()
        self.setup_hgrn()
        self.setup_s5()
        k.barrier()
        halves = [0, 1] if self.stop is None else self.stop.get("halves", [0, 1])
        for hi_, h in enumerate(halves):
            self.load_x(h, off=(0 if hi_ == 0 else 2 * D))
            k.barrier()
            for l in range(DEPTH):
                if self.stop is not None and l >= self.stop.get("layers", DEPTH):
                    break
                if self.stop is not None and self.stop.get("phase") == "io":
                    continue
                if self.stop is not None and self.stop.get("phase") == "norm":
                    self.arena.reset()
                    self.prenorm(l, 0, h)
                    continue
                chain = False
                self.ffn(l, 0, h, pre=(l == 0 or not chain), nxt=(l, 2) if chain else None)
                if self.stop is not None and self.stop.get("phase") == "ffn1":
                    continue
                self.mixer(l, h, pre=not chain, nxt=(l, 4) if chain else None)
                if self.stop is not None and self.stop.get("phase") in ("gmlp", "s5", "hgrn", "mixer"):
                    continue
                self.ffn(l, 1, h, pre=not chain, nxt=(l + 1, 0) if (chain and l + 1 < DEPTH) else None)
            k.barrier()
            self.store_x(h)
            if hi_ == len(halves) - 1:
                k.barrier()
        eng = k.engs["sp"]
        waits = {}
        for so in self.out_sems:
            if so.count:
                k._need(eng, waits, (so, so.count))
        eng.ops.append((None, list(waits.values()), None))


_CACHE = {}


def get_prog(stop=None):
    key = repr(stop)
    if key not in _CACHE:
        _CACHE[key] = Prog(stop=stop)
    return _CACHE[key]


def make_in_maps(inputs):
    c = host_consts()
    maps = []
    f = lambda a: np.ascontiguousarray(np.asarray(a, dtype=np.float32))
    shared = {n: f(inputs[n]) for n in ("norm_g", "ffn_w_gate", "ffn_w_up", "ffn_w_down", "w_in", "gmlp_ws", "gmlp_bs",
                                        "gmlp_norm_g", "gmlp_norm_b", "hgrn_lb_logits", "hgrn_norm_g", "s5_lam_re", "s5_lam_im", "s5_log_dt",
                                        "s5_b_re", "s5_b_im", "s5_c_re", "s5_c_im", "s5_d", "s5_glu_w", "s5_glu_b", "w_branch", "w_out")}
    xp = f(inputs["x_prompt"])
    xs = f(inputs["x_sample"])
    for core in range(NCORES):
        m = dict(shared)
        m["xp"] = xp[core]
        m["xs"] = xs[core * NS:(core + 1) * NS, 0, :]
        m["c_ident"] = c["ident"]
        m["c_mask128"] = c["mask128"]
        m["c_scanmask"] = c["scanmask"]
        m["c_tri2"] = c["tri2"]
        m["c_pmask"] = c["pmask"]
        m["c_bdmask"] = c["bdmask"]
        m["sre"] = f(inputs["state_s5_re"])[:, core * NS:(core + 1) * NS]
        m["sim"] = f(inputs["state_s5_im"])[:, core * NS:(core + 1) * NS]
        m["shg"] = f(inputs["state_hgrn"])[:, core * NS:(core + 1) * NS]
        maps.append(m)
    return maps


def assemble(r):
    yp = np.stack([r[c]["yp"] for c in range(NCORES)], axis=0)
    ys = np.concatenate([r[c]["ys"] for c in range(NCORES)], axis=0)[:, None, :]
    hgp = np.stack([r[c]["hgp"] for c in range(NCORES)], axis=1)
    rep = np.stack([r[c]["rep"] for c in range(NCORES)], axis=1)
    imp = np.stack([r[c]["imp"] for c in range(NCORES)], axis=1)
    hgs = np.concatenate([r[c]["hgs"] for c in range(NCORES)], axis=1)
    res = np.concatenate([r[c]["res"] for c in range(NCORES)], axis=1)
    ims = np.concatenate([r[c]["ims"] for c in range(NCORES)], axis=1)
    vs = np.concatenate([r[c]["vs"] for c in range(NCORES)], axis=1)[:, :, None, :]
    return tuple(np.ascontiguousarray(a, dtype=np.float32) for a in (yp, ys, hgp, rep, imp, hgs, res, ims, vs))


def kernel(**inputs):
    prog = get_prog()
    res = run_bass_kernel_spmd(prog.nc, make_in_maps(inputs), core_ids=list(range(NCORES)))
    return assemble(res.results)
```

```python
import contextlib
import numpy as np
import concourse.bass as bass
import concourse.mybir as mybir
from concourse.bass_utils import run_bass_kernel_spmd

F32 = mybir.dt.float32
BF16 = mybir.dt.bfloat16
AF = mybir.ActivationFunctionType
ALU = mybir.AluOpType

P = 128
D = 1024
DC = 8
DFF = 2816
FC = 22
NIN = 6656
SEQ = 2048
NPH = 1024
NS = 16
DEPTH = 2
EPS = 1e-6
NCORES = 8


class T:
    __slots__ = ("w", "r", "name")

    def __init__(self, name=""):
        self.w = None
        self.r = []
        self.name = name


class Sem:
    __slots__ = ("h", "count", "name")

    def __init__(self, h, name):
        self.h = h
        self.count = 0
        self.name = name


class Eng:
    def __init__(self, name, sem):
        self.name = name
        self.sem = sem
        self.seen = {}
        self.ops = []


class K:
    def __init__(self, nc, stack):
        self.nc = nc
        self.stack = stack
        self.engs = {}
        for n in ("pe", "act", "dve", "pool", "sp"):
            s = Sem(stack.enter_context(nc.semaphore("s_" + n)), n)
            self.engs[n] = Eng(n, s)
        self.maxwait = {}
        self.dsems = []

    def dsem(self, name):
        s = Sem(self.stack.enter_context(self.nc.semaphore("d_" + name)), name)
        self.dsems.append(s)
        return s

    def _need(self, eng, waits, ev):
        s, v = ev
        if eng.seen.get(id(s), 0) >= v:
            return
        eng.seen[id(s)] = v
        waits[id(s)] = (s, v)
        if v > self.maxwait.get(id(s), (s, 0))[1]:
            self.maxwait[id(s)] = (s, v)

    def _deps(self, eng, reads, writes, own_sem):
        waits = {}
        for t in reads:
            if t.w is not None:
                self._need(eng, waits, t.w)
        for t in writes:
            if t.w is not None and t.w[0] is not own_sem:
                self._need(eng, waits, t.w)
            for ev in t.r:
                if ev[0] is not own_sem:
                    self._need(eng, waits, ev)
        return list(waits.values())

    def op(self, en, fn, reads=(), writes=(), inc=True):
        eng = self.engs[en]
        waits = self._deps(eng, reads, writes, eng.sem if en == "pe" else None)
        ev = (eng.sem, eng.sem.count + 1)
        if inc:
            eng.sem.count += 1
        for t in reads:
            t.r.append(ev)
        for t in writes:
            t.w = ev
            t.r = []
        eng.ops.append((fn, waits, (eng.sem, 1) if inc else None))

    def dma(self, en, sem, fn, reads=(), writes=()):
        eng = self.engs[en]
        waits = self._deps(eng, reads, writes, None)
        sem.count += 16
        ev = (sem, sem.count)
        for t in reads:
            t.r.append(ev)
        for t in writes:
            t.w = ev
            t.r = []
        eng.ops.append((fn, waits, (sem, 16)))

    def barrier(self, exclude=()):
        evs = [(e.sem, e.sem.count) for e in self.engs.values() if e.sem.count > 0]
        evs += [(s, s.count) for s in self.dsems if s.count > 0]
        for en_, eng in self.engs.items():
            if en_ in exclude:
                continue
            waits = {}
            for ev in evs:
                if ev[0] is not eng.sem:
                    self._need(eng, waits, ev)
            if waits:
                eng.ops.append((None, list(waits.values()), None))

    def emit(self):
        nc = self.nc
        for s, v in self.maxwait.values():
            assert v <= s.count, f"wait on {s.name} for {v} > {s.count}"
        hmap = {"pe": "tensor", "act": "scalar", "dve": "vector", "pool": "gpsimd", "sp": "sync"}
        with nc.Block() as block:
            for en, eng in self.engs.items():
                def body(e, eng=eng):
                    for fn, waits, inc in eng.ops:
                        if fn is None:
                            for s, v in waits:
                                e.wait_ge(s.h, v)
                            continue
                        for s, v in waits[1:]:
                            e.wait_ge(s.h, v)
                        ins = fn(e)
                        if waits:
                            ins._wait_ge(waits[0][0].h, waits[0][1])
                        if inc is not None:
                            ins.then_inc(inc[0].h, inc[1])
                getattr(block, hmap[en])(body)


class Arena:
    def __init__(self, ap, nwords):
        self.ap = ap
        self.n = nwords
        self.off = 0
        self.peak = 0

    def reset(self):
        self.off = 0

    def f32(self, shape):
        n = int(np.prod(shape))
        a = self.ap[:, self.off:self.off + n]
        self.off += n
        self.peak = max(self.peak, self.off)
        assert self.off <= self.n, f"arena overflow {self.off} > {self.n}"
        if len(shape) == 2:
            return a.rearrange("p (a b) -> p a b", a=shape[0])
        if len(shape) == 3:
            return a.rearrange("p (a b c) -> p a b c", a=shape[0], b=shape[1])
        return a

    def bf16(self, shape):
        n = int(np.prod(shape))
        nw = (n + 1) // 2
        a = self.ap[:, self.off:self.off + nw].bitcast(BF16)[:, 0:n]
        self.off += nw
        self.peak = max(self.peak, self.off)
        assert self.off <= self.n, f"arena overflow {self.off} > {self.n}"
        if len(shape) == 2:
            return a.rearrange("p (a b) -> p a b", a=shape[0])
        if len(shape) == 3:
            return a.rearrange("p (a b c) -> p a b c", a=shape[0], b=shape[1])
        return a


def host_consts():
    c = {}
    c["ident"] = np.eye(128, dtype=np.float32)
    i = np.arange(128)
    c["mask128"] = (i[:, None] <= i[None, :]).astype(np.float32)
    sm = np.ones((128, 528), np.float32)
    sm[:, 0:512:64] = 0.0
    sm[:, 512:] = 0.0
    c["scanmask"] = sm
    pm = np.zeros((128, 4), np.float32)
    pm[:, 0] = ((i // 16) % 2 == 0)
    pm[:, 1] = ((i // 16) % 2 == 1)
    pm[:, 2] = np.where(i < 64, -1.0, 1.0)
    pm[:, 3] = -pm[:, 2]
    c["pmask"] = pm
    c["bdmask"] = ((i[:, None] // 16) == (i[None, :] // 16)).astype(np.float32)
    c["tri2"] = ((i[:, None] % 64) <= np.arange(64)[None, :]).astype(np.float32)
    return c


class Prog:
    def __init__(self, stop=None, dbg=None):
        self.stop = stop
        self.dbg = dbg
        nc = bass.Bass("TRN2", target_bir_lowering=False)
        self.nc = nc
        with contextlib.ExitStack() as st:
            self.st = st
            self.k = K(nc, st)
            self.declare_io()
            self.alloc()
            self.body()
            self.k.emit()

    def declare_io(self):
        nc = self.nc

        def din(name, shape):
            return nc.dram_tensor(name, list(shape), F32, kind="ExternalInput").ap()

        def dout(name, shape):
            return nc.dram_tensor(name, list(shape), F32, kind="ExternalOutput").ap()

        self.xp = din("xp", [SEQ, D])
        self.xs = din("xs", [NS, D])
        self.norm_g = din("norm_g", [DEPTH, 6, D])
        self.w_gate = din("ffn_w_gate", [DEPTH, 2, D, DFF])
        self.w_up = din("ffn_w_up", [DEPTH, 2, D, DFF])
        self.w_down = din("ffn_w_down", [DEPTH, 2, DFF, D])
        self.w_in = din("w_in", [DEPTH, D, NIN])
        self.gmlp_ws = din("gmlp_ws", [DEPTH, 4, 128, 128])
        self.gmlp_bs = din("gmlp_bs", [DEPTH, 4, 128])
        self.gmlp_ng = din("gmlp_norm_g", [DEPTH, 512])
        self.gmlp_nb = din("gmlp_norm_b", [DEPTH, 512])
        self.lb_logits = din("hgrn_lb_logits", [DEPTH, 512])
        self.hgrn_ng = din("hgrn_norm_g", [DEPTH, 512])
        self.shg = din("shg", [DEPTH, NS, 4, 128, 128])
        self.c_scanmask = din("c_scanmask", [128, 528])
        self.c_tri2 = din("c_tri2", [128, 64])
        self.lam_re = din("s5_lam_re", [DEPTH, 32, 64])
        self.lam_im = din("s5_lam_im", [DEPTH, 32, 64])
        self.log_dt = din("s5_log_dt", [DEPTH, 32])
        self.b_re = din("s5_b_re", [DEPTH, 32, 64, 16])
        self.b_im = din("s5_b_im", [DEPTH, 32, 64, 16])
        self.c_re = din("s5_c_re", [DEPTH, 32, 16, 64])
        self.c_im = din("s5_c_im", [DEPTH, 32, 16, 64])
        self.s5_d = din("s5_d", [DEPTH, 512])
        self.glu_w = din("s5_glu_w", [DEPTH, 512, 1024])
        self.glu_b = din("s5_glu_b", [DEPTH, 1024])
        self.w_branch = din("w_branch", [DEPTH, 3, 512, D])
        self.w_out = din("w_out", [DEPTH, D, D])
        self.sre = din("sre", [DEPTH, NS, 32, 64])
        self.sim = din("sim", [DEPTH, NS, 32, 64])
        self.s5scr = nc.dram_tensor("s5scr", [DEPTH, 4, 128, 5120], BF16, kind="Internal").ap()
        self.t_scr = [[T(f"scr{l}{c}") for c in range(4)] for l in range(DEPTH)]
        self.s5rot = nc.dram_tensor("s5rot", [DEPTH, 128, 3, 32, 128], F32, kind="Internal").ap()
        self.t_rot = [T(f"rot{l}") for l in range(DEPTH)]
        self.c_pmask = din("c_pmask", [128, 4])
        self.c_bdmask = din("c_bdmask", [128, 128])
        self.c_ident = din("c_ident", [128, 128])
        self.c_mask128 = din("c_mask128", [128, 128])
        self.yp = dout("yp", [SEQ, D])
        self.ys = dout("ys", [NS, D])
        self.vs_out = dout("vs", [DEPTH, NS, 512])
        self.rep_out = dout("rep", [DEPTH, 32, 64])
        self.imp_out = dout("imp", [DEPTH, 32, 64])
        self.res_out = dout("res", [DEPTH, NS, 32, 64])
        self.ims_out = dout("ims", [DEPTH, NS, 32, 64])
        self.hgp_out = dout("hgp", [DEPTH, 4, 128, 128])
        self.hgs_out = dout("hgs", [DEPTH, NS, 4, 128, 128])
        if self.dbg:
            self.dbg_out = dout("dbg", [128, self.dbg])
            self.dbg_off = 0
            self.dbg_map = {}

    def alloc(self):
        nc, st, k = self.nc, self.st, self.k

        self.sb_bytes = {}

        def sb(name, shape, dt):
            self.sb_bytes[name] = int(np.prod(shape[1:])) * (2 if dt == BF16 else 4)
            return st.enter_context(nc.sbuf_tensor(name, list(shape), dt))

        NCM = NPH + NS
        self.xT = sb("xT", [P, DC, NCM], F32)
        self.hT = sb("hT", [P, DC, NCM], BF16)
        self.t_x = [T(f"x{c}") for c in range(3)]
        self.t_h = [T(f"h{c}") for c in range(3)]
        self.NSLOT = 3
        self.SLOTW = 4096
        self.ring = [sb(f"ring{i}", [P, self.SLOTW], BF16) for i in range(self.NSLOT)]
        self.t_ring = [T(f"ring{i}") for i in range(self.NSLOT)]
        self.s_ring = [k.dsem(f"ring{i}") for i in range(self.NSLOT)]
        self.ring_i = 0
        self.ARENA_W = 22 * 1024
        ar = sb("arena", [P, self.ARENA_W], F32)
        self.arena = Arena(ar, self.ARENA_W)
        self.ident = sb("ident", [P, P], F32)
        self.identb = sb("identb", [P, P], BF16)
        self.onesb = sb("onesb", [P, P], BF16)
        self.gstage = sb("gstage", [P, P], F32)
        self.gcol = sb("gcol", [P, 96], F32)
        self.sq = [sb(f"sq{i}", [P, 512], BF16) for i in range(3)]
        self.t_sq = [T(f"sq{i}") for i in range(3)]
        self.sq_i = 0
        self.rs = [sb(f"rs{i}", [P, 512], F32) for i in range(3)]
        self.t_rs = [T(f"rs{i}") for i in range(3)]
        self.rs_i = 0
        self.sg = [sb(f"sg{i}", [P, 512], BF16) for i in range(3)]
        self.t_sg = [T(f"sg{i}") for i in range(3)]
        self.sg_i = 0
        self.t_const = T("const")
        self.s_const = k.dsem("const")
        self.s_vs = k.dsem("vs")
        self.s_bsr = k.dsem("bsr")
        self.t_stage_in = [T(f"stage{i}") for i in range(4)]
        self.s_stage_in = [k.dsem(f"stin{i}") for i in range(4)]
        self.t_stage_out = [T(f"ostage{i}") for i in range(2)]
        self.s_stage_out = [k.dsem(f"stout{i}") for i in range(2)]
        self.out_sems = list(self.s_stage_out) + [self.s_vs]
        self.wmT = sb("wmT", [P, DEPTH, 4, 128], BF16)
        self.ngB = sb("ngB", [P, DEPTH, 512], F32)
        self.nbB = sb("nbB", [P, DEPTH, 512], F32)
        self.onesf = sb("onesf", [NS, 128], F32)
        self.S32 = sb("S32", [P, DEPTH, 4, 128], F32)
        self.Sb = sb("Sb", [P, DEPTH, 4, 128], BF16)
        self.t_S32 = [[T(f"S32_{l}{hh}") for hh in range(4)] for l in range(DEPTH)]
        self.t_Sb = [[T(f"Sb_{l}{hh}") for hh in range(4)] for l in range(DEPTH)]
        self.hcol = sb("hcol", [P, 16], F32)
        self.lbc = sb("lbc", [P, DEPTH, 4], F32)
        self.omlc = sb("omlc", [P, DEPTH, 4], F32)
        self.scanmask = sb("scanmask", [P, 528], F32)
        self.tri2 = sb("tri2", [P, 64], F32)
        self.s5tab = sb("s5tab", [P, DEPTH, 8, 2, 32], F32)
        self.s5coef = sb("s5coef", [P, DEPTH, 2, 32], F32)
        self.Hprev = sb("Hprev", [P, DEPTH, 2, 32], F32)
        self.Hnew = sb("Hnew", [P, 2, 32], F32)
        self.t_Hnew = T("Hnew")
        self.pmask = sb("pmask", [P, 4], F32)
        self.bdmask = sb("bdmask", [P, P], F32)
        self.halfpi = sb("halfpi", [P, 1], F32)
        self.scol = sb("scol", [P, 24], F32)
        self.s_s5in = [k.dsem(f"s5in{i}") for i in range(3)]
        self.s_s5x = [k.dsem(f"s5x{i}") for i in range(2)]
        self.s_s5ck = k.dsem("s5ck")
        self.s_s5p = k.dsem("s5p")
        self.s_s5s = k.dsem("s5s")
        self.out_sems += [self.s_s5p, self.s_s5s]
        self.s_hg = [k.dsem(f"hg{i}") for i in range(2)]
        self.s_hgp = k.dsem("hgp")
        self.out_sems += self.s_hg + [self.s_hgp]
        self.wsc = sb("wsc", [P, DEPTH, 4], F32)
        self.bsc = sb("bsc", [P, DEPTH, 4], F32)
        self.mask128 = sb("mask128", [P, P], F32)
        self.st6 = [sb(f"st6_{i}", [P, 8], F32) for i in range(2)]
        self.t_st6 = [T(f"st6_{i}") for i in range(2)]
        self.st6_i = 0
        if self.dbg:
            self.dbgst = sb("dbgst", [P, 1040], F32)
            self.t_dbgst = T("dbgst")
            self.s_dbg = k.dsem("dbg")
            self.out_sems.append(self.s_dbg)
        self.ps = [st.enter_context(nc.psum_tensor(f"ps{i}", [P, 512], F32)) for i in range(8)]
        self.t_ps = [T(f"ps{i}") for i in range(8)]
        self.ps_i = 0

    def psum(self, lo=0, hi=8):
        if not (lo <= self.ps_i < hi):
            self.ps_i = lo
        i = self.ps_i
        self.ps_i = lo + (i + 1 - lo) % (hi - lo)
        return self.ps[i], self.t_ps[i]

    def nxt(self, what):
        lst = getattr(self, what)
        tl = getattr(self, "t_" + what)
        i = getattr(self, what + "_i")
        setattr(self, what + "_i", (i + 1) % len(lst))
        return lst[i], tl[i]

    def cts(self, h):
        c = [(0, 0, 512), (1, 512, 512)]
        if h == 0:
            c.append((2, 1024, NS))
        return c

    def wslot(self):
        i = self.ring_i
        self.ring_i = (i + 1) % self.NSLOT
        return self.ring[i], self.t_ring[i], self.s_ring[i]

    def setup_consts(self):
        k = self.k
        tc = self.t_const
        k.dma("sp", self.s_const, lambda e: e.dma_start(out=self.ident[:], in_=self.c_ident[:, :]), writes=[tc])
        k.dma("sp", self.s_const, lambda e: e.dma_start(
            out=self.gstage[0:96, :], in_=self.norm_g.rearrange("l j (c p) -> (l j c) p", p=P)), writes=[tc])
        k.op("dve", lambda e: e.tensor_copy(out=self.identb[:], in_=self.ident[:]), reads=[tc], writes=[tc])
        k.op("dve", lambda e: e.memset(self.onesb[:], 1.0), writes=[tc])
        ps, tp = self.psum()
        k.op("pe", lambda e: e.transpose(ps[:, 0:96], self.gstage[0:96, :], self.ident[0:96, 0:96]),
             reads=[tc], writes=[tp])
        k.op("dve", lambda e: e.tensor_copy(out=self.gcol[:], in_=ps[:, 0:96]), reads=[tp], writes=[tc])
        for l in range(DEPTH):
            for j in (1, 5):
                o = (l * 6 + j) * 8
                k.op("dve", lambda e, o=o: e.tensor_scalar(out=self.gcol[:, o:o + 8], in0=self.gcol[:, o:o + 8],
                                                           scalar1=0.5, scalar2=None, op0=ALU.mult),
                     reads=[tc], writes=[tc])

    def setup_gmlp(self):
        k = self.k
        tc = self.t_const
        sc = self.s_const
        k.dma("act", sc, lambda e: e.dma_start(out=self.mask128[:], in_=self.c_mask128[:, :]), writes=[tc])
        k.op("dve", lambda e: e.memset(self.onesf[:], 1.0), writes=[tc])
        for l in range(DEPTH):
            def one_layer(l):
                k.dma("act", sc, lambda e: e.dma_start(out=self.ngB[:, l, :], in_=self.gmlp_ng[l:l + 1, :].partition_broadcast(P)),
                      writes=[tc])
                k.dma("act", sc, lambda e: e.dma_start(out=self.nbB[:, l, :], in_=self.gmlp_nb[l:l + 1, :].partition_broadcast(P)),
                      writes=[tc])
                for g in range(4):
                    def one_g(g):
                        k.dma("act", sc, lambda e: e.dma_start(
                            out=self.wsc[:, l, g:g + 1], in_=self.gmlp_ws[l, g, 0:1, 0:1].partition_broadcast(P)), writes=[tc])
                        k.dma("act", sc, lambda e: e.dma_start(
                            out=self.bsc[:, l, g:g + 1], in_=self.gmlp_bs[l, g:g + 1, 0:1].partition_broadcast(P)), writes=[tc])
                        tg = T("wstage")
                        sg_ = k.dsem(f"wst{l}{g}")
                        k.dma("act", sg_, lambda e: e.dma_start(out=self.gstage[:, :], in_=self.gmlp_ws[l, g, :, :]),
                              reads=[tc], writes=[tc, tg])
                        ps, tp = self.psum()
                        k.op("pe", lambda e: e.transpose(ps[:, 0:128], self.gstage[:, :], self.ident[:]),
                             reads=[tc, tg], writes=[tp])
                        k.op("dve", lambda e: e.tensor_tensor(out=self.wmT[:, l, g, :], in0=ps[:, 0:128], in1=self.mask128[:],
                                                              op=ALU.mult), reads=[tp, tc], writes=[tc])
                    one_g(g)
            one_layer(l)

    def setup_hgrn(self):
        k = self.k
        tc, sc = self.t_const, self.s_const
        k.dma("act", sc, lambda e: e.dma_start(out=self.scanmask[:], in_=self.c_scanmask[:, :]), writes=[tc])
        k.dma("act", sc, lambda e: e.dma_start(out=self.tri2[:], in_=self.c_tri2[:, :]), writes=[tc])
        tg = T("hstage")
        sg_ = k.dsem("hstage")
        k.dma("act", sg_, lambda e: e.dma_start(out=self.gstage[0:8, :], in_=self.lb_logits.rearrange("l (h p) -> (l h) p", p=P)),
              reads=[tc], writes=[tc, tg])
        k.dma("act", sg_, lambda e: e.dma_start(out=self.gstage[8:16, :], in_=self.hgrn_ng.rearrange("l (h p) -> (l h) p", p=P)),
              writes=[tg])
        ps, tp = self.psum()
        k.op("pe", lambda e: e.transpose(ps[:, 0:16], self.gstage[0:16, :], self.ident[0:16, 0:16]), reads=[tc, tg], writes=[tp])
        k.op("dve", lambda e: e.tensor_copy(out=self.hcol[:], in_=ps[:, 0:16]), reads=[tp], writes=[tc])
        k.op("dve", lambda e: e.memset(self.lbc[:, 0, :], 0.0), writes=[tc])
        k.op("dve", lambda e: e.tensor_tensor(out=self.lbc[:, 1, :], in0=self.hcol[:, 4:8], in1=self.hcol[:, 0:4], op=ALU.subtract),
             reads=[tc], writes=[tc])
        k.op("act", lambda e: e.activation(out=self.lbc[:, 1, :], in_=self.lbc[:, 1, :], func=AF.Sigmoid), reads=[tc], writes=[tc])
        k.op("dve", lambda e: e.tensor_scalar(out=self.omlc[:], in0=self.lbc[:], scalar1=-1.0, scalar2=1.0, op0=ALU.mult, op1=ALU.add),
             reads=[tc], writes=[tc])
        k.op("dve", lambda e: e.memset(self.S32[:], 0.0), writes=[t for l in range(DEPTH) for t in self.t_S32[l]])
        k.op("dve", lambda e: e.memset(self.Sb[:], 0.0), writes=[t for l in range(DEPTH) for t in self.t_Sb[l]])

    def hgrn_tile(self, l, h, ci, aout, t_a):
        k = self.k
        ar = self.arena
        c0 = ci * 512
        samp = (h == 0 and ci == 1)
        NW = 512 + (NS if samp else 0)
        NWA = 512 + NS
        ABC = [(ar.f32([NWA])[:, 0:NW], ar.f32([NWA])[:, 0:NW], ar.f32([NWA])[:, 0:NW]) for _ in range(2)]
        tABC = [(self.pt(f"A{i}"), self.pt(f"Bk{i}"), self.pt(f"C{i}")) for i in range(2)]
        QB = ar.bf16([4, 512]); KBh = ar.bf16([4, 512]); KD = ar.bf16([4, 512])
        tQB = [self.pt(f"QB{i}") for i in range(4)]; tKB = [self.pt(f"KB{i}") for i in range(4)]; tKD = [self.pt(f"KD{i}") for i in range(4)]
        ebl = ar.f32([4, 8]); t_ebl = [self.pt(f"ebl{i}") for i in range(4)]
        vtok = ar.bf16([4, 512]); t_vtok = [self.pt(f"hv{i}") for i in range(4)]
        kdtok = ar.bf16([4, 512]); t_kdtok = [self.pt(f"kdt{i}") for i in range(4)]
        AmT = ar.bf16([4, 4 * 64]); t_AmT = [self.pt(f"AmT{i}") for i in range(4)]
        sgh = ar.bf16([4, NWA]); t_sgh = [self.pt(f"sgh{i}") for i in range(4)]
        O = [ar.f32([512]) for _ in range(2)]; tO = [self.pt("O0"), self.pt("O1")]
        if samp:
            f_s = ar.f32([4, NS]); kk_s = ar.f32([4, NS]); q_s = ar.f32([4, NS])
            t_fs, t_ks, t_qs = self.pt("f_s"), self.pt("kk_s"), self.pt("q_s")
            v_s = ar.f32([512]); t_vs = self.pt("hv_s")
            vmask = [ar.f32([512]) for _ in range(2)]; t_vm = [self.pt("vm0"), self.pt("vm1")]
            Sj = [ar.f32([4, 128]) for _ in range(2)]; t_Sj = [self.pt("Sj0"), self.pt("Sj1")]
        svf, tsf = self.wblock(l, 512)
        svq, tsq_ = self.wblock(l, 0)

        def head(hh):
            A, Bk, C = ABC[hh % 2]
            tA, tB, tC = tABC[hh % 2]
            ps, tp = self.psum(0, 8)
            self.proj_fm(svf, tsf, hh, ci, c0, 512, ps, tp)
            k.op("act", lambda e: e.activation(out=A[:, 0:512], in_=ps[:, 0:512], func=AF.Sigmoid), reads=[tp], writes=[tA])
            if samp:
                ps2, tp2 = self.psum(0, 8)
                self.proj_fm(svf, tsf, hh, 2, NPH, NS, ps2, tp2)
                k.op("act", lambda e: e.activation(out=A[:, 512:NW], in_=ps2[:, 0:NS], func=AF.Sigmoid), reads=[tp2], writes=[tA])
            k.op("dve", lambda e: e.tensor_scalar(out=A[:, :], in0=A[:, :], scalar1=self.omlc[:, l, hh:hh + 1],
                                                  scalar2=self.lbc[:, l, hh:hh + 1], op0=ALU.mult, op1=ALU.add),
                 reads=[tA, self.t_const], writes=[tA])
            k.op("dve", lambda e: e.tensor_scalar(out=Bk[:, :], in0=A[:, :], scalar1=-1.0, scalar2=1.0, op0=ALU.mult, op1=ALU.add),
                 reads=[tA], writes=[tB])
            if samp:
                k.op("dve", lambda e: e.tensor_copy(out=f_s[:, hh, :], in_=A[:, 512:NW]), reads=[tA], writes=[t_fs])
                k.op("dve", lambda e: e.tensor_copy(out=kk_s[:, hh, :], in_=Bk[:, 512:NW]), reads=[tB], writes=[t_ks])
            k.op("act", lambda e: e.activation(out=C[:, :], in_=A[:, :], func=AF.Ln), reads=[tA], writes=[tC])
            k.op("dve", lambda e: e.tensor_tensor_scan(out=A[:, :], data0=self.scanmask[:, 0:NW], data1=C[:, :], initial=0.0,
                                                       op0=ALU.mult, op1=ALU.add), reads=[tC, self.t_const], writes=[tA])
            k.op("act", lambda e: e.activation(out=C[:, :], in_=A[:, :], func=AF.Exp), reads=[tA], writes=[tC])
            k.op("act", lambda e: e.activation(out=A[:, :], in_=A[:, :], func=AF.Exp, scale=-1.0), reads=[tA], writes=[tA])
            k.op("dve", lambda e: e.tensor_tensor(out=Bk[:, 0:512], in0=Bk[:, 0:512], in1=A[:, 0:512], op=ALU.mult),
                 reads=[tB, tA], writes=[tB])
            k.op("act", lambda e: e.activation(out=KBh[:, hh, :], in_=Bk[:, 0:512], func=AF.Copy), reads=[tB], writes=[tKB[hh]])
            k.op("dve", lambda e: e.tensor_tensor(
                out=KD[:, hh, :].rearrange("p (c t) -> p c t", t=64), in0=Bk[:, 0:512].rearrange("p (c t) -> p c t", t=64),
                in1=C[:, 0:512].rearrange("p (c t) -> p c t", t=64)[:, :, 63:64].to_broadcast([P, 8, 64]), op=ALU.mult),
                reads=[tB, tC], writes=[tKD[hh]])
            k.op("dve", lambda e: e.tensor_copy(out=ebl[:, hh, :], in_=C[:, 0:512].rearrange("p (c t) -> p c t", t=64)[:, :, 63]),
                 reads=[tC], writes=[t_ebl[hh]])
            psq, tpq = self.psum(0, 8)
            self.proj_fm(svq, tsq_, hh, ci, c0, 512, psq, tpq)
            k.op("dve", lambda e: e.tensor_tensor(out=QB[:, hh, :], in0=psq[:, 0:512], in1=C[:, 0:512], op=ALU.mult),
                 reads=[tpq, tC], writes=[tQB[hh]])
            if samp:
                psq2, tpq2 = self.psum(0, 8)
                self.proj_fm(svq, tsq_, hh, 2, NPH, NS, psq2, tpq2)
                k.op("dve", lambda e: e.tensor_copy(out=q_s[:, hh, :], in_=psq2[:, 0:NS]), reads=[tpq2], writes=[t_qs])
        for hh in range(4):
            head(hh)
        svi, tsi = self.wblock(l, 1024)

        def vslice(s4):
            ps, tp = self.psum(0, 8)
            self.proj_tm(svi, tsi, ci, c0 + s4 * 128, 128, ps, tp)
            k.op("act", lambda e: e.activation(out=vtok[:, s4, :], in_=ps[:, :], func=AF.Copy), reads=[tp], writes=[t_vtok[s4]])
        for s4 in range(4):
            vslice(s4)
        if samp:
            ps, tp = self.psum(0, 8)
            self.proj_tm(svi, tsi, 2, NPH, NS, ps, tp)
            k.op("act", lambda e: e.activation(out=v_s[0:NS, :], in_=ps[0:NS, :], func=AF.Copy), reads=[tp], writes=[t_vs])
        svg, tsg_ = self.wblock(l, 1536)

        def ghead(hh):
            ps, tp = self.psum(0, 8)
            self.proj_fm(svg, tsg_, hh, ci, c0, 512, ps, tp)
            k.op("act", lambda e: e.activation(out=sgh[:, hh, 0:512], in_=ps[:, 0:512], func=AF.Silu), reads=[tp], writes=[t_sgh[hh]])
            if samp:
                ps2, tp2 = self.psum(0, 8)
                self.proj_fm(svg, tsg_, hh, 2, NPH, NS, ps2, tp2)
                k.op("act", lambda e: e.activation(out=sgh[:, hh, 512:NW], in_=ps2[:, 0:NS], func=AF.Silu), reads=[tp2], writes=[t_sgh[hh]])
        for hh in range(4):
            ghead(hh)

        def kdT(s4):
            ps, tp = self.psum(0, 8)
            psb = ps[:, 0:256].bitcast(BF16)
            for hh in range(4):
                k.op("pe", lambda e, hh=hh: e.transpose(psb[:, hh * 128:(hh + 1) * 128], KD[:, hh, s4 * 128:(s4 + 1) * 128],
                                                        self.identb[:]),
                     reads=[tKD[hh], self.t_const], writes=[tp], inc=(hh == 3))
            k.op("act", lambda e: e.activation(out=kdtok[:, s4, :], in_=psb[:, :], func=AF.Copy), reads=[tp], writes=[t_kdtok[s4]])
        for s4 in range(4):
            kdT(s4)

        def scores(hh):
            ps, tp = self.psum(0, 8)
            for c in range(8):
                b0 = (c % 2) * 64
                pr = c // 2
                k.op("pe", lambda e, c=c, b0=b0, pr=pr: e.matmul(ps[b0:b0 + 64, pr * 64:(pr + 1) * 64],
                                                                 lhsT=KBh[:, hh, c * 64:(c + 1) * 64], rhs=QB[:, hh, c * 64:(c + 1) * 64],
                                                                 start=True, stop=True),
                     reads=[tKB[hh], tQB[hh]], writes=[tp], inc=(c == 7))
            k.op("dve", lambda e: e.tensor_tensor(out=AmT[:, hh, :].rearrange("p (a t) -> p a t", t=64),
                                                  in0=ps[:, 0:256].rearrange("p (a t) -> p a t", t=64),
                                                  in1=self.tri2[:, :].unsqueeze(1).to_broadcast([P, 4, 64]), op=ALU.mult),
                 reads=[tp, self.t_const], writes=[t_AmT[hh]])
        for hh in range(4):
            scores(hh)
        po = [(self.ps[hh], self.t_ps[hh]) for hh in range(4)]
        psu_i = [0]

        def chunk(c, hh):
            b0 = (c % 2) * 64
            s4 = c // 2
            pO, tpO = po[hh]
            k.op("pe", lambda e: e.matmul(pO[:, c * 64:(c + 1) * 64], lhsT=self.Sb[:, l, hh, :], rhs=QB[:, hh, c * 64:(c + 1) * 64],
                                          start=True, stop=False), reads=[self.t_Sb[l][hh], tQB[hh]], writes=[tpO], inc=False)
            k.op("pe", lambda e: e.matmul(pO[:, c * 64:(c + 1) * 64], lhsT=vtok[b0:b0 + 64, s4, hh * 128:(hh + 1) * 128],
                                          rhs=AmT[b0:b0 + 64, hh, s4 * 64:(s4 + 1) * 64], start=False, stop=True),
                 reads=[t_vtok[s4], t_AmT[hh]], writes=[tpO], inc=True)
            bi = 4 + (psu_i[0] % 2)
            psu_i[0] += 1
            pU, tpU = self.ps[bi], self.t_ps[bi]
            k.op("pe", lambda e: e.matmul(pU[:, 0:128], lhsT=kdtok[b0:b0 + 64, s4, hh * 128:(hh + 1) * 128],
                                          rhs=vtok[b0:b0 + 64, s4, hh * 128:(hh + 1) * 128], start=True, stop=True),
                 reads=[t_kdtok[s4], t_vtok[s4]], writes=[tpU], inc=True)
            k.op("dve", lambda e: e.scalar_tensor_tensor(out=self.S32[:, l, hh, :], in0=self.S32[:, l, hh, :], scalar=ebl[:, hh, c:c + 1],
                                                         in1=pU[:, 0:128], op0=ALU.mult, op1=ALU.add),
                 reads=[self.t_S32[l][hh], t_ebl[hh], tpU], writes=[self.t_S32[l][hh]])
            k.op("act", lambda e: e.activation(out=self.Sb[:, l, hh, :], in_=self.S32[:, l, hh, :], func=AF.Copy),
                 reads=[self.t_S32[l][hh]], writes=[self.t_Sb[l][hh]])
        for c in range(8):
            for hh in range(4):
                chunk(c, hh)

        def onorm(hh, pO, tpO, n, acol0, scol0):
            o, to = O[hh % 2], tO[hh % 2]
            k.op("dve", lambda e: e.tensor_copy(out=o[:, 0:n], in_=pO[:, 0:n]), reads=[tpO], writes=[to])
            sq, tsq = self.nxt("sq")
            k.op("act", lambda e: e.activation(out=sq[:, 0:n], in_=o[:, 0:n], func=AF.Square), reads=[to], writes=[tsq])
            pn, tpn = self.psum(6, 8)
            k.op("pe", lambda e: e.matmul(pn[:, 0:n], lhsT=self.onesb[:], rhs=sq[:, 0:n], start=True, stop=True),
                 reads=[tsq, self.t_const], writes=[tpn])
            rs, trs = self.rstd_from(pn, tpn, n, 128)
            k.op("dve", lambda e: e.tensor_tensor(out=o[:, 0:n], in0=o[:, 0:n], in1=rs[:, 0:n], op=ALU.mult), reads=[to, trs], writes=[to])
            k.op("dve", lambda e: e.scalar_tensor_tensor(out=aout[:, hh, acol0:acol0 + n], in0=o[:, 0:n],
                                                         scalar=self.hcol[:, 8 + l * 4 + hh:9 + l * 4 + hh],
                                                         in1=sgh[:, hh, scol0:scol0 + n], op0=ALU.mult, op1=ALU.mult),
                 reads=[to, self.t_const, t_sgh[hh]], writes=[t_a[2 if acol0 >= NPH else ci]])
        for hh in range(4):
            onorm(hh, po[hh][0], po[hh][1], 512, c0, 0)
        if samp:
            pso, tpso = self.ps[5], self.t_ps[5]

            def sample(j):
                sj, tsj, ssj = Sj[j % 2], t_Sj[j % 2], self.s_hg[j % 2]
                vm, tvm = vmask[j % 2], t_vm[j % 2]
                k.dma("sp", ssj, lambda e: e.dma_start(out=sj[:, :, :], in_=self.shg[l, j].rearrange("h k v -> k h v")), writes=[tsj])
                k.op("dve", lambda e: e.tensor_scalar(out=vm[0:NS, :], in0=v_s[0:NS, :], scalar1=self.ident[0:NS, j:j + 1], scalar2=None,
                                                      op0=ALU.mult), reads=[t_vs, self.t_const], writes=[tvm])
                pv, tpv = self.psum(6, 8)
                k.op("pe", lambda e: e.matmul(pv[:, :], lhsT=self.onesf[0:NS, :], rhs=vm[0:NS, :], start=True, stop=True),
                     reads=[tvm, self.t_const], writes=[tpv])
                for hh in range(4):
                    k.op("dve", lambda e, hh=hh: e.tensor_scalar(out=sj[:, hh, :], in0=sj[:, hh, :], scalar1=f_s[:, hh, j:j + 1],
                                                                 scalar2=None, op0=ALU.mult), reads=[tsj, t_fs], writes=[tsj])
                    k.op("dve", lambda e, hh=hh: e.scalar_tensor_tensor(out=sj[:, hh, :], in0=pv[:, hh * 128:(hh + 1) * 128],
                                                                        scalar=kk_s[:, hh, j:j + 1], in1=sj[:, hh, :],
                                                                        op0=ALU.mult, op1=ALU.add),
                         reads=[tpv, t_ks, tsj], writes=[tsj])
                k.dma("sp", ssj, lambda e: e.dma_start(out=self.hgs_out[l, j].rearrange("h k v -> k h v"), in_=sj[:, :, :]), reads=[tsj])
                for hh in range(4):
                    k.op("pe", lambda e, hh=hh: e.matmul(pso[:, hh * NS + j:hh * NS + j + 1], lhsT=sj[:, hh, :], rhs=q_s[:, hh, j:j + 1],
                                                         start=True, stop=True), reads=[tsj, t_qs], writes=[tpso], inc=(hh == 3))
            for j in range(NS):
                sample(j)
            for hh in range(4):
                onorm(hh, self.ps[5][:, hh * NS:(hh + 1) * NS], tpso, NS, NPH, 512)

    PW = [1, 2, 3, 4, 5, 6, 7, 8]

    def tab(self, l, m, which):
        return self.s5tab[:, l, self.PW.index(m), which, :]

    def setup_s5(self):
        k = self.k
        tc, sc = self.t_const, self.s_const
        nc, st = self.nc, self.st

        def sbt(name, shape):
            return st.enter_context(nc.sbuf_tensor(name, list(shape), F32))
        k.dma("sp", sc, lambda e: e.dma_start(out=self.pmask[:], in_=self.c_pmask[:, :]), writes=[tc])
        k.dma("sp", sc, lambda e: e.dma_start(out=self.bdmask[:], in_=self.c_bdmask[:, :]), writes=[tc])
        k.op("dve", lambda e: e.memset(self.halfpi[:], float(np.pi / 2)), writes=[tc])
        k.op("dve", lambda e: e.memset(self.Hprev[:], 0.0), writes=[tc])
        tg = T("s5stage")
        sg_ = k.dsem("s5stage")
        k.dma("sp", sg_, lambda e: e.dma_start(out=self.gstage[0:8, :], in_=self.s5_d.rearrange("l (c p) -> (l c) p", p=P)),
              reads=[tc], writes=[tc, tg])
        k.dma("sp", sg_, lambda e: e.dma_start(out=self.gstage[8:24, :], in_=self.glu_b.rearrange("l (c p) -> (l c) p", p=P)),
              writes=[tg])
        ps, tp = self.psum()
        k.op("pe", lambda e: e.transpose(ps[:, 0:24], self.gstage[0:24, :], self.ident[0:24, 0:24]), reads=[tc, tg], writes=[tp])
        k.op("dve", lambda e: e.tensor_copy(out=self.scol[:], in_=ps[:, 0:24]), reads=[tp], writes=[tc])
        self.arena.reset()
        W = [self.arena.f32([32]) for i in range(12)]
        lam_st = self.arena.f32([256])
        self.s5raw = self.arena.f32([DEPTH * 8 * 2, 32])
        self.s5unit = self.arena.f32([DEPTH * 7 * 2, 32])
        self.s5r8 = self.arena.f32([DEPTH, 32])

        def dv(fn):
            k.op("dve", fn, reads=[tc], writes=[tc])

        def av(fn):
            k.op("act", fn, reads=[tc], writes=[tc])

        def cmul(orr, oi, ar_, ai, br, bi, t0, t1):
            dv(lambda e: e.tensor_tensor(out=t0[:], in0=ar_, in1=br, op=ALU.mult))
            dv(lambda e: e.tensor_tensor(out=t1[:], in0=ai, in1=bi, op=ALU.mult))
            dv(lambda e: e.tensor_tensor(out=orr, in0=t0[:], in1=t1[:], op=ALU.subtract))
            dv(lambda e: e.tensor_tensor(out=t0[:], in0=ar_, in1=bi, op=ALU.mult))
            dv(lambda e: e.tensor_tensor(out=t1[:], in0=ai, in1=br, op=ALU.mult))
            dv(lambda e: e.tensor_tensor(out=oi, in0=t0[:], in1=t1[:], op=ALU.add))
        for l in range(DEPTH):
            def layer(l):
                LR, LI, DT, ER, CS, SN, T0, T1, T2, T3, PRn, PIn = W
                tl = T("lamst")
                sl_ = k.dsem(f"lamst{l}")
                for i, src in enumerate((self.lam_re, self.lam_re, self.lam_im, self.lam_im)):
                    k.dma("sp", sl_, lambda e, i=i, src=src: e.dma_start(out=lam_st[0:32, i * 64:(i + 1) * 64], in_=src[l, :, :]),
                          reads=[tc] if i == 0 else [], writes=[tc, tl] if i == 0 else [tl])
                k.dma("sp", sl_, lambda e: e.dma_start(out=DT[:], in_=self.log_dt[l:l + 1, :].partition_broadcast(P)), writes=[tl])
                ps, tp = self.psum()
                k.op("pe", lambda e: e.transpose(ps[:, 0:32], lam_st[0:32, 0:128], self.ident[0:32, 0:32]), reads=[tc, tl], writes=[tp], inc=False)
                k.op("pe", lambda e: e.transpose(ps[:, 32:64], lam_st[0:32, 128:256], self.ident[0:32, 0:32]), reads=[tc, tl], writes=[tp])
                k.op("dve", lambda e: e.tensor_copy(out=LR[:], in_=ps[:, 0:32]), reads=[tp], writes=[tc])
                k.op("dve", lambda e: e.tensor_copy(out=LI[:], in_=ps[:, 32:64]), reads=[tp], writes=[tc])
                k.op("act", lambda e: e.activation(out=DT[:], in_=DT[:], func=AF.Exp), reads=[tc, tl], writes=[tc])
                dv(lambda e: e.tensor_tensor(out=T0[:], in0=LR[:], in1=DT[:], op=ALU.mult))
                av(lambda e: e.activation(out=ER[:], in_=T0[:], func=AF.Exp))
                dv(lambda e: e.tensor_tensor(out=T0[:], in0=LI[:], in1=DT[:], op=ALU.mult))
                av(lambda e: e.activation(out=SN[:], in_=T0[:], func=AF.Sin, scale=1.0 / 16))
                av(lambda e: e.activation(out=CS[:], in_=T0[:], func=AF.Sin, scale=1.0 / 16, bias=self.halfpi[:, 0:1]))
                for _ in range(4):
                    dv(lambda e: e.tensor_tensor(out=T0[:], in0=CS[:], in1=CS[:], op=ALU.mult))
                    dv(lambda e: e.tensor_tensor(out=T1[:], in0=SN[:], in1=SN[:], op=ALU.mult))
                    dv(lambda e: e.scalar_tensor_tensor(out=SN[:], in0=CS[:], scalar=2.0, in1=SN[:], op0=ALU.mult, op1=ALU.mult))
                    dv(lambda e: e.tensor_tensor(out=CS[:], in0=T0[:], in1=T1[:], op=ALU.subtract))
                pr = {}
                pi = {}

                def store(m, prt, pit):
                    dv(lambda e: e.tensor_copy(out=self.tab(l, m, 0), in_=prt))
                    dv(lambda e: e.tensor_scalar(out=self.tab(l, m, 1), in0=pit, scalar1=self.pmask[:, 2:3], scalar2=None, op0=ALU.mult))
                def raw(m, c):
                    return self.s5raw[:, (l * 8 + self.PW.index(m)) * 2 + c, :]
                dv(lambda e: e.tensor_tensor(out=raw(1, 0), in0=ER[:], in1=CS[:], op=ALU.mult))
                dv(lambda e: e.tensor_tensor(out=raw(1, 1), in0=ER[:], in1=SN[:], op=ALU.mult))
                for m in (2, 3, 4, 5, 6, 7, 8):
                    cmul(raw(m, 0), raw(m, 1), raw(m - 1, 0), raw(m - 1, 1), raw(1, 0), raw(1, 1), T0, T1)
                for m in self.PW:
                    store(m, raw(m, 0), raw(m, 1))
                def unit(kk, c):
                    return self.s5unit[:, (l * 7 + kk) * 2 + c, :]
                dv(lambda e: e.tensor_tensor(out=T0[:], in0=LR[:], in1=DT[:], op=ALU.mult))
                av(lambda e: e.activation(out=self.s5r8[:, l, :], in_=T0[:], func=AF.Exp, scale=8.0))
                dv(lambda e: e.reciprocal(out=T1[:], in_=self.s5r8[:, l, :]))
                dv(lambda e: e.tensor_tensor(out=unit(0, 0), in0=raw(8, 0), in1=T1[:], op=ALU.mult))
                dv(lambda e: e.tensor_tensor(out=unit(0, 1), in0=raw(8, 1), in1=T1[:], op=ALU.mult))
                for kk in range(1, 7):
                    dv(lambda e, kk=kk: e.tensor_tensor(out=T0[:], in0=unit(kk - 1, 0), in1=unit(kk - 1, 0), op=ALU.mult))
                    dv(lambda e, kk=kk: e.tensor_tensor(out=T1[:], in0=unit(kk - 1, 1), in1=unit(kk - 1, 1), op=ALU.mult))
                    dv(lambda e, kk=kk: e.tensor_tensor(out=unit(kk, 0), in0=T0[:], in1=T1[:], op=ALU.subtract))
                    dv(lambda e, kk=kk: e.scalar_tensor_tensor(out=unit(kk, 1), in0=unit(kk - 1, 0), scalar=2.0, in1=unit(kk - 1, 1),
                                                               op0=ALU.mult, op1=ALU.mult))
                dv(lambda e: e.tensor_scalar(out=T2[:], in0=raw(1, 0), scalar1=-1.0, scalar2=None, op0=ALU.add))
                dv(lambda e: e.tensor_tensor(out=T0[:], in0=LR[:], in1=LR[:], op=ALU.mult))
                dv(lambda e: e.tensor_tensor(out=T1[:], in0=LI[:], in1=LI[:], op=ALU.mult))
                dv(lambda e: e.tensor_tensor(out=T3[:], in0=T0[:], in1=T1[:], op=ALU.add))
                dv(lambda e: e.reciprocal(out=T3[:], in_=T3[:]))
                dv(lambda e: e.tensor_tensor(out=T0[:], in0=T2[:], in1=LR[:], op=ALU.mult))
                dv(lambda e: e.tensor_tensor(out=T1[:], in0=raw(1, 1), in1=LI[:], op=ALU.mult))
                dv(lambda e: e.tensor_tensor(out=T0[:], in0=T0[:], in1=T1[:], op=ALU.add))
                dv(lambda e: e.tensor_tensor(out=self.s5coef[:, l, 0, :], in0=T0[:], in1=T3[:], op=ALU.mult))
                dv(lambda e: e.tensor_tensor(out=T0[:], in0=raw(1, 1), in1=LR[:], op=ALU.mult))
                dv(lambda e: e.tensor_tensor(out=T1[:], in0=T2[:], in1=LI[:], op=ALU.mult))
                dv(lambda e: e.tensor_tensor(out=T0[:], in0=T0[:], in1=T1[:], op=ALU.subtract))
                dv(lambda e: e.tensor_tensor(out=T0[:], in0=T0[:], in1=T3[:], op=ALU.mult))
                dv(lambda e: e.tensor_scalar(out=self.s5coef[:, l, 1, :], in0=T0[:], scalar1=self.pmask[:, 2:3], scalar2=None, op0=ALU.mult))
            layer(l)
        self.s5_pre = []
        for l in range(DEPTH):
            def preload(l):
                f32 = self.arena.f32
                bS = f32([32, 16]); bX = f32([32, 16]); Cn1 = f32([4, 128]); Cn2 = f32([4, 128])
                t_b = T(f"bSX{l}"); t_Cn = T(f"Cn{l}")
                sb_, sc_ = k.dsem(f"s5b{l}"), k.dsem(f"s5c{l}")
                bre = self.b_re[l].rearrange("g p h -> p g h")
                bim = self.b_im[l].rearrange("g p h -> p g h")
                for (dst, top, bot) in ((bS, bre, bim), (bX, bim, bre)):
                    k.dma("sp", sb_, lambda e, dst=dst, top=top: e.dma_start(out=dst[0:64, :, :], in_=top), writes=[t_b])
                    k.dma("sp", sb_, lambda e, dst=dst, bot=bot: e.dma_start(out=dst[64:128, :, :], in_=bot), writes=[t_b])
                cre = self.c_re[l].rearrange("(c g) h p -> (g h) c p", c=4)
                cim = self.c_im[l].rearrange("(c g) h p -> (g h) c p", c=4)
                for (dst, left, right) in ((Cn1, cre, cim), (Cn2, cim, cre)):
                    k.dma("sp", sc_, lambda e, dst=dst, left=left: e.dma_start(out=dst[:, :, 0:64], in_=left), writes=[t_Cn])
                    k.dma("sp", sc_, lambda e, dst=dst, right=right: e.dma_start(out=dst[:, :, 64:128], in_=right), writes=[t_Cn])
                self.s5_pre.append((bS, bX, Cn1, Cn2, t_b, t_Cn))
            preload(l)
        self.s5_setup_mark = self.arena.off
        for l in range(DEPTH):
            def consts_layer(l):
                k.barrier()
                self.arena.off = self.s5_setup_mark
                CT4 = self.arena.bf16([4, 5120]); t_CT = T("CTs")
                self.s5_consts(l, CT4, t_CT, self.s5_pre[l])
                k.dma("sp", self.s_s5in[2], lambda e: e.dma_start(out=self.s5scr[l].rearrange("c p f -> p c f"), in_=CT4), reads=[t_CT],
                      writes=self.t_scr[l])
            consts_layer(l)
        for l in range(DEPTH):
            def rot_layer(l):
                k.barrier()
                self.arena.off = self.s5_setup_mark
                TAB = self.arena.f32([3, 32 * 128])
                tA = self.arena.f32([32, 64])
                t_T = T("rotT")
                Tc = TAB[:, 0, :].rearrange("p (g n) -> p g n", g=32)
                Ts = TAB[:, 1, :].rearrange("p (g n) -> p g n", g=32)
                Rr = TAB[:, 2, :].rearrange("p (g n) -> p g n", g=32)

                def dv(fn):
                    k.op("dve", fn, reads=[tc, t_T], writes=[t_T])
                dv(lambda e: e.memset(Tc[:, :, 0:1], 1.0))
                dv(lambda e: e.memset(Ts[:, :, 0:1], 0.0))
                for kk in range(7):
                    def lvl(kk):
                        L = 1 << kk
                        ck = self.s5unit[:, (l * 7 + kk) * 2 + 0, :].unsqueeze(2).to_broadcast([P, 32, L])
                        sk = self.s5unit[:, (l * 7 + kk) * 2 + 1, :].unsqueeze(2).to_broadcast([P, 32, L])
                        dv(lambda e: e.tensor_tensor(out=Tc[:, :, L:2 * L], in0=Tc[:, :, 0:L], in1=ck, op=ALU.mult))
                        dv(lambda e: e.tensor_tensor(out=tA[:, :, 0:L], in0=Ts[:, :, 0:L], in1=sk, op=ALU.mult))
                        dv(lambda e: e.tensor_tensor(out=Tc[:, :, L:2 * L], in0=Tc[:, :, L:2 * L], in1=tA[:, :, 0:L], op=ALU.subtract))
                        dv(lambda e: e.tensor_tensor(out=Ts[:, :, L:2 * L], in0=Ts[:, :, 0:L], in1=ck, op=ALU.mult))
                        dv(lambda e: e.tensor_tensor(out=tA[:, :, 0:L], in0=Tc[:, :, 0:L], in1=sk, op=ALU.mult))
                        dv(lambda e: e.tensor_tensor(out=Ts[:, :, L:2 * L], in0=Ts[:, :, L:2 * L], in1=tA[:, :, 0:L], op=ALU.add))
                    lvl(kk)
                dv(lambda e: e.tensor_scalar(out=TAB[:, 1, :], in0=TAB[:, 1, :], scalar1=self.pmask[:, 3:4], scalar2=None, op0=ALU.mult))
                dv(lambda e: e.tensor_copy(out=Rr, in_=self.s5r8[:, l, :].unsqueeze(2).to_broadcast([P, 32, 128])))
                dv(lambda e: e.memset(Rr[:, :, 0:1], 0.0))
                k.dma("sp", self.s_s5in[2], lambda e: e.dma_start(out=self.s5rot[l].rearrange("p a g n -> p a (g n)"), in_=TAB), reads=[t_T],
                      writes=[self.t_rot[l]])
            rot_layer(l)

    def s5(self, l, h, cout, t_c):
        k = self.k
        ar = self.arena
        tc = self.t_const
        NCM = NPH + NS
        NB = 128
        samp = (h == 0)
        uP = ar.bf16([4, 8 * NB])
        us = ar.bf16([4, NS])
        z = ar.bf16([4, NCM])
        t_uP = [T(f"uP{i}") for i in range(4)]
        t_us = T("us")
        t_z = [[T(f"z{c4}_{ci}") for ci in range(3)] for c4 in range(4)]
        svu, tsu = self.wblock(l, 3072)

        def uproj(c4, ci):
            ps, tp = self.psum()
            self.proj_fm(svu, tsu, c4, ci, ci * 512, 512, ps, tp)
            k.op("act", lambda e: e.activation(
                out=uP[:, c4, :].rearrange("p (t n) -> p t n", t=8)[:, :, ci * 64:(ci + 1) * 64],
                in_=ps[:, 0:512].rearrange("p (n t) -> p t n", t=8), func=AF.Copy), reads=[tp], writes=[t_uP[c4]])
        for c4 in range(4):
            for ci in range(2):
                uproj(c4, ci)
        if samp:
            psU, tpsU = self.psum()
            for c4 in range(4):
                self.proj_fm(svu, tsu, c4, 2, NPH, NS, psU[:, c4 * NS:(c4 + 1) * NS], tpsU)
            k.op("act", lambda e: e.activation(out=us[:, :, :], in_=psU[:, 0:4 * NS].rearrange("p (a b) -> p a b", a=4), func=AF.Copy),
                 reads=[tpsU], writes=[t_us])
        hp_s = self.Hprev[:, l, 0, :]
        hp_x = self.Hprev[:, l, 1, :]
        hcs = ar.f32([32]); hcx = ar.f32([32]); htmp = ar.f32([32])
        t_hc = T("hc")
        if h == 1:
            k.op("dve", lambda e: e.tensor_tensor(out=hcs, in0=hp_s, in1=self.tab(l, 8, 0), op=ALU.mult), reads=[tc], writes=[t_hc])
            k.op("dve", lambda e: e.tensor_tensor(out=htmp, in0=hp_x, in1=self.tab(l, 8, 1), op=ALU.mult), reads=[tc], writes=[t_hc])
            k.op("dve", lambda e: e.tensor_tensor(out=hcs, in0=hcs, in1=htmp, op=ALU.add), reads=[t_hc], writes=[t_hc])
        H0s_all = H0x_all = Hns_all = None
        t_H0 = T("H0")
        if samp:
            H0s_all = ar.f32([NS, 32]); H0x_all = ar.f32([NS, 32]); Hns_all = ar.f32([NS, 32])
            Hst = [ar.f32([4, 128]) for _ in range(2)]
            t_Hst = T("Hst")
            for (dst, left, right) in ((Hst[0], self.sre, self.sim), (Hst[1], self.sim, self.sre)):
                k.dma("sp", self.s_s5in[2], lambda e, dst=dst, left=left: e.dma_start(
                    out=dst[:, :, 0:64], in_=left[l].rearrange("(a j4) g p -> (j4 g) a p", j4=4)), writes=[t_Hst])
                k.dma("sp", self.s_s5in[2], lambda e, dst=dst, right=right: e.dma_start(
                    out=dst[:, :, 64:128], in_=right[l].rearrange("(a j4) g p -> (j4 g) a p", j4=4)), writes=[t_Hst])
            for si, dsta in ((0, H0s_all), (1, H0x_all)):
                ps, tp = self.psum()
                for a in range(4):
                    k.op("pe", lambda e, a=a, si=si, ps=ps: e.transpose(ps[:, a * 128:(a + 1) * 128], Hst[si][:, a, :], self.ident[:]),
                         reads=[t_Hst, tc], writes=[tp], inc=(a == 3))
                k.op("dve", lambda e, ps=ps, dsta=dsta: e.tensor_copy(out=dsta.rearrange("p j g -> p (j g)"), in_=ps[:, :]),
                     reads=[tp], writes=[t_H0])
        mark = ar.off
        self._pt = {}
        k.barrier()
        args = (uP, us, z, t_uP, t_us, t_z, hcs, hcx, t_hc, H0s_all, H0x_all, Hns_all, t_H0)
        ar.off = mark
        self.s5_pass("A", l, h, 0, *args)
        for c4 in range(4):
            if c4 + 1 < 4:
                ar.off = mark
                self.s5_pass("A", l, h, c4 + 1, *args)
            ar.off = mark
            self.s5_pass("B", l, h, c4, *args)
        k.barrier()
        ar.off = mark
        if samp:
            psN, tpsN = self.psum()
            for a in range(4):
                k.op("pe", lambda e, a=a: e.transpose(psN[:, a * 128:(a + 1) * 128],
                                                      Hns_all.rearrange("p j g -> p (j g)")[:, a * 128:(a + 1) * 128], self.ident[:]),
                     reads=[t_H0, tc], writes=[tpsN], inc=(a == 3))
            k.op("dve", lambda e: e.tensor_copy(out=Hst[0].rearrange("p a c -> p (a c)"), in_=psN[:, :]), reads=[tpsN, t_Hst], writes=[t_Hst])
            so = self.s_s5s
            k.dma("sp", so, lambda e: e.dma_start(out=self.res_out[l].rearrange("(a j4) g p -> (j4 g) a p", j4=4), in_=Hst[0][:, :, 0:64]),
                  reads=[t_Hst])
            k.dma("sp", so, lambda e: e.dma_start(out=self.ims_out[l].rearrange("(a j4) g p -> (j4 g) a p", j4=4), in_=Hst[0][:, :, 64:128]),
                  reads=[t_Hst])
        k.op("dve", lambda e: e.tensor_copy(out=self.Hprev[:, l, :, :], in_=self.Hnew[:, :, :]), reads=[tc, self.t_Hnew], writes=[tc])
        if h == 1:
            psF, tpsF = self.psum()
            k.op("pe", lambda e: e.transpose(psF[0:32, 0:128], self.Hnew[:, 0, :], self.ident[:]), reads=[self.t_Hnew, tc], writes=[tpsF])
            hst = ar.f32([128]); t_hst = T("hst")
            k.op("dve", lambda e: e.tensor_copy(out=hst[0:32, :], in_=psF[0:32, 0:128]), reads=[tpsF], writes=[t_hst])
            k.dma("sp", self.s_s5p, lambda e: e.dma_start(out=self.rep_out[l, :, :], in_=hst[0:32, 0:64]), reads=[t_hst])
            k.dma("sp", self.s_s5p, lambda e: e.dma_start(out=self.imp_out[l, :, :], in_=hst[0:32, 64:128]), reads=[t_hst])
        gw = self.glu_w[l].rearrange("(kc kp) f -> kp kc f", kp=P)
        slA, tsA, ssA = self.wslot()
        svA = slA[:, 0:4 * 512].rearrange("p (kc f) -> p kc f", kc=4)
        k.dma("pool", ssA, lambda e: e.dma_start(out=svA, in_=gw[:, :, 0:512]), writes=[tsA])
        slB, tsB, ssB = self.wslot()
        svB = slB[:, 0:4 * 512].rearrange("p (kc f) -> p kc f", kc=4)
        k.dma("pool", ssB, lambda e: e.dma_start(out=svB, in_=gw[:, :, 512:1024]), writes=[tsB])

        def glu(oc, ci, c0, n):
            pA, tpA = self.psum()
            pB, tpB = self.psum()
            for kc in range(4):
                k.op("pe", lambda e, kc=kc: e.matmul(pA[:, 0:n], lhsT=svA[:, kc, oc * 128:(oc + 1) * 128], rhs=z[:, kc, c0:c0 + n],
                                                     start=(kc == 0), stop=(kc == 3)), reads=[tsA, t_z[kc][ci]], writes=[tpA], inc=(kc == 3))
            for kc in range(4):
                k.op("pe", lambda e, kc=kc: e.matmul(pB[:, 0:n], lhsT=svB[:, kc, oc * 128:(oc + 1) * 128], rhs=z[:, kc, c0:c0 + n],
                                                     start=(kc == 0), stop=(kc == 3)), reads=[tsB, t_z[kc][ci]], writes=[tpB], inc=(kc == 3))
            sg, tsg = self.nxt("sg")
            k.op("act", lambda e: e.activation(out=sg[:, 0:n], in_=pB[:, 0:n], func=AF.Sigmoid,
                                               bias=self.scol[:, 8 + l * 8 + 4 + oc:8 + l * 8 + 5 + oc]),
                 reads=[tpB, tc], writes=[tsg])
            k.op("dve", lambda e: e.scalar_tensor_tensor(out=cout[:, oc, c0:c0 + n], in0=pA[:, 0:n],
                                                         scalar=self.scol[:, 8 + l * 8 + oc:8 + l * 8 + 1 + oc],
                                                         in1=sg[:, 0:n], op0=ALU.add, op1=ALU.mult),
                 reads=[tpA, tsg, tc], writes=[t_c[ci]])
        for oc in range(4):
            for (ci, c0, n) in self.cts(h):
                glu(oc, ci, c0, n)

    def pt(self, name):
        t = self._pt.get(name)
        if t is None:
            t = self._pt[name] = T(name)
        return t

    def ct_views(self, CT):
        LSe = CT[:, 0:1024].rearrange("p (a b) -> p a b", a=8)
        LSo = CT[:, 1024:2048].rearrange("p (a b) -> p a b", a=8)
        CAb = CT[:, 2048:4096].rearrange("p (a b) -> p a b", a=8)
        KBb = CT[:, 4096:5120].rearrange("p (a b) -> p a b", a=8)
        return LSe, LSo, CAb, KBb

    def s5_consts(self, l, CT4, t_CT, pre):
        k = self.k
        ar = self.arena
        tc = self.t_const
        f32 = ar.f32
        bS, bX, Cn1, Cn2, t_b, t_Cn = pre
        Bs = f32([32, 16]); Bx = f32([32, 16]); S0 = f32([32, 16]); X0 = f32([32, 16])
        Xs = [f32([32, 16]) for _ in range(2)]; t_Xs = [T("Xs0"), T("Xs1")]
        t1 = f32([32, 16]); t2 = f32([32, 16])
        t_B = T("BsBx"); t_SX = T("S0X0"); t_t12 = T("t12")

        def bc(ap32):
            return ap32.unsqueeze(2).to_broadcast([P, 32, 16])
        C1 = bc(self.s5coef[:, l, 0, :])
        C2 = bc(self.s5coef[:, l, 1, :])

        def dv(fn, reads, writes):
            k.op("dve", fn, reads=reads, writes=writes)

        def flat(a):
            return a.rearrange("p g h -> p (g h)")
        dv(lambda e: e.tensor_tensor(out=Bs[:, :, :], in0=bS[:, :, :], in1=C1, op=ALU.mult), [t_b, tc], [t_B])
        dv(lambda e: e.tensor_tensor(out=t1[:, :, :], in0=bX[:, :, :], in1=C2, op=ALU.mult), [t_b, tc], [t_t12])
        dv(lambda e: e.tensor_tensor(out=Bs[:, :, :], in0=Bs[:, :, :], in1=t1[:, :, :], op=ALU.add), [t_B, t_t12], [t_B])
        dv(lambda e: e.tensor_tensor(out=Bx[:, :, :], in0=bX[:, :, :], in1=C1, op=ALU.mult), [t_b, tc], [t_B])
        dv(lambda e: e.tensor_tensor(out=t1[:, :, :], in0=bS[:, :, :], in1=C2, op=ALU.mult), [t_b, tc], [t_t12])
        dv(lambda e: e.tensor_tensor(out=Bx[:, :, :], in0=Bx[:, :, :], in1=t1[:, :, :], op=ALU.subtract), [t_B, t_t12], [t_B])
        for (src, dst, col) in ((Cn1, S0, 3), (Cn2, X0, 2)):
            def cstack(src, dst, col):
                psC, tpsC = self.psum()
                for c4 in range(4):
                    k.op("pe", lambda e, c4=c4: e.transpose(psC[:, c4 * 128:(c4 + 1) * 128], src[:, c4, :], self.ident[:]),
                         reads=[t_Cn, tc], writes=[tpsC], inc=(c4 == 3))
                dv(lambda e: e.tensor_scalar(out=flat(dst), in0=psC[:, :], scalar1=self.pmask[:, col:col + 1], scalar2=None, op0=ALU.mult),
                   [tpsC, tc], [t_SX])
            cstack(src, dst, col)
        for m in range(8):
            def power(m):
                xs, txs = Xs[m % 2], t_Xs[m % 2]
                if m == 0:
                    dv(lambda e: e.tensor_copy(out=xs[:, :, :], in_=Bs[:, :, :]), [t_B], [txs])
                else:
                    dv(lambda e: e.tensor_tensor(out=xs[:, :, :], in0=Bs[:, :, :], in1=bc(self.tab(l, m, 0)), op=ALU.mult), [t_B, tc], [txs])
                    dv(lambda e: e.tensor_tensor(out=t1[:, :, :], in0=Bx[:, :, :], in1=bc(self.tab(l, m, 1)), op=ALU.mult), [t_B, tc], [t_t12])
                    dv(lambda e: e.tensor_tensor(out=xs[:, :, :], in0=xs[:, :, :], in1=t1[:, :, :], op=ALU.add), [txs, t_t12], [txs])
                xs2 = flat(xs)
                s02 = flat(S0)
                psT, tpT = self.psum()
                psK, tpK = self.psum()
                for c4 in range(4):
                    k.op("pe", lambda e, c4=c4: e.transpose(psT[:, c4 * 128:(c4 + 1) * 128], xs2[:, c4 * 128:(c4 + 1) * 128], self.ident[:]),
                         reads=[txs, tc], writes=[tpT], inc=(c4 == 3))
                for c4 in range(4):
                    k.op("pe", lambda e, c4=c4: e.matmul(psK[:, c4 * 128:(c4 + 1) * 128], lhsT=xs2[:, c4 * 128:(c4 + 1) * 128],
                                                         rhs=s02[:, c4 * 128:(c4 + 1) * 128], start=True, stop=True),
                         reads=[txs, t_SX], writes=[tpK], inc=(c4 == 3))
                s8 = 7 - m
                psT3 = psT[:, :].rearrange("p (c f) -> p c f", c=4)
                psK3 = psK[:, :].rearrange("p (c f) -> p c f", c=4)
                dv(lambda e: e.tensor_scalar(out=CT4[:, :, s8 * 128:(s8 + 1) * 128], in0=psT3, scalar1=self.pmask[:, 0:1], scalar2=None,
                                             op0=ALU.mult), [tpT, tc], [t_CT])
                dv(lambda e: e.tensor_scalar(out=CT4[:, :, 1024 + s8 * 128:1024 + (s8 + 1) * 128], in0=psT3, scalar1=self.pmask[:, 1:2],
                                             scalar2=None, op0=ALU.mult), [tpT, tc], [t_CT])
                bd3 = self.bdmask[:, :].unsqueeze(1).to_broadcast([P, 4, 128])
                if m == 0:
                    t2v = flat(t2).rearrange("p (c f) -> p c f", c=4)
                    dv(lambda e: e.tensor_tensor(out=t2v, in0=psK3, in1=bd3, op=ALU.mult), [tpK, tc], [t_t12])
                    for c4 in range(4):
                        dv(lambda e, c4=c4: e.scalar_tensor_tensor(out=CT4[:, c4, 4096:4096 + 128], in0=self.ident[:, :],
                                                                   scalar=self.scol[:, l * 4 + c4:l * 4 + c4 + 1], in1=t2v[:, c4, :],
                                                                   op0=ALU.mult, op1=ALU.add), [t_t12, tc], [t_CT])
                else:
                    dv(lambda e: e.tensor_tensor(out=CT4[:, :, 4096 + m * 128:4096 + (m + 1) * 128], in0=psK3, in1=bd3, op=ALU.mult),
                       [tpK, tc], [t_CT])
            power(m)
        k.op("dve", lambda e: e.memset(CT4[:, :, 2048:4096], 0.0), writes=[t_CT])
        for m in range(1, 9):
            def capow(m):
                r = m - 1
                dv(lambda e: e.tensor_tensor(out=t1[:, :, :], in0=S0[:, :, :], in1=bc(self.tab(l, m, 0)), op=ALU.mult), [t_SX, tc], [t_t12])
                dv(lambda e: e.tensor_tensor(out=t2[:, :, :], in0=X0[:, :, :], in1=bc(self.tab(l, m, 1)), op=ALU.mult), [t_SX, tc], [t_t12])
                cav = CT4[:, :, 2048 + r * 256:2048 + (r + 1) * 256].rearrange("p c (gp two f) -> p c gp two f", two=2, f=32)
                t1v = t1.rearrange("p (c gp two) h -> p c gp two h", c=4, two=2)
                t2v_ = t2.rearrange("p (c gp two) h -> p c gp two h", c=4, two=2)
                dv(lambda e: e.tensor_tensor(out=cav[:, :, :, 0, 0:16], in0=t1v[:, :, :, 0, :], in1=t2v_[:, :, :, 0, :], op=ALU.subtract),
                   [t_t12], [t_CT])
                dv(lambda e: e.tensor_tensor(out=cav[:, :, :, 1, 16:32], in0=t1v[:, :, :, 1, :], in1=t2v_[:, :, :, 1, :], op=ALU.subtract),
                   [t_t12], [t_CT])
            capow(m)

    def s5_pass(self, part, l, h, c4, uP, us, z, t_uP, t_us, t_z, hcs, hcx, t_hc, H0s_all, H0x_all, Hns_all, t_H0):
        k = self.k
        ar = self.arena
        tc = self.t_const
        NB = 128
        samp = (h == 0)
        g0 = c4 * 8
        f32 = ar.f32
        par = c4 % 2
        LSb = ar.bf16([2048]); CKb = ar.bf16([3072])
        LSe = LSb[:, 0:1024].rearrange("p (a b) -> p a b", a=8)
        LSo = LSb[:, 1024:2048].rearrange("p (a b) -> p a b", a=8)
        CAb = CKb[:, 0:2048].rearrange("p (a b) -> p a b", a=8)
        KBb = CKb[:, 2048:3072].rearrange("p (a b) -> p a b", a=8)
        t_LS = self.pt("LSb"); t_CA = t_KB = self.pt("CKb")
        Hb2 = [ar.bf16([8, NB + 2]) for _ in range(2)]
        Hb = Hb2[par]; t_Hb = [self.pt(f"Hb{par}_{i}") for i in range(8)]
        HS = ar.f32([4, 128]); HX = ar.f32([4, 128]); WS = ar.f32([4, 128]); WX = ar.f32([4, 128]); TMP = ar.f32([4, 128])
        TBb = ar.f32([3, 512])
        t_HS, t_HX, t_HX2, t_WS, t_WX, t_TMP, t_TB = (self.pt(n_) for n_ in ("HS", "HX", "HX2", "WS", "WX", "TMPr", "TBb"))
        hl = ar.f32([4, 4]); t_hl = self.pt("hl")
        if samp:
            H0s = H0s_all[:, :, g0:g0 + 8]; H0x = H0x_all[:, :, g0:g0 + 8]; Hns = Hns_all[:, :, g0:g0 + 8]
            H0b2 = [ar.bf16([8, NS]) for _ in range(2)]
            H0b = H0b2[par]; t_H0b = self.pt(f"H0b{par}")
            tmpx = f32([NS, 8]); t_tmpx = self.pt("tmpx")
            hls = f32([8, NS]); t_hls = self.pt("hls")

        def dv(fn, reads, writes):
            k.op("dve", fn, reads=reads, writes=writes)
        if part == "A":
            k.dma("sp", self.s_s5in[0], lambda e: e.dma_start(out=LSb, in_=self.s5scr[l, c4][:, 0:2048]), reads=[self.t_scr[l][c4]],
                  writes=[t_LS])
            if samp:
                k.op("act", lambda e: e.activation(out=H0b[:, :, :], in_=H0s.rearrange("p j g -> p g j"), func=AF.Copy),
                     reads=[t_H0], writes=[t_H0b])
        else:
            k.dma("pool", self.s_s5ck, lambda e: e.dma_start(out=CKb, in_=self.s5scr[l, c4][:, 2048:5120]), reads=[self.t_scr[l][c4]],
                  writes=[t_CA])

        def lsw(g8, s8):
            q = g8 // 2
            src = LSe if g8 % 2 == 0 else LSo
            return src[32 * q:32 * q + 32, s8, :]
        if part == "A":
            for gb in range(2):
                def gbatch(gb):
                    pss_ = [self.psum(), self.psum()]
                    for gl in range(4):
                        g8 = gb * 4 + gl
                        q = g8 // 2
                        ps, tp = pss_[gl // 2]
                        for s8 in range(8):
                            k.op("pe", lambda e, gl=gl, g8=g8, q=q, s8=s8, ps=ps: e.matmul(
                                ps[:, (gl % 2) * 128:(gl % 2 + 1) * 128], lhsT=lsw(g8, s8), rhs=uP[32 * q:32 * q + 32, c4, s8 * NB:(s8 + 1) * NB],
                                start=(s8 == 0), stop=(s8 == 7), tile_position=(32 * q, 0)),
                                reads=[t_LS, t_uP[c4]], writes=[tp], inc=(s8 == 7))
                    gq = g0 + gb * 4
                    k.dma("sp", self.s_s5in[1], lambda e: e.dma_start(out=TBb.rearrange("p a (g n) -> p a g n", g=4),
                                                                      in_=self.s5rot[l][:, :, gq:gq + 4, :]),
                          reads=[self.t_rot[l]], writes=[t_TB])
                    for half_ in range(2):
                        ps, tp = pss_[half_]
                        k.op("dve", lambda e, ps=ps, half_=half_: e.tensor_copy(out=HS[:, half_ * 2:half_ * 2 + 2, :],
                                                                                  in_=ps[:, 0:256].rearrange("p (g n) -> p g n", g=2)),
                             reads=[tp], writes=[t_HS])
                    if h == 1:
                        k.op("dve", lambda e: e.tensor_tensor(out=HS[:, :, 0:1], in0=HS[:, :, 0:1],
                                                              in1=hcs[:, gq:gq + 4].unsqueeze(2), op=ALU.add),
                             reads=[t_HS, t_hc], writes=[t_HS])
                    k.dma("sp", self.s_s5x[0], lambda e: e.dma_start(out=HX[0:64, :, :], in_=HS[64:128, :, :]), reads=[t_HS], writes=[t_HX])
                    k.dma("sp", self.s_s5x[1], lambda e: e.dma_start(out=HX[64:128, :, :], in_=HS[0:64, :, :]), reads=[t_HS], writes=[t_HX2])
                    fl = lambda a: a.rearrange("p g n -> p (g n)")
                    Tc, TsS, Rr = TBb[:, 0, :], TBb[:, 1, :], TBb[:, 2, :]
                    tHX = [t_HX, t_HX2]

                    def dvb(fn, reads, writes):
                        k.op("dve", fn, reads=reads, writes=writes)
                    dvb(lambda e: e.tensor_tensor(out=fl(WS), in0=fl(HS), in1=Tc, op=ALU.mult), [t_HS, t_TB], [t_WS])
                    dvb(lambda e: e.tensor_tensor(out=fl(TMP), in0=fl(HX), in1=TsS, op=ALU.mult), tHX + [t_TB], [t_TMP])
                    dvb(lambda e: e.tensor_tensor(out=fl(WS), in0=fl(WS), in1=fl(TMP), op=ALU.add), [t_WS, t_TMP], [t_WS])
                    dvb(lambda e: e.tensor_tensor(out=fl(WX), in0=fl(HX), in1=Tc, op=ALU.mult), tHX + [t_TB], [t_WX])
                    dvb(lambda e: e.tensor_tensor(out=fl(TMP), in0=fl(HS), in1=TsS, op=ALU.mult), [t_HS, t_TB], [t_TMP])
                    dvb(lambda e: e.tensor_tensor(out=fl(WX), in0=fl(WX), in1=fl(TMP), op=ALU.subtract), [t_WX, t_TMP], [t_WX])
                    dvb(lambda e: e.tensor_tensor_scan(out=fl(HS), data0=Rr, data1=fl(WS), initial=0.0, op0=ALU.mult, op1=ALU.add),
                        [t_WS, t_TB, t_HS], [t_HS])
                    dvb(lambda e: e.tensor_tensor_scan(out=fl(HX), data0=Rr, data1=fl(WX), initial=0.0, op0=ALU.mult, op1=ALU.add),
                        [t_WX, t_TB] + tHX, tHX)
                    dvb(lambda e: e.tensor_tensor(out=fl(WS), in0=fl(HS), in1=Tc, op=ALU.mult), [t_HS, t_TB], [t_WS])
                    dvb(lambda e: e.tensor_tensor(out=fl(TMP), in0=fl(HX), in1=TsS, op=ALU.mult), tHX + [t_TB], [t_TMP])
                    g8a = gb * 4
                    dvb(lambda e: e.tensor_tensor(out=Hb[:, g8a:g8a + 4, 1:NB + 1], in0=WS[:, :, :], in1=TMP[:, :, :], op=ALU.subtract),
                        [t_WS, t_TMP], t_Hb[g8a:g8a + 4])
                    dvb(lambda e: e.tensor_tensor(out=self.Hnew[:, 0, gq:gq + 4], in0=WS[:, :, NB - 1], in1=TMP[:, :, NB - 1], op=ALU.subtract),
                        [t_WS, t_TMP], [self.t_Hnew])
                    Tc3 = Tc.rearrange("p (g n) -> p g n", g=4)
                    Ts3 = TsS.rearrange("p (g n) -> p g n", g=4)
                    dvb(lambda e: e.tensor_tensor(out=hl[:, 0, :], in0=HX[:, :, NB - 1], in1=Tc3[:, :, NB - 1], op=ALU.mult), tHX + [t_TB], [t_hl])
                    dvb(lambda e: e.tensor_tensor(out=hl[:, 1, :], in0=HS[:, :, NB - 1], in1=Ts3[:, :, NB - 1], op=ALU.mult), [t_HS, t_TB], [t_hl])
                    dvb(lambda e: e.tensor_tensor(out=self.Hnew[:, 1, gq:gq + 4], in0=hl[:, 0, :], in1=hl[:, 1, :], op=ALU.add),
                        [t_hl], [self.t_Hnew])
                    k.op("act", lambda e: e.activation(out=Hb[:, g8a:g8a + 4, 0:1], in_=self.Hprev[:, l, 0, gq:gq + 4].unsqueeze(2), func=AF.Copy),
                         reads=[tc], writes=t_Hb[g8a:g8a + 4])
                gbatch(gb)
            if samp:
                for q in range(4):
                    def sq_(q):
                        psH, tpsH = self.psum()
                        for e2 in range(2):
                            g8 = 2 * q + e2
                            k.op("pe", lambda e, g8=g8, e2=e2: e.matmul(psH[:, e2 * NS:(e2 + 1) * NS], lhsT=lsw(g8, 7), rhs=us[32 * q:32 * q + 32, c4, :],
                                                                      start=True, stop=True, tile_position=(32 * q, 0)),
                                 reads=[t_LS, t_us], writes=[tpsH], inc=(e2 == 1))
                        dv(lambda e: e.tensor_copy(out=hls[:, 2 * q:2 * q + 2, :], in_=psH[:, 0:2 * NS].rearrange("p (g j) -> p g j", g=2)),
                           [tpsH], [t_hls])
                    sq_(q)
                A1b = self.tab(l, 1, 0)[:, g0:g0 + 8].unsqueeze(1).to_broadcast([P, NS, 8])
                A2b = self.tab(l, 1, 1)[:, g0:g0 + 8].unsqueeze(1).to_broadcast([P, NS, 8])
                dv(lambda e: e.tensor_tensor(out=Hns, in0=H0s, in1=A1b, op=ALU.mult), [t_H0, tc], [t_H0])
                dv(lambda e: e.tensor_tensor(out=tmpx[:, :, :], in0=H0x, in1=A2b, op=ALU.mult), [t_H0, tc], [t_tmpx])
                dv(lambda e: e.tensor_tensor(out=Hns, in0=Hns, in1=tmpx[:, :, :], op=ALU.add), [t_H0, t_tmpx], [t_H0])
                dv(lambda e: e.tensor_tensor(out=Hns, in0=Hns, in1=hls.rearrange("p g j -> p j g"), op=ALU.add),
                   [t_H0, t_hls], [t_H0])
            return
        if self.stop and self.stop.get('s5_stop') == 2:
            return
        t_Hb_all = list(t_Hb)

        def inter(psr, r, rhs_of, last_stop=True):
            for g8 in range(8):
                q = g8 // 2
                if g8 % 2 == 0:
                    o = psr[32 * q:32 * q + 16]
                    w = CAb[:, r, g8 * 32:g8 * 32 + 16]
                else:
                    o = psr[32 * q:32 * q + 32]
                    w = CAb[:, r, g8 * 32:g8 * 32 + 32]
                yield g8, o, w, q
        for bank in range(2):
            def ybank(bank):
                ps, tp = self.psum()
                for rr in range(4):
                    r = bank * 4 + rr
                    psr = ps[:, rr * NB:(rr + 1) * NB]
                    for j in range(r + 1):
                        k.op("pe", lambda e, j=j, r=r, psr=psr: e.matmul(psr, lhsT=KBb[:, j, :], rhs=uP[:, c4, (r - j) * NB:(r - j + 1) * NB],
                                                                         start=(j == 0), stop=False),
                             reads=[t_KB, t_uP[c4]], writes=[tp], inc=False)
                    for g8, o, w, q in inter(psr, r, None):
                        k.op("pe", lambda e, g8=g8, o=o, w=w, q=q: e.matmul(o, lhsT=w, rhs=Hb[:, g8, 0:NB], start=False, stop=(g8 % 2 == 1),
                                                                            tile_position=(0, 32 * q)),
                             reads=[t_CA, t_Hb[g8]], writes=[tp], inc=(g8 == 7))
                k.op("act", lambda e: e.activation(
                    out=z[:, c4, 0:NPH].rearrange("p (n r) -> p r n", r=8)[:, bank * 4:bank * 4 + 4, :],
                    in_=ps[:, :].rearrange("p (r n) -> p r n", r=4), func=AF.Gelu_apprx_tanh),
                    reads=[tp], writes=[t_z[c4][0], t_z[c4][1]])
            ybank(bank)
        if samp:
            ps, tp = self.psum()
            psr = ps[:, 0:NS]
            k.op("pe", lambda e: e.matmul(psr, lhsT=KBb[:, 0, :], rhs=us[:, c4, :], start=True, stop=False),
                 reads=[t_KB, t_us], writes=[tp], inc=False)
            for g8, o, w, q in inter(psr, 0, None):
                k.op("pe", lambda e, g8=g8, o=o, w=w, q=q: e.matmul(o, lhsT=w, rhs=H0b[:, g8, :], start=False, stop=(g8 % 2 == 1),
                                                                    tile_position=(0, 32 * q)),
                     reads=[t_CA, t_H0b], writes=[tp], inc=(g8 == 7))
            k.op("act", lambda e: e.activation(out=z[:, c4, NPH:NPH + NS], in_=ps[:, 0:NS], func=AF.Gelu_apprx_tanh),
                 reads=[tp], writes=[t_z[c4][2]])

    def merge(self, l, h, branches, nxt=None):
        k = self.k
        ar = self.arena
        NCM = NPH + NS
        cts = self.cts(h)
        merged = ar.bf16([DC, NCM])
        t_m = [[T(f"m{dc}_{c}") for c in range(3)] for dc in range(DC)]
        ybuf = ar.f32([DC, NCM])
        t_y = [T(f"my{c}") for c in range(3)]
        acc = [ar.f32([512]) for _ in range(2)]; t_acc = [T("acc0"), T("acc1")]
        sgf = [ar.f32([512]) for _ in range(2)]; t_sgf = [T("sgf0"), T("sgf1")]
        cnt = [0, 0]
        wv = self.w_in[l].rearrange("(kc kp) f -> kp kc f", kp=P)

        def one_dc(dc):
            slA, tsA, ssA = self.wslot()
            gA = slA[:, 0:8 * 3 * 128].rearrange("p (kc n f) -> p kc n f", kc=8, n=3)
            for n in range(3):
                col = 3584 + n * D + dc * 128
                k.dma("pool", ssA, lambda e, n=n, col=col: e.dma_start(out=gA[:, :, n, :], in_=wv[:, :, col:col + 128]),
                      writes=[tsA] if n == 0 else [])
            tsA.w = (ssA, ssA.count)
            slB, tsB, ssB = self.wslot()
            wB = slB[:, 0:4 * 3 * 128].rearrange("p (kc n f) -> p kc n f", kc=4, n=3)
            for n in range(3):
                src = self.w_branch[l, n].rearrange("(kc kp) d -> kp kc d", kp=P)
                k.dma("pool", ssB, lambda e, n=n, src=src: e.dma_start(out=wB[:, :, n, :], in_=src[:, :, dc * 128:(dc + 1) * 128]),
                      writes=[tsB] if n == 0 else [])
            tsB.w = (ssB, ssB.count)

            def one_ct(ci, c0, nn):
                a, ta = acc[cnt[0] % 2], t_acc[cnt[0] % 2]
                cnt[0] += 1
                for n in range(3):
                    def one_n(n):
                        br, t_br = branches[n]
                        pg, tpg = self.psum()
                        pb, tpb = self.psum()
                        for kc in range(8):
                            k.op("pe", lambda e, kc=kc: e.matmul(pg[:, 0:nn], lhsT=gA[:, kc, n, :], rhs=self.hT[:, kc, c0:c0 + nn],
                                                                 start=(kc == 0), stop=(kc == 7)),
                                 reads=[tsA, self.t_h[ci]], writes=[tpg], inc=(kc == 7))
                        for kc in range(4):
                            k.op("pe", lambda e, kc=kc: e.matmul(pb[:, 0:nn], lhsT=wB[:, kc, n, :], rhs=br[:, kc, c0:c0 + nn],
                                                                 start=(kc == 0), stop=(kc == 3)),
                                 reads=[tsB, t_br[ci]], writes=[tpb], inc=(kc == 3))
                        sg_, tsg = sgf[cnt[1] % 2], t_sgf[cnt[1] % 2]
                        cnt[1] += 1
                        k.op("act", lambda e: e.activation(out=sg_[:, 0:nn], in_=pg[:, 0:nn], func=AF.Sigmoid), reads=[tpg], writes=[tsg])
                        if n == 0:
                            k.op("dve", lambda e: e.tensor_tensor(out=a[:, 0:nn], in0=pb[:, 0:nn], in1=sg_[:, 0:nn], op=ALU.mult),
                                 reads=[tpb, tsg], writes=[ta])
                        else:
                            k.op("dve", lambda e: e.tensor_tensor(out=sg_[:, 0:nn], in0=pb[:, 0:nn], in1=sg_[:, 0:nn], op=ALU.mult),
                                 reads=[tpb, tsg], writes=[tsg])
                            if n == 1:
                                k.op("dve", lambda e: e.tensor_tensor(out=a[:, 0:nn], in0=a[:, 0:nn], in1=sg_[:, 0:nn], op=ALU.add),
                                     reads=[ta, tsg], writes=[ta])
                            else:
                                k.op("dve", lambda e: e.tensor_tensor(out=merged[:, dc, c0:c0 + nn], in0=a[:, 0:nn], in1=sg_[:, 0:nn], op=ALU.add),
                                     reads=[ta, tsg], writes=[t_m[dc][ci]])
                    one_n(n)
            for (ci, c0, nn) in cts:
                one_ct(ci, c0, nn)
        for dc in range(DC):
            one_dc(dc)
        pss = {ci: (self.ps[5 + ci], self.t_ps[5 + ci]) for (ci, _, _) in cts}
        wo = self.w_out[l].rearrange("(kc kp) d -> kp kc d", kp=P)
        def load_wo(ob):
            slot, tsl, ssl = self.wslot()
            sv = slot[:, 0:8 * 512].rearrange("p (kc f) -> p kc f", kc=8)
            k.dma("pool", ssl, lambda e: e.dma_start(out=sv, in_=wo[:, :, ob * 512:(ob + 1) * 512]), writes=[tsl])
            return sv, tsl

        def out_tile(sv, tsl, ob, j, ci, c0, nn):
            oc = ob * 4 + j
            po, tpo = self.psum(0, 5)
            for kc in range(8):
                k.op("pe", lambda e, kc=kc: e.matmul(po[:, 0:nn], lhsT=sv[:, kc, j * 128:(j + 1) * 128], rhs=merged[:, kc, c0:c0 + nn],
                                                     start=(kc == 0), stop=(kc == 7)),
                     reads=[tsl, t_m[kc][ci]], writes=[tpo], inc=(kc == 7))
            k.op("dve", lambda e: e.tensor_copy(out=ybuf[:, oc, c0:c0 + nn], in_=po[:, 0:nn]), reads=[tpo], writes=[t_y[ci]])
            sq, tsq = self.nxt("sq")
            k.op("act", lambda e: e.activation(out=sq[:, 0:nn], in_=ybuf[:, oc, c0:c0 + nn], func=AF.Square), reads=[t_y[ci]], writes=[tsq])
            pS, tpS = pss[ci]
            k.op("pe", lambda e: e.matmul(pS[:, 0:nn], lhsT=self.onesb[:], rhs=sq[:, 0:nn], start=(oc == 0), stop=(oc == DC - 1)),
                 reads=[tsq, self.t_const], writes=[tpS], inc=True)
        sv0, tsl0 = load_wo(0)
        for j in range(4):
            for (ci, c0, nn) in cts:
                out_tile(sv0, tsl0, 0, j, ci, c0, nn)
        sv1, tsl1 = load_wo(1)
        for ti, (ci, c0, nn) in enumerate(cts):
            for j in range(4):
                out_tile(sv1, tsl1, 1, j, ci, c0, nn)
            self.post_norm(l, 3, ybuf, t_y, pss, [(ci, c0, nn)], nxt, do_barrier=(ti == len(cts) - 1))

    def dump(self, name, ap, t, rows=P):
        if not self.dbg:
            return
        k = self.k
        n = int(np.prod(ap.shape[1:]))
        off = self.dbg_off
        self.dbg_off += n
        assert self.dbg_off <= self.dbg
        self.dbg_map[name] = (off, rows, tuple(ap.shape[1:]))
        dst = self.dbgst[0:rows, 0:n]
        if len(ap.shape) == 3:
            dst = dst.rearrange("p (a b) -> p a b", a=ap.shape[1])
        k.op("dve", lambda e: e.tensor_copy(out=dst, in_=ap), reads=[t], writes=[self.t_dbgst])
        k.dma("sp", self.s_dbg, lambda e: e.dma_start(out=self.dbg_out[0:rows, off:off + n], in_=self.dbgst[0:rows, 0:n]),
              reads=[self.t_dbgst])

    def wblock(self, l, col0, ncols=512):
        k = self.k
        slot, tsl, ssl = self.wslot()
        sv = slot[:, 0:8 * ncols].rearrange("p (kc f) -> p kc f", kc=8)
        wv = self.w_in[l].rearrange("(kc kp) f -> kp kc f", kp=P)
        k.dma("pool", ssl, lambda e: e.dma_start(out=sv, in_=wv[:, :, col0:col0 + ncols]), writes=[tsl])
        return sv, tsl

    def proj_fm(self, sv, tsl, j, ci, c0, n, ps, tp):
        k = self.k
        for kc in range(8):
            k.op("pe", lambda e, kc=kc: e.matmul(ps[:, 0:n], lhsT=sv[:, kc, j * 128:(j + 1) * 128], rhs=self.hT[:, kc, c0:c0 + n],
                                                 start=(kc == 0), stop=(kc == 7)),
                 reads=[tsl, self.t_h[ci]], writes=[tp], inc=(kc == 7))

    def proj_tm(self, sv, tsl, ci, t0, m, ps, tp):
        k = self.k
        for kc in range(8):
            k.op("pe", lambda e, kc=kc: e.matmul(ps[0:m, :], lhsT=self.hT[:, kc, t0:t0 + m], rhs=sv[:, kc, :],
                                                 start=(kc == 0), stop=(kc == 7)),
                 reads=[tsl, self.t_h[ci]], writes=[tp], inc=(kc == 7))

    def gmlp(self, l, h, bout, t_bout):
        k = self.k
        ar = self.arena
        NCM = NPH + NS
        vtok = ar.bf16([8, 512])
        u = ar.bf16([4, NCM])
        gt = [ar.f32([512]) for _ in range(2)]
        t_gt = [T("gt0"), T("gt1")]
        vs32 = ar.f32([512])
        t_vs = T("vs32")
        tmps = ar.f32([4, NS])
        t_tmps = T("tmps")
        t_vtok = [T(f"vtok{i}") for i in range(8)]
        t_u = [[T(f"u{j}_{c}") for c in range(3)] for j in range(4)]
        cts = self.cts(h)
        bsr = ar.f32([512])
        t_bsr = T("bsr")
        k.dma("sp", self.s_bsr, lambda e: e.dma_start(out=bsr[0:1, :].rearrange("p (g t) -> p g t", g=4), in_=self.gmlp_bs[l:l + 1, :, :]),
              writes=[t_bsr])
        sv, tsl = self.wblock(l, 2560)
        slices = [(s, s * 128, 128) for s in range(8)] + ([(8, 1024, NS)] if h == 0 else [])

        def do_slice(s, t0, m):
            ci = 2 if s == 8 else s // 4
            ps, tp = self.psum()
            self.proj_tm(sv, tsl, ci, t0, m, ps, tp)
            g, tg = gt[s % 2], t_gt[s % 2]
            st6, tst = self.nxt("st6")
            k.op("act", lambda e: e.activation(out=g[0:m, :], in_=ps[0:m, :], func=AF.Gelu_apprx_tanh), reads=[tp], writes=[tg])
            k.op("dve", lambda e: e.bn_stats(out=st6[0:m, 0:6], in_=g[0:m, :]), reads=[tg], writes=[tst])
            k.op("dve", lambda e: e.bn_aggr(out=st6[0:m, 6:8], in_=st6[0:m, 0:6]), reads=[tst], writes=[tst])
            k.op("act", lambda e: e.activation(out=st6[0:m, 7:8], in_=st6[0:m, 7:8], func=AF.Sqrt, bias=self.epsc[0:m, 0:1]),
                 reads=[tst, self.t_const], writes=[tst])
            k.op("dve", lambda e: e.reciprocal(out=st6[0:m, 7:8], in_=st6[0:m, 7:8]), reads=[tst], writes=[tst])
            k.op("dve", lambda e: e.tensor_scalar(out=g[0:m, :], in0=g[0:m, :], scalar1=st6[0:m, 6:7], scalar2=st6[0:m, 7:8],
                                                  op0=ALU.subtract, op1=ALU.mult), reads=[tg, tst], writes=[tg])
            k.op("dve", lambda e: e.tensor_tensor(out=g[0:m, :], in0=g[0:m, :], in1=self.ngB[0:m, l, :], op=ALU.mult),
                 reads=[tg, self.t_const], writes=[tg])
            if s < 8:
                k.op("dve", lambda e: e.tensor_tensor(out=vtok[0:m, s, :], in0=g[0:m, :], in1=self.nbB[0:m, l, :], op=ALU.add),
                     reads=[tg, self.t_const], writes=[t_vtok[s]])
            else:
                k.op("dve", lambda e: e.tensor_tensor(out=vs32[0:m, :], in0=g[0:m, :], in1=self.nbB[0:m, l, :], op=ALU.add),
                     reads=[tg, self.t_const], writes=[t_vs])
                k.dma("sp", self.s_vs, lambda e: e.dma_start(out=self.vs_out[l, :, :], in_=vs32[0:NS, :]), reads=[t_vs])
        for (s, t0, m) in slices:
            do_slice(s, t0, m)
        sv2, tsl2 = self.wblock(l, 2048)

        def do_u(j, ci, c0, n):
            ps, tp = self.psum()
            self.proj_fm(sv2, tsl2, j, ci, c0, n, ps, tp)
            k.op("act", lambda e: e.activation(out=u[:, j, c0:c0 + n], in_=ps[:, 0:n], func=AF.Gelu_apprx_tanh),
                 reads=[tp], writes=[t_u[j][ci]])
        for j in range(4):
            for (ci, c0, n) in cts:
                do_u(j, ci, c0, n)

        def do_mix(g, ci, c0):
            ps, tp = self.psum()
            for s4 in range(4):
                sl = ci * 4 + s4
                k.op("pe", lambda e, s4=s4, sl=sl: e.matmul(ps[:, s4 * 128:(s4 + 1) * 128], lhsT=vtok[:, sl, g * 128:(g + 1) * 128],
                                                            rhs=self.wmT[:, l, g, :], start=True, stop=False),
                     reads=[t_vtok[sl], self.t_const], writes=[tp], inc=False)
                k.op("pe", lambda e, s4=s4: e.matmul(ps[:, s4 * 128:(s4 + 1) * 128], lhsT=self.onesf[0:1, :],
                                                     rhs=bsr[0:1, g * 128:(g + 1) * 128], start=False, stop=True),
                     reads=[self.t_const, t_bsr], writes=[tp], inc=(s4 == 3))
            k.op("dve", lambda e: e.tensor_tensor(out=bout[:, g, c0:c0 + 512], in0=ps[:, :], in1=u[:, g, c0:c0 + 512], op=ALU.mult),
                 reads=[tp, t_u[g][ci]], writes=[t_bout[ci]])
        for g in range(4):
            for ci in range(2):
                do_mix(g, ci, ci * 512)
        if h == 0:
            ps, tp = self.psum()
            for g in range(4):
                k.op("pe", lambda e, g=g: e.transpose(ps[:, g * NS:(g + 1) * NS], vs32[0:NS, g * 128:(g + 1) * 128],
                                                      self.ident[0:NS, 0:NS]),
                     reads=[t_vs, self.t_const], writes=[tp], inc=(g == 3))
            for g in range(4):
                k.op("dve", lambda e, g=g: e.tensor_scalar(out=tmps[:, g, :], in0=ps[:, g * NS:(g + 1) * NS],
                                                           scalar1=self.wsc[:, l, g:g + 1], scalar2=self.bsc[:, l, g:g + 1],
                                                           op0=ALU.mult, op1=ALU.add),
                     reads=[tp, self.t_const], writes=[t_tmps])
            k.op("dve", lambda e: e.tensor_tensor(out=bout[:, :, NPH:NPH + NS], in0=tmps[:, :, :], in1=u[:, :, NPH:NPH + NS],
                                                  op=ALU.mult),
                 reads=[t_tmps] + [t_u[j][2] for j in range(4)], writes=[t_bout[2]])

    def mixer(self, l, h, pre=True, nxt=None):
        k = self.k
        if pre:
            k.barrier()
        ar = self.arena
        ar.reset()
        NCM = NPH + NS
        aout = ar.bf16([4, NCM])
        bout = ar.bf16([4, NCM])
        cout = ar.bf16([4, NCM])
        t_a = [T(f"aout{c}") for c in range(3)]
        t_b = [T(f"bout{c}") for c in range(3)]
        t_c = [T(f"cout{c}") for c in range(3)]
        if pre:
            self.prenorm(l, 2, h)
        mark = ar.off
        self.gmlp(l, h, bout, t_b)
        if self.dbg and self.stop and self.stop.get("phase") == "gmlp":
            for j in range(4):
                self.dump(f"bout{j}", bout[:, j, :], t_b[0] if False else t_b[2] if h == 0 else t_b[1])
            return
        k.barrier()
        ar.off = mark
        if not (self.stop and self.stop.get("skip_s5")):
            self.s5(l, h, cout, t_c)
            if self.dbg and self.stop and self.stop.get("phase") == "s5":
                for j in range(4):
                    self.dump(f"cout{j}", cout[:, j, :], t_c[2 if h == 0 else 1])
                return
            k.barrier()
            ar.off = mark
        self._pt = {}
        for ci in range(2):
            ar.off = mark
            self.hgrn_tile(l, h, ci, aout, t_a)
        k.barrier()
        ar.off = mark
        if h == 1:
            k.dma("sp", self.s_hgp, lambda e: e.dma_start(out=self.hgp_out[l].rearrange("h k v -> k h v"), in_=self.S32[:, l, :, :]),
                  reads=self.t_S32[l])
        if self.dbg and self.stop and self.stop.get("phase") == "hgrn":
            for j in range(4):
                self.dump(f"aout{j}", aout[:, j, :], t_a[2 if h == 0 else 1])
            return
        self.merge(l, h, [(aout, t_a), (bout, t_b), (cout, t_c)], nxt=nxt)

    def g_ap(self, l, j, dc):
        o = (l * 6 + j) * 8 + dc
        return self.gcol[:, o:o + 1]

    def load_x(self, h):
        k = self.k
        self.arena.reset()
        stage = [self.arena.f32([D]) for _ in range(4)]
        t_st = self.t_stage_in
        s_st = self.s_stage_in
        for (ci, c0, n) in self.cts(h):
            if n == 512:
                for s in range(4):
                    r0 = h * NPH + c0 + s * 128
                    k.dma("sp", s_st[s], lambda e, s=s, r0=r0: e.dma_start(out=stage[s], in_=self.xp[r0:r0 + 128, :]),
                          writes=[t_st[s]])
                for dc in range(DC):
                    ps, tp = self.psum()
                    for s in range(4):
                        k.op("pe", lambda e, s=s, dc=dc, ps=ps: e.transpose(
                            ps[:, s * 128:(s + 1) * 128], stage[s][:, dc * 128:(dc + 1) * 128], self.ident[:]),
                            reads=[t_st[s], self.t_const], writes=[tp], inc=(s == 3))
                    en = "act" if dc % 2 == 0 else "dve"
                    if en == "act":
                        k.op("act", lambda e, dc=dc, ps=ps, c0=c0: e.activation(
                            out=self.xT[:, dc, c0:c0 + 512], in_=ps[:], func=AF.Copy), reads=[tp], writes=[self.t_x[ci]])
                    else:
                        k.op("dve", lambda e, dc=dc, ps=ps, c0=c0: e.tensor_copy(
                            out=self.xT[:, dc, c0:c0 + 512], in_=ps[:]), reads=[tp], writes=[self.t_x[ci]])
            else:
                k.dma("sp", s_st[0], lambda e: e.dma_start(out=stage[0][0:NS, :], in_=self.xs[:, :]), writes=[t_st[0]])
                ps, tp = self.psum()
                for dc in range(DC):
                    k.op("pe", lambda e, dc=dc, ps=ps: e.transpose(
                        ps[:, dc * NS:(dc + 1) * NS], stage[0][0:NS, dc * 128:(dc + 1) * 128], self.ident[0:NS, 0:NS]),
                        reads=[t_st[0], self.t_const], writes=[tp], inc=(dc == DC - 1))
                k.op("dve", lambda e, ps=ps, c0=c0: e.tensor_copy(
                    out=self.xT[:, :, c0:c0 + NS], in_=ps[:, 0:DC * NS].rearrange("p (a b) -> p a b", a=DC)),
                    reads=[tp], writes=[self.t_x[ci]])

    def store_x(self, h):
        k = self.k
        self.arena.reset()
        stage = [self.arena.f32([D]) for _ in range(2)]
        t_st = self.t_stage_out
        s_st = self.s_stage_out
        si = 0
        for (ci, c0, n) in self.cts(h):
            nsl = 4 if n == 512 else 1
            rows = 128 if n == 512 else NS
            for s in range(nsl):
                stg, tst, sst = stage[si % 2], t_st[si % 2], s_st[si % 2]
                si += 1
                for half in range(2):
                    ps, tp = self.psum()
                    for q in range(4):
                        dc = half * 4 + q
                        k.op("pe", lambda e, ps=ps, q=q, dc=dc, s=s, c0=c0, rows=rows: e.transpose(
                            ps[0:rows, q * 128:(q + 1) * 128], self.xT[:, dc, c0 + s * 128:c0 + s * 128 + rows], self.ident[:]),
                            reads=[self.t_x[ci], self.t_const], writes=[tp], inc=(q == 3))
                    if half == 0:
                        k.op("act", lambda e, ps=ps, stg=stg, rows=rows: e.activation(
                            out=stg[0:rows, 0:512], in_=ps[0:rows, :], func=AF.Copy), reads=[tp], writes=[tst])
                    else:
                        k.op("dve", lambda e, ps=ps, stg=stg, rows=rows: e.tensor_copy(
                            out=stg[0:rows, 512:1024], in_=ps[0:rows, :]), reads=[tp], writes=[tst])
                if n == 512:
                    r0 = h * NPH + c0 + s * 128
                    k.dma("sp", sst, lambda e, stg=stg, r0=r0: e.dma_start(out=self.yp[r0:r0 + 128, :], in_=stg[:, :]),
                          reads=[tst])
                else:
                    k.dma("sp", sst, lambda e, stg=stg: e.dma_start(out=self.ys[:, :], in_=stg[0:NS, :]), reads=[tst])

    def rstd_from(self, ps, tp, n, width):
        k = self.k
        rs, trs = self.nxt("rs")
        k.op("act", lambda e: e.activation(out=rs[:, 0:n], in_=ps[:, 0:n], func=AF.Ln, scale=1.0 / width, bias=self.epsc[:, 0:1]),
             reads=[tp, self.t_const], writes=[trs])
        k.op("act", lambda e: e.activation(out=rs[:, 0:n], in_=rs[:, 0:n], func=AF.Exp, scale=-0.5), reads=[trs], writes=[trs])
        return rs, trs

    def prenorm(self, l, j, h):
        for (ci, c0, n) in self.cts(h):
            self.prenorm_tile(l, j, ci, c0, n)

    def prenorm_tile(self, l, j, ci, c0, n):
        k = self.k
        ps, tp = self.ps[5 + ci], self.t_ps[5 + ci]
        for dc in range(DC):
            sq, tsq = self.nxt("sq")
            k.op("act", lambda e, sq=sq, dc=dc: e.activation(out=sq[:, 0:n], in_=self.xT[:, dc, c0:c0 + n], func=AF.Square),
                 reads=[self.t_x[ci]], writes=[tsq])
            k.op("pe", lambda e, sq=sq, dc=dc: e.matmul(ps[:, 0:n], lhsT=self.onesb[:], rhs=sq[:, 0:n], start=(dc == 0), stop=(dc == DC - 1)),
                 reads=[tsq, self.t_const], writes=[tp], inc=True)
        rs, trs = self.rstd_from(ps, tp, n, D)
        for dc in range(DC):
            k.op("dve", lambda e, dc=dc: e.scalar_tensor_tensor(
                out=self.hT[:, dc, c0:c0 + n], in0=self.xT[:, dc, c0:c0 + n], scalar=self.g_ap(l, j, dc),
                in1=rs[:, 0:n], op0=ALU.mult, op1=ALU.mult),
                reads=[self.t_x[ci], trs, self.t_const], writes=[self.t_h[ci]])

    def post_norm(self, l, jpost, ybuf, t_y, pss, cts, nxt, do_barrier=True):
        k = self.k
        for (ci, c0, n) in cts:
            def post(ci, c0, n):
                pS, tpS = pss[ci]
                rs, trs = self.rstd_from(pS, tpS, n, D)
                for dc in range(DC):
                    k.op("dve", lambda e, dc=dc: e.tensor_tensor(out=ybuf[:, dc, c0:c0 + n], in0=ybuf[:, dc, c0:c0 + n], in1=rs[:, 0:n],
                                                                 op=ALU.mult), reads=[t_y[ci], trs], writes=[t_y[ci]])
                    k.op("dve", lambda e, dc=dc: e.scalar_tensor_tensor(out=self.xT[:, dc, c0:c0 + n], in0=ybuf[:, dc, c0:c0 + n],
                                                                        scalar=self.g_ap(l, jpost, dc), in1=self.xT[:, dc, c0:c0 + n],
                                                                        op0=ALU.mult, op1=ALU.add),
                         reads=[t_y[ci], self.t_const, self.t_x[ci]], writes=[self.t_x[ci]])
                if nxt is not None:
                    self.prenorm_tile(nxt[0], nxt[1], ci, c0, n)
            post(ci, c0, n)
        if not do_barrier:
            return
        if nxt is not None:
            k.barrier(exclude=("pe", "pool"))
        else:
            k.barrier()

    def ffn(self, l, i, h, pre=True, nxt=None):
        k = self.k
        cts = self.cts(h)
        NCM = NPH + NS
        self.arena.reset()
        act = self.arena.bf16([FC, NCM])
        ybuf = self.arena.f32([DC, NCM])
        t_act = [[T(f"act{fc}_{c}") for c in range(3)] for fc in range(FC)]
        t_y = [T(f"y{c}") for c in range(3)]
        jpre, jpost = (0, 1) if i == 0 else (4, 5)
        if pre:
            self.prenorm(l, jpre, h)
        wg = self.w_gate[l, i].rearrange("(kc kp) f -> kp kc f", kp=P)
        wu = self.w_up[l, i].rearrange("(kc kp) f -> kp kc f", kp=P)
        wd = self.w_down[l, i].rearrange("(fc fp) d -> fp fc d", fp=P)
        for b in range(11 if self.stop is None else self.stop.get('gu', 11)):
            slot, tsl, ssl = self.wslot()
            sv = slot[:, 0:8 * 2 * 256].rearrange("p (kc g f) -> p kc g f", kc=8, g=2)
            f0 = b * 256
            k.dma("pool", ssl, lambda e, sv=sv, f0=f0: e.dma_start(out=sv[:, :, 0, :], in_=wg[:, :, f0:f0 + 256]), writes=[tsl])
            k.dma("pool", ssl, lambda e, sv=sv, f0=f0: e.dma_start(out=sv[:, :, 1, :], in_=wu[:, :, f0:f0 + 256]), writes=[])
            tsl.w = (ssl, ssl.count)
            for jf in range(2):
                fc = 2 * b + jf
                for (ci, c0, n) in cts:
                    psg, tpg = self.psum(0, 8)
                    psu, tpu = self.psum(0, 8)
                    for kc in range(8):
                        k.op("pe", lambda e, sv=sv, kc=kc, jf=jf, psg=psg, c0=c0, n=n: e.matmul(
                            psg[:, 0:n], lhsT=sv[:, kc, 0, jf * 128:(jf + 1) * 128], rhs=self.hT[:, kc, c0:c0 + n],
                            start=(kc == 0), stop=(kc == 7)), reads=[tsl, self.t_h[ci]], writes=[tpg], inc=(kc == 7))
                    for kc in range(8):
                        k.op("pe", lambda e, sv=sv, kc=kc, jf=jf, psu=psu, c0=c0, n=n: e.matmul(
                            psu[:, 0:n], lhsT=sv[:, kc, 1, jf * 128:(jf + 1) * 128], rhs=self.hT[:, kc, c0:c0 + n],
                            start=(kc == 0), stop=(kc == 7)), reads=[tsl, self.t_h[ci]], writes=[tpu], inc=(kc == 7))
                    sg, tsg = self.nxt("sg")
                    k.op("act", lambda e, sg=sg, psg=psg, n=n: e.activation(out=sg[:, 0:n], in_=psg[:, 0:n], func=AF.Silu),
                         reads=[tpg], writes=[tsg])
                    k.op("dve", lambda e, sg=sg, psu=psu, fc=fc, c0=c0, n=n: e.tensor_tensor(
                        out=act[:, fc, c0:c0 + n], in0=psu[:, 0:n], in1=sg[:, 0:n], op=ALU.mult),
                        reads=[tpu, tsg], writes=[t_act[fc][ci]])
        pss = {ci: (self.ps[5 + ci], self.t_ps[5 + ci]) for (ci, _, _) in cts}

        def load_down(dc):
            slot, tsl, ssl = self.wslot()
            sv = slot[:, 0:22 * 128].rearrange("p (fc d) -> p fc d", fc=22)
            k.dma("pool", ssl, lambda e: e.dma_start(out=sv[:, :, :], in_=wd[:, :, dc * 128:(dc + 1) * 128]), writes=[tsl])
            return sv, tsl

        def down_tile(sv, tsl, dc, ci, c0, n):
            psd, tpd = self.psum(0, 5)
            for fc in range(FC):
                k.op("pe", lambda e, fc=fc: e.matmul(psd[:, 0:n], lhsT=sv[:, fc, :], rhs=act[:, fc, c0:c0 + n],
                                                     start=(fc == 0), stop=(fc == FC - 1)),
                     reads=[tsl, t_act[fc][ci]], writes=[tpd], inc=(fc == FC - 1))
            k.op("dve", lambda e: e.tensor_copy(out=ybuf[:, dc, c0:c0 + n], in_=psd[:, 0:n]), reads=[tpd], writes=[t_y[ci]])
            sq, tsq = self.nxt("sq")
            k.op("act", lambda e: e.activation(out=sq[:, 0:n], in_=ybuf[:, dc, c0:c0 + n], func=AF.Square), reads=[t_y[ci]], writes=[tsq])
            pS, tpS = pss[ci]
            k.op("pe", lambda e: e.matmul(pS[:, 0:n], lhsT=self.onesb[:], rhs=sq[:, 0:n], start=(dc == 0), stop=(dc == DC - 1)),
                 reads=[tsq, self.t_const], writes=[tpS], inc=True)
        for dc in range(DC - 2):
            sv, tsl = load_down(dc)
            for (ci, c0, n) in cts:
                down_tile(sv, tsl, dc, ci, c0, n)
        last = [(dc,) + load_down(dc) for dc in (DC - 2, DC - 1)]
        for ti, (ci, c0, n) in enumerate(cts):
            for (dc, sv, tsl) in last:
                down_tile(sv, tsl, dc, ci, c0, n)
            self.post_norm(l, jpost, ybuf, t_y, pss, [(ci, c0, n)], nxt, do_barrier=(ti == len(cts) - 1))

    def body(self):
        k = self.k
        nc, st = self.nc, self.st
        self.epsc = st.enter_context(nc.sbuf_tensor("epsc", [P, 1], F32))
        k.op("dve", lambda e: e.memset(self.epsc[:], EPS), writes=[self.t_const])
        self.setup_consts()
        self.setup_gmlp()
        self.setup_hgrn()
        self.setup_s5()
        k.barrier()
        halves = [0, 1] if self.stop is None else self.stop.get("halves", [0, 1])
        for h in halves:
            self.load_x(h)
            k.barrier()
            for l in range(DEPTH):
                if self.stop is not None and l >= self.stop.get("layers", DEPTH):
                    break
                if self.stop is not None and self.stop.get("phase") == "io":
                    continue
                if self.stop is not None and self.stop.get("phase") == "norm":
                    self.arena.reset()
                    self.prenorm(l, 0, h)
                    continue
                chain = False
                self.ffn(l, 0, h, pre=(l == 0 or not chain), nxt=(l, 2) if chain else None)
                if self.stop is not None and self.stop.get("phase") == "ffn1":
                    continue
                self.mixer(l, h, pre=not chain, nxt=(l, 4) if chain else None)
                if self.stop is not None and self.stop.get("phase") in ("gmlp", "s5", "hgrn", "mixer"):
                    continue
                self.ffn(l, 1, h, pre=not chain, nxt=(l + 1, 0) if (chain and l + 1 < DEPTH) else None)
            k.barrier()
            self.store_x(h)
            k.barrier()
        eng = k.engs["sp"]
        waits = {}
        for so in self.out_sems:
            if so.count:
                k._need(eng, waits, (so, so.count))
        eng.ops.append((None, list(waits.values()), None))


_CACHE = {}


def get_prog(stop=None):
    key = repr(stop)
    if key not in _CACHE:
        _CACHE[key] = Prog(stop=stop)
    return _CACHE[key]


def make_in_maps(inputs):
    c = host_consts()
    maps = []
    f = lambda a: np.ascontiguousarray(np.asarray(a, dtype=np.float32))
    shared = {n: f(inputs[n]) for n in ("norm_g", "ffn_w_gate", "ffn_w_up", "ffn_w_down", "w_in", "gmlp_ws", "gmlp_bs",
                                        "gmlp_norm_g", "gmlp_norm_b", "hgrn_lb_logits", "hgrn_norm_g", "s5_lam_re", "s5_lam_im", "s5_log_dt",
                                        "s5_b_re", "s5_b_im", "s5_c_re", "s5_c_im", "s5_d", "s5_glu_w", "s5_glu_b", "w_branch", "w_out")}
    xp = f(inputs["x_prompt"])
    xs = f(inputs["x_sample"])
    for core in range(NCORES):
        m = dict(shared)
        m["xp"] = xp[core]
        m["xs"] = xs[core * NS:(core + 1) * NS, 0, :]
        m["c_ident"] = c["ident"]
        m["c_mask128"] = c["mask128"]
        m["c_scanmask"] = c["scanmask"]
        m["c_tri2"] = c["tri2"]
        m["c_pmask"] = c["pmask"]
        m["c_bdmask"] = c["bdmask"]
        m["sre"] = f(inputs["state_s5_re"])[:, core * NS:(core + 1) * NS]
        m["sim"] = f(inputs["state_s5_im"])[:, core * NS:(core + 1) * NS]
        m["shg"] = f(inputs["state_hgrn"])[:, core * NS:(core + 1) * NS]
        maps.append(m)
    return maps


def assemble(r):
    yp = np.stack([r[c]["yp"] for c in range(NCORES)], axis=0)
    ys = np.concatenate([r[c]["ys"] for c in range(NCORES)], axis=0)[:, None, :]
    hgp = np.stack([r[c]["hgp"] for c in range(NCORES)], axis=1)
    rep = np.stack([r[c]["rep"] for c in range(NCORES)], axis=1)
    imp = np.stack([r[c]["imp"] for c in range(NCORES)], axis=1)
    hgs = np.concatenate([r[c]["hgs"] for c in range(NCORES)], axis=1)
    res = np.concatenate([r[c]["res"] for c in range(NCORES)], axis=1)
    ims = np.concatenate([r[c]["ims"] for c in range(NCORES)], axis=1)
    vs = np.concatenate([r[c]["vs"] for c in range(NCORES)], axis=1)[:, :, None, :]
    return tuple(np.ascontiguousarray(a, dtype=np.float32) for a in (yp, ys, hgp, rep, imp, hgs, res, ims, vs))


def kernel(**inputs):
    prog = get_prog()
    res = run_bass_kernel_spmd(prog.nc, make_in_maps(inputs), core_ids=list(range(NCORES)))
    return assemble(res.results)
```

```python
import contextlib
import numpy as np
import concourse.bass as bass
import concourse.mybir as mybir
from concourse.bass_utils import run_bass_kernel_spmd

F32 = mybir.dt.float32
BF16 = mybir.dt.bfloat16
AF = mybir.ActivationFunctionType
ALU = mybir.AluOpType

P = 128
D = 1024
DC = 8
DFF = 2816
FC = 22
NIN = 6656
SEQ = 2048
NPH = 1024
NS = 16
DEPTH = 2
EPS = 1e-6
NCORES = 8


class T:
    __slots__ = ("w", "r", "name")

    def __init__(self, name=""):
        self.w = None
        self.r = []
        self.name = name


class Sem:
    __slots__ = ("h", "count", "name")

    def __init__(self, h, name):
        self.h = h
        self.count = 0
        self.name = name


class Eng:
    def __init__(self, name, sem):
        self.name = name
        self.sem = sem
        self.seen = {}
        self.ops = []


class K:
    def __init__(self, nc, stack):
        self.nc = nc
        self.stack = stack
        self.engs = {}
        for n in ("pe", "act", "dve", "pool", "sp"):
            s = Sem(stack.enter_context(nc.semaphore("s_" + n)), n)
            self.engs[n] = Eng(n, s)
        self.maxwait = {}
        self.dsems = []

    def dsem(self, name):
        s = Sem(self.stack.enter_context(self.nc.semaphore("d_" + name)), name)
        self.dsems.append(s)
        return s

    def _need(self, eng, waits, ev):
        s, v = ev
        if eng.seen.get(id(s), 0) >= v:
            return
        eng.seen[id(s)] = v
        waits[id(s)] = (s, v)
        if v > self.maxwait.get(id(s), (s, 0))[1]:
            self.maxwait[id(s)] = (s, v)

    def _deps(self, eng, reads, writes, own_sem):
        waits = {}
        for t in reads:
            if t.w is not None:
                self._need(eng, waits, t.w)
        for t in writes:
            if t.w is not None and t.w[0] is not own_sem:
                self._need(eng, waits, t.w)
            for ev in t.r:
                if ev[0] is not own_sem:
                    self._need(eng, waits, ev)
        return list(waits.values())

    def op(self, en, fn, reads=(), writes=(), inc=True):
        eng = self.engs[en]
        waits = self._deps(eng, reads, writes, eng.sem if en == "pe" else None)
        ev = (eng.sem, eng.sem.count + 1)
        if inc:
            eng.sem.count += 1
        for t in reads:
            t.r.append(ev)
        for t in writes:
            t.w = ev
            t.r = []
        eng.ops.append((fn, waits, (eng.sem, 1) if inc else None))

    def dma(self, en, sem, fn, reads=(), writes=()):
        eng = self.engs[en]
        waits = self._deps(eng, reads, writes, None)
        sem.count += 16
        ev = (sem, sem.count)
        for t in reads:
            t.r.append(ev)
        for t in writes:
            t.w = ev
            t.r = []
        eng.ops.append((fn, waits, (sem, 16)))

    def barrier(self, exclude=()):
        evs = [(e.sem, e.sem.count) for e in self.engs.values() if e.sem.count > 0]
        evs += [(s, s.count) for s in self.dsems if s.count > 0]
        for en_, eng in self.engs.items():
            if en_ in exclude:
                continue
            waits = {}
            for ev in evs:
                if ev[0] is not eng.sem:
                    self._need(eng, waits, ev)
            if waits:
                eng.ops.append((None, list(waits.values()), None))

    def emit(self):
        nc = self.nc
        for s, v in self.maxwait.values():
            assert v <= s.count, f"wait on {s.name} for {v} > {s.count}"
        hmap = {"pe": "tensor", "act": "scalar", "dve": "vector", "pool": "gpsimd", "sp": "sync"}
        with nc.Block() as block:
            for en, eng in self.engs.items():
                def body(e, eng=eng):
                    for fn, waits, inc in eng.ops:
                        if fn is None:
                            for s, v in waits:
                                e.wait_ge(s.h, v)
                            continue
                        for s, v in waits[1:]:
                            e.wait_ge(s.h, v)
                        ins = fn(e)
                        if waits:
                            ins._wait_ge(waits[0][0].h, waits[0][1])
                        if inc is not None:
                            ins.then_inc(inc[0].h, inc[1])
                getattr(block, hmap[en])(body)


class Arena:
    def __init__(self, ap, nwords):
        self.ap = ap
        self.n = nwords
        self.off = 0
        self.peak = 0

    def reset(self):
        self.off = 0

    def f32(self, shape):
        n = int(np.prod(shape))
        a = self.ap[:, self.off:self.off + n]
        self.off += n
        self.peak = max(self.peak, self.off)
        assert self.off <= self.n, f"arena overflow {self.off} > {self.n}"
        if len(shape) == 2:
            return a.rearrange("p (a b) -> p a b", a=shape[0])
        if len(shape) == 3:
            return a.rearrange("p (a b c) -> p a b c", a=shape[0], b=shape[1])
        return a

    def bf16(self, shape):
        n = int(np.prod(shape))
        nw = (n + 1) // 2
        a = self.ap[:, self.off:self.off + nw].bitcast(BF16)[:, 0:n]
        self.off += nw
        self.peak = max(self.peak, self.off)
        assert self.off <= self.n, f"arena overflow {self.off} > {self.n}"
        if len(shape) == 2:
            return a.rearrange("p (a b) -> p a b", a=shape[0])
        if len(shape) == 3:
            return a.rearrange("p (a b c) -> p a b c", a=shape[0], b=shape[1])
        return a


def host_consts():
    c = {}
    c["ident"] = np.eye(128, dtype=np.float32)
    i = np.arange(128)
    c["mask128"] = (i[:, None] <= i[None, :]).astype(np.float32)
    sm = np.ones((128, 528), np.float32)
    sm[:, 0:512:64] = 0.0
    sm[:, 512:] = 0.0
    c["scanmask"] = sm
    pm = np.zeros((128, 4), np.float32)
    pm[:, 0] = ((i // 16) % 2 == 0)
    pm[:, 1] = ((i // 16) % 2 == 1)
    pm[:, 2] = np.where(i < 64, -1.0, 1.0)
    pm[:, 3] = -pm[:, 2]
    c["pmask"] = pm
    c["bdmask"] = ((i[:, None] // 16) == (i[None, :] // 16)).astype(np.float32)
    c["tri2"] = ((i[:, None] % 64) <= np.arange(64)[None, :]).astype(np.float32)
    return c


class Prog:
    def __init__(self, stop=None, dbg=None):
        self.stop = stop
        self.dbg = dbg
        nc = bass.Bass("TRN2", target_bir_lowering=False)
        self.nc = nc
        with contextlib.ExitStack() as st:
            self.st = st
            self.k = K(nc, st)
            self.declare_io()
            self.alloc()
            self.body()
            self.k.emit()

    def declare_io(self):
        nc = self.nc

        def din(name, shape):
            return nc.dram_tensor(name, list(shape), F32, kind="ExternalInput").ap()

        def dout(name, shape):
            return nc.dram_tensor(name, list(shape), F32, kind="ExternalOutput").ap()

        self.xp = din("xp", [SEQ, D])
        self.xs = din("xs", [NS, D])
        self.norm_g = din("norm_g", [DEPTH, 6, D])
        self.w_gate = din("ffn_w_gate", [DEPTH, 2, D, DFF])
        self.w_up = din("ffn_w_up", [DEPTH, 2, D, DFF])
        self.w_down = din("ffn_w_down", [DEPTH, 2, DFF, D])
        self.w_in = din("w_in", [DEPTH, D, NIN])
        self.gmlp_ws = din("gmlp_ws", [DEPTH, 4, 128, 128])
        self.gmlp_bs = din("gmlp_bs", [DEPTH, 4, 128])
        self.gmlp_ng = din("gmlp_norm_g", [DEPTH, 512])
        self.gmlp_nb = din("gmlp_norm_b", [DEPTH, 512])
        self.lb_logits = din("hgrn_lb_logits", [DEPTH, 512])
        self.hgrn_ng = din("hgrn_norm_g", [DEPTH, 512])
        self.shg = din("shg", [DEPTH, NS, 4, 128, 128])
        self.c_scanmask = din("c_scanmask", [128, 528])
        self.c_tri2 = din("c_tri2", [128, 64])
        self.lam_re = din("s5_lam_re", [DEPTH, 32, 64])
        self.lam_im = din("s5_lam_im", [DEPTH, 32, 64])
        self.log_dt = din("s5_log_dt", [DEPTH, 32])
        self.b_re = din("s5_b_re", [DEPTH, 32, 64, 16])
        self.b_im = din("s5_b_im", [DEPTH, 32, 64, 16])
        self.c_re = din("s5_c_re", [DEPTH, 32, 16, 64])
        self.c_im = din("s5_c_im", [DEPTH, 32, 16, 64])
        self.s5_d = din("s5_d", [DEPTH, 512])
        self.glu_w = din("s5_glu_w", [DEPTH, 512, 1024])
        self.glu_b = din("s5_glu_b", [DEPTH, 1024])
        self.w_branch = din("w_branch", [DEPTH, 3, 512, D])
        self.w_out = din("w_out", [DEPTH, D, D])
        self.sre = din("sre", [DEPTH, NS, 32, 64])
        self.sim = din("sim", [DEPTH, NS, 32, 64])
        self.s5scr = nc.dram_tensor("s5scr", [DEPTH, 4, 128, 5120], BF16, kind="Internal").ap()
        self.t_scr = [[T(f"scr{l}{c}") for c in range(4)] for l in range(DEPTH)]
        self.s5rot = nc.dram_tensor("s5rot", [DEPTH, 128, 3, 32, 128], F32, kind="Internal").ap()
        self.t_rot = [T(f"rot{l}") for l in range(DEPTH)]
        self.c_pmask = din("c_pmask", [128, 4])
        self.c_bdmask = din("c_bdmask", [128, 128])
        self.c_ident = din("c_ident", [128, 128])
        self.c_mask128 = din("c_mask128", [128, 128])
        self.yp = dout("yp", [SEQ, D])
        self.ys = dout("ys", [NS, D])
        self.vs_out = dout("vs", [DEPTH, NS, 512])
        self.rep_out = dout("rep", [DEPTH, 32, 64])
        self.imp_out = dout("imp", [DEPTH, 32, 64])
        self.res_out = dout("res", [DEPTH, NS, 32, 64])
        self.ims_out = dout("ims", [DEPTH, NS, 32, 64])
        self.hgp_out = dout("hgp", [DEPTH, 4, 128, 128])
        self.hgs_out = dout("hgs", [DEPTH, NS, 4, 128, 128])
        if self.dbg:
            self.dbg_out = dout("dbg", [128, self.dbg])
            self.dbg_off = 0
            self.dbg_map = {}

    def alloc(self):
        nc, st, k = self.nc, self.st, self.k

        self.sb_bytes = {}

        def sb(name, shape, dt):
            self.sb_bytes[name] = int(np.prod(shape[1:])) * (2 if dt == BF16 else 4)
            return st.enter_context(nc.sbuf_tensor(name, list(shape), dt))

        NCM = NPH + NS
        self.xT = sb("xT", [P, DC, NCM], F32)
        self.hT = sb("hT", [P, DC, NCM], BF16)
        self.t_x = [T(f"x{c}") for c in range(3)]
        self.t_h = [T(f"h{c}") for c in range(3)]
        self.NSLOT = 3
        self.SLOTW = 4096
        self.ring = [sb(f"ring{i}", [P, self.SLOTW], BF16) for i in range(self.NSLOT)]
        self.t_ring = [T(f"ring{i}") for i in range(self.NSLOT)]
        self.s_ring = [k.dsem(f"ring{i}") for i in range(self.NSLOT)]
        self.ring_i = 0
        self.ARENA_W = 22 * 1024
        ar = sb("arena", [P, self.ARENA_W], F32)
        self.arena = Arena(ar, self.ARENA_W)
        self.ident = sb("ident", [P, P], F32)
        self.identb = sb("identb", [P, P], BF16)
        self.onesb = sb("onesb", [P, P], BF16)
        self.gstage = sb("gstage", [P, P], F32)
        self.gcol = sb("gcol", [P, 96], F32)
        self.sq = [sb(f"sq{i}", [P, 512], BF16) for i in range(3)]
        self.t_sq = [T(f"sq{i}") for i in range(3)]
        self.sq_i = 0
        self.rs = [sb(f"rs{i}", [P, 512], F32) for i in range(3)]
        self.t_rs = [T(f"rs{i}") for i in range(3)]
        self.rs_i = 0
        self.sg = [sb(f"sg{i}", [P, 512], BF16) for i in range(3)]
        self.t_sg = [T(f"sg{i}") for i in range(3)]
        self.sg_i = 0
        self.t_const = T("const")
        self.s_const = k.dsem("const")
        self.s_vs = k.dsem("vs")
        self.s_bsr = k.dsem("bsr")
        self.t_stage_in = [T(f"stage{i}") for i in range(4)]
        self.s_stage_in = [k.dsem(f"stin{i}") for i in range(4)]
        self.t_stage_out = [T(f"ostage{i}") for i in range(2)]
        self.s_stage_out = [k.dsem(f"stout{i}") for i in range(2)]
        self.out_sems = list(self.s_stage_out) + [self.s_vs]
        self.wmT = sb("wmT", [P, DEPTH, 4, 128], BF16)
        self.ngB = sb("ngB", [P, DEPTH, 512], F32)
        self.nbB = sb("nbB", [P, DEPTH, 512], F32)
        self.onesf = sb("onesf", [NS, 128], F32)
        self.S32 = sb("S32", [P, DEPTH, 4, 128], F32)
        self.Sb = sb("Sb", [P, DEPTH, 4, 128], BF16)
        self.t_S32 = [[T(f"S32_{l}{hh}") for hh in range(4)] for l in range(DEPTH)]
        self.t_Sb = [[T(f"Sb_{l}{hh}") for hh in range(4)] for l in range(DEPTH)]
        self.hcol = sb("hcol", [P, 16], F32)
        self.lbc = sb("lbc", [P, DEPTH, 4], F32)
        self.omlc = sb("omlc", [P, DEPTH, 4], F32)
        self.scanmask = sb("scanmask", [P, 528], F32)
        self.tri2 = sb("tri2", [P, 64], F32)
        self.s5tab = sb("s5tab", [P, DEPTH, 8, 2, 32], F32)
        self.s5coef = sb("s5coef", [P, DEPTH, 2, 32], F32)
        self.Hprev = sb("Hprev", [P, DEPTH, 2, 32], F32)
        self.Hnew = sb("Hnew", [P, 2, 32], F32)
        self.t_Hnew = T("Hnew")
        self.pmask = sb("pmask", [P, 4], F32)
        self.bdmask = sb("bdmask", [P, P], F32)
        self.halfpi = sb("halfpi", [P, 1], F32)
        self.scol = sb("scol", [P, 24], F32)
        self.s_s5in = [k.dsem(f"s5in{i}") for i in range(3)]
        self.s_s5x = [k.dsem(f"s5x{i}") for i in range(2)]
        self.s_s5ck = k.dsem("s5ck")
        self.s_s5p = k.dsem("s5p")
        self.s_s5s = k.dsem("s5s")
        self.out_sems += [self.s_s5p, self.s_s5s]
        self.s_hg = [k.dsem(f"hg{i}") for i in range(2)]
        self.s_hgp = k.dsem("hgp")
        self.out_sems += self.s_hg + [self.s_hgp]
        self.wsc = sb("wsc", [P, DEPTH, 4], F32)
        self.bsc = sb("bsc", [P, DEPTH, 4], F32)
        self.mask128 = sb("mask128", [P, P], F32)
        self.st6 = [sb(f"st6_{i}", [P, 8], F32) for i in range(2)]
        self.t_st6 = [T(f"st6_{i}") for i in range(2)]
        self.st6_i = 0
        if self.dbg:
            self.dbgst = sb("dbgst", [P, 1040], F32)
            self.t_dbgst = T("dbgst")
            self.s_dbg = k.dsem("dbg")
            self.out_sems.append(self.s_dbg)
        self.ps = [st.enter_context(nc.psum_tensor(f"ps{i}", [P, 512], F32)) for i in range(8)]
        self.t_ps = [T(f"ps{i}") for i in range(8)]
        self.ps_i = 0

    def psum(self, lo=0, hi=8):
        if not (lo <= self.ps_i < hi):
            self.ps_i = lo
        i = self.ps_i
        self.ps_i = lo + (i + 1 - lo) % (hi - lo)
        return self.ps[i], self.t_ps[i]

    def nxt(self, what):
        lst = getattr(self, what)
        tl = getattr(self, "t_" + what)
        i = getattr(self, what + "_i")
        setattr(self, what + "_i", (i + 1) % len(lst))
        return lst[i], tl[i]

    def cts(self, h):
        c = [(0, 0, 512), (1, 512, 512)]
        if h == 0:
            c.append((2, 1024, NS))
        return c

    def wslot(self):
        i = self.ring_i
        self.ring_i = (i + 1) % self.NSLOT
        return self.ring[i], self.t_ring[i], self.s_ring[i]

    def setup_consts(self):
        k = self.k
        tc = self.t_const
        k.dma("sp", self.s_const, lambda e: e.dma_start(out=self.ident[:], in_=self.c_ident[:, :]), writes=[tc])
        k.dma("sp", self.s_const, lambda e: e.dma_start(
            out=self.gstage[0:96, :], in_=self.norm_g.rearrange("l j (c p) -> (l j c) p", p=P)), writes=[tc])
        k.op("dve", lambda e: e.tensor_copy(out=self.identb[:], in_=self.ident[:]), reads=[tc], writes=[tc])
        k.op("dve", lambda e: e.memset(self.onesb[:], 1.0), writes=[tc])
        ps, tp = self.psum()
        k.op("pe", lambda e: e.transpose(ps[:, 0:96], self.gstage[0:96, :], self.ident[0:96, 0:96]),
             reads=[tc], writes=[tp])
        k.op("dve", lambda e: e.tensor_copy(out=self.gcol[:], in_=ps[:, 0:96]), reads=[tp], writes=[tc])
        for l in range(DEPTH):
            for j in (1, 5):
                o = (l * 6 + j) * 8
                k.op("dve", lambda e, o=o: e.tensor_scalar(out=self.gcol[:, o:o + 8], in0=self.gcol[:, o:o + 8],
                                                           scalar1=0.5, scalar2=None, op0=ALU.mult),
                     reads=[tc], writes=[tc])

    def setup_gmlp(self):
        k = self.k
        tc = self.t_const
        sc = self.s_const
        k.dma("act", sc, lambda e: e.dma_start(out=self.mask128[:], in_=self.c_mask128[:, :]), writes=[tc])
        k.op("dve", lambda e: e.memset(self.onesf[:], 1.0), writes=[tc])
        for l in range(DEPTH):
            def one_layer(l):
                k.dma("act", sc, lambda e: e.dma_start(out=self.ngB[:, l, :], in_=self.gmlp_ng[l:l + 1, :].partition_broadcast(P)),
                      writes=[tc])
                k.dma("act", sc, lambda e: e.dma_start(out=self.nbB[:, l, :], in_=self.gmlp_nb[l:l + 1, :].partition_broadcast(P)),
                      writes=[tc])
                for g in range(4):
                    def one_g(g):
                        k.dma("act", sc, lambda e: e.dma_start(
                            out=self.wsc[:, l, g:g + 1], in_=self.gmlp_ws[l, g, 0:1, 0:1].partition_broadcast(P)), writes=[tc])
                        k.dma("act", sc, lambda e: e.dma_start(
                            out=self.bsc[:, l, g:g + 1], in_=self.gmlp_bs[l, g:g + 1, 0:1].partition_broadcast(P)), writes=[tc])
                        tg = T("wstage")
                        sg_ = k.dsem(f"wst{l}{g}")
                        k.dma("act", sg_, lambda e: e.dma_start(out=self.gstage[:, :], in_=self.gmlp_ws[l, g, :, :]),
                              reads=[tc], writes=[tc, tg])
                        ps, tp = self.psum()
                        k.op("pe", lambda e: e.transpose(ps[:, 0:128], self.gstage[:, :], self.ident[:]),
                             reads=[tc, tg], writes=[tp])
                        k.op("dve", lambda e: e.tensor_tensor(out=self.wmT[:, l, g, :], in0=ps[:, 0:128], in1=self.mask128[:],
                                                              op=ALU.mult), reads=[tp, tc], writes=[tc])
                    one_g(g)
            one_layer(l)

    def setup_hgrn(self):
        k = self.k
        tc, sc = self.t_const, self.s_const
        k.dma("act", sc, lambda e: e.dma_start(out=self.scanmask[:], in_=self.c_scanmask[:, :]), writes=[tc])
        k.dma("act", sc, lambda e: e.dma_start(out=self.tri2[:], in_=self.c_tri2[:, :]), writes=[tc])
        tg = T("hstage")
        sg_ = k.dsem("hstage")
        k.dma("act", sg_, lambda e: e.dma_start(out=self.gstage[0:8, :], in_=self.lb_logits.rearrange("l (h p) -> (l h) p", p=P)),
              reads=[tc], writes=[tc, tg])
        k.dma("act", sg_, lambda e: e.dma_start(out=self.gstage[8:16, :], in_=self.hgrn_ng.rearrange("l (h p) -> (l h) p", p=P)),
              writes=[tg])
        ps, tp = self.psum()
        k.op("pe", lambda e: e.transpose(ps[:, 0:16], self.gstage[0:16, :], self.ident[0:16, 0:16]), reads=[tc, tg], writes=[tp])
        k.op("dve", lambda e: e.tensor_copy(out=self.hcol[:], in_=ps[:, 0:16]), reads=[tp], writes=[tc])
        k.op("dve", lambda e: e.memset(self.lbc[:, 0, :], 0.0), writes=[tc])
        k.op("dve", lambda e: e.tensor_tensor(out=self.lbc[:, 1, :], in0=self.hcol[:, 4:8], in1=self.hcol[:, 0:4], op=ALU.subtract),
             reads=[tc], writes=[tc])
        k.op("act", lambda e: e.activation(out=self.lbc[:, 1, :], in_=self.lbc[:, 1, :], func=AF.Sigmoid), reads=[tc], writes=[tc])
        k.op("dve", lambda e: e.tensor_scalar(out=self.omlc[:], in0=self.lbc[:], scalar1=-1.0, scalar2=1.0, op0=ALU.mult, op1=ALU.add),
             reads=[tc], writes=[tc])
        k.op("dve", lambda e: e.memset(self.S32[:], 0.0), writes=[t for l in range(DEPTH) for t in self.t_S32[l]])
        k.op("dve", lambda e: e.memset(self.Sb[:], 0.0), writes=[t for l in range(DEPTH) for t in self.t_Sb[l]])

    def hgrn_tile(self, l, h, ci, aout, t_a):
        k = self.k
        ar = self.arena
        c0 = ci * 512
        samp = (h == 0 and ci == 1)
        NW = 512 + (NS if samp else 0)
        NWA = 512 + NS
        ABC = [(ar.f32([NWA])[:, 0:NW], ar.f32([NWA])[:, 0:NW], ar.f32([NWA])[:, 0:NW]) for _ in range(2)]
        tABC = [(self.pt(f"A{i}"), self.pt(f"Bk{i}"), self.pt(f"C{i}")) for i in range(2)]
        QB = ar.bf16([4, 512]); KBh = ar.bf16([4, 512]); KD = ar.bf16([4, 512])
        tQB = [self.pt(f"QB{i}") for i in range(4)]; tKB = [self.pt(f"KB{i}") for i in range(4)]; tKD = [self.pt(f"KD{i}") for i in range(4)]
        ebl = ar.f32([4, 8]); t_ebl = [self.pt(f"ebl{i}") for i in range(4)]
        vtok = ar.bf16([4, 512]); t_vtok = [self.pt(f"hv{i}") for i in range(4)]
        kdtok = ar.bf16([4, 512]); t_kdtok = [self.pt(f"kdt{i}") for i in range(4)]
        AmT = ar.bf16([4, 4 * 64]); t_AmT = [self.pt(f"AmT{i}") for i in range(4)]
        sgh = ar.bf16([4, NWA]); t_sgh = [self.pt(f"sgh{i}") for i in range(4)]
        O = [ar.f32([512]) for _ in range(2)]; tO = [self.pt("O0"), self.pt("O1")]
        if samp:
            f_s = ar.f32([4, NS]); kk_s = ar.f32([4, NS]); q_s = ar.f32([4, NS])
            t_fs, t_ks, t_qs = self.pt("f_s"), self.pt("kk_s"), self.pt("q_s")
            v_s = ar.f32([512]); t_vs = self.pt("hv_s")
            vmask = [ar.f32([512]) for _ in range(2)]; t_vm = [self.pt("vm0"), self.pt("vm1")]
            Sj = [ar.f32([4, 128]) for _ in range(2)]; t_Sj = [self.pt("Sj0"), self.pt("Sj1")]
        svf, tsf = self.wblock(l, 512)
        svq, tsq_ = self.wblock(l, 0)

        def head(hh):
            A, Bk, C = ABC[hh % 2]
            tA, tB, tC = tABC[hh % 2]
            ps, tp = self.psum(0, 8)
            self.proj_fm(svf, tsf, hh, ci, c0, 512, ps, tp)
            k.op("act", lambda e: e.activation(out=A[:, 0:512], in_=ps[:, 0:512], func=AF.Sigmoid), reads=[tp], writes=[tA])
            if samp:
                ps2, tp2 = self.psum(0, 8)
                self.proj_fm(svf, tsf, hh, 2, NPH, NS, ps2, tp2)
                k.op("act", lambda e: e.activation(out=A[:, 512:NW], in_=ps2[:, 0:NS], func=AF.Sigmoid), reads=[tp2], writes=[tA])
            k.op("dve", lambda e: e.tensor_scalar(out=A[:, :], in0=A[:, :], scalar1=self.omlc[:, l, hh:hh + 1],
                                                  scalar2=self.lbc[:, l, hh:hh + 1], op0=ALU.mult, op1=ALU.add),
                 reads=[tA, self.t_const], writes=[tA])
            k.op("dve", lambda e: e.tensor_scalar(out=Bk[:, :], in0=A[:, :], scalar1=-1.0, scalar2=1.0, op0=ALU.mult, op1=ALU.add),
                 reads=[tA], writes=[tB])
            if samp:
                k.op("dve", lambda e: e.tensor_copy(out=f_s[:, hh, :], in_=A[:, 512:NW]), reads=[tA], writes=[t_fs])
                k.op("dve", lambda e: e.tensor_copy(out=kk_s[:, hh, :], in_=Bk[:, 512:NW]), reads=[tB], writes=[t_ks])
            k.op("act", lambda e: e.activation(out=C[:, :], in_=A[:, :], func=AF.Ln), reads=[tA], writes=[tC])
            k.op("dve", lambda e: e.tensor_tensor_scan(out=A[:, :], data0=self.scanmask[:, 0:NW], data1=C[:, :], initial=0.0,
                                                       op0=ALU.mult, op1=ALU.add), reads=[tC, self.t_const], writes=[tA])
            k.op("act", lambda e: e.activation(out=C[:, :], in_=A[:, :], func=AF.Exp), reads=[tA], writes=[tC])
            k.op("act", lambda e: e.activation(out=A[:, :], in_=A[:, :], func=AF.Exp, scale=-1.0), reads=[tA], writes=[tA])
            k.op("dve", lambda e: e.tensor_tensor(out=Bk[:, 0:512], in0=Bk[:, 0:512], in1=A[:, 0:512], op=ALU.mult),
                 reads=[tB, tA], writes=[tB])
            k.op("act", lambda e: e.activation(out=KBh[:, hh, :], in_=Bk[:, 0:512], func=AF.Copy), reads=[tB], writes=[tKB[hh]])
            k.op("dve", lambda e: e.tensor_tensor(
                out=KD[:, hh, :].rearrange("p (c t) -> p c t", t=64), in0=Bk[:, 0:512].rearrange("p (c t) -> p c t", t=64),
                in1=C[:, 0:512].rearrange("p (c t) -> p c t", t=64)[:, :, 63:64].to_broadcast([P, 8, 64]), op=ALU.mult),
                reads=[tB, tC], writes=[tKD[hh]])
            k.op("dve", lambda e: e.tensor_copy(out=ebl[:, hh, :], in_=C[:, 0:512].rearrange("p (c t) -> p c t", t=64)[:, :, 63]),
                 reads=[tC], writes=[t_ebl[hh]])
            psq, tpq = self.psum(0, 8)
            self.proj_fm(svq, tsq_, hh, ci, c0, 512, psq, tpq)
            k.op("dve", lambda e: e.tensor_tensor(out=QB[:, hh, :], in0=psq[:, 0:512], in1=C[:, 0:512], op=ALU.mult),
                 reads=[tpq, tC], writes=[tQB[hh]])
            if samp:
                psq2, tpq2 = self.psum(0, 8)
                self.proj_fm(svq, tsq_, hh, 2, NPH, NS, psq2, tpq2)
                k.op("dve", lambda e: e.tensor_copy(out=q_s[:, hh, :], in_=psq2[:, 0:NS]), reads=[tpq2], writes=[t_qs])
        for hh in range(4):
            head(hh)
        svi, tsi = self.wblock(l, 1024)

        def vslice(s4):
            ps, tp = self.psum(0, 8)
            self.proj_tm(svi, tsi, ci, c0 + s4 * 128, 128, ps, tp)
            k.op("act", lambda e: e.activation(out=vtok[:, s4, :], in_=ps[:, :], func=AF.Copy), reads=[tp], writes=[t_vtok[s4]])
        for s4 in range(4):
            vslice(s4)
        if samp:
            ps, tp = self.psum(0, 8)
            self.proj_tm(svi, tsi, 2, NPH, NS, ps, tp)
            k.op("act", lambda e: e.activation(out=v_s[0:NS, :], in_=ps[0:NS, :], func=AF.Copy), reads=[tp], writes=[t_vs])
        svg, tsg_ = self.wblock(l, 1536)

        def ghead(hh):
            ps, tp = self.psum(0, 8)
            self.proj_fm(svg, tsg_, hh, ci, c0, 512, ps, tp)
            k.op("act", lambda e: e.activation(out=sgh[:, hh, 0:512], in_=ps[:, 0:512], func=AF.Silu), reads=[tp], writes=[t_sgh[hh]])
            if samp:
                ps2, tp2 = self.psum(0, 8)
                self.proj_fm(svg, tsg_, hh, 2, NPH, NS, ps2, tp2)
                k.op("act", lambda e: e.activation(out=sgh[:, hh, 512:NW], in_=ps2[:, 0:NS], func=AF.Silu), reads=[tp2], writes=[t_sgh[hh]])
        for hh in range(4):
            ghead(hh)

        def kdT(s4):
            ps, tp = self.psum(0, 8)
            psb = ps[:, 0:256].bitcast(BF16)
            for hh in range(4):
                k.op("pe", lambda e, hh=hh: e.transpose(psb[:, hh * 128:(hh + 1) * 128], KD[:, hh, s4 * 128:(s4 + 1) * 128],
                                                        self.identb[:]),
                     reads=[tKD[hh], self.t_const], writes=[tp], inc=(hh == 3))
            k.op("act", lambda e: e.activation(out=kdtok[:, s4, :], in_=psb[:, :], func=AF.Copy), reads=[tp], writes=[t_kdtok[s4]])
        for s4 in range(4):
            kdT(s4)

        def scores(hh):
            ps, tp = self.psum(0, 8)
            for c in range(8):
                b0 = (c % 2) * 64
                pr = c // 2
                k.op("pe", lambda e, c=c, b0=b0, pr=pr: e.matmul(ps[b0:b0 + 64, pr * 64:(pr + 1) * 64],
                                                                 lhsT=KBh[:, hh, c * 64:(c + 1) * 64], rhs=QB[:, hh, c * 64:(c + 1) * 64],
                                                                 start=True, stop=True),
                     reads=[tKB[hh], tQB[hh]], writes=[tp], inc=(c == 7))
            k.op("dve", lambda e: e.tensor_tensor(out=AmT[:, hh, :].rearrange("p (a t) -> p a t", t=64),
                                                  in0=ps[:, 0:256].rearrange("p (a t) -> p a t", t=64),
                                                  in1=self.tri2[:, :].unsqueeze(1).to_broadcast([P, 4, 64]), op=ALU.mult),
                 reads=[tp, self.t_const], writes=[t_AmT[hh]])
        for hh in range(4):
            scores(hh)
        po = [(self.ps[hh], self.t_ps[hh]) for hh in range(4)]
        psu_i = [0]

        def chunk(c, hh):
            b0 = (c % 2) * 64
            s4 = c // 2
            pO, tpO = po[hh]
            k.op("pe", lambda e: e.matmul(pO[:, c * 64:(c + 1) * 64], lhsT=self.Sb[:, l, hh, :], rhs=QB[:, hh, c * 64:(c + 1) * 64],
                                          start=True, stop=False), reads=[self.t_Sb[l][hh], tQB[hh]], writes=[tpO], inc=False)
            k.op("pe", lambda e: e.matmul(pO[:, c * 64:(c + 1) * 64], lhsT=vtok[b0:b0 + 64, s4, hh * 128:(hh + 1) * 128],
                                          rhs=AmT[b0:b0 + 64, hh, s4 * 64:(s4 + 1) * 64], start=False, stop=True),
                 reads=[t_vtok[s4], t_AmT[hh]], writes=[tpO], inc=True)
            bi = 4 + (psu_i[0] % 2)
            psu_i[0] += 1
            pU, tpU = self.ps[bi], self.t_ps[bi]
            k.op("pe", lambda e: e.matmul(pU[:, 0:128], lhsT=kdtok[b0:b0 + 64, s4, hh * 128:(hh + 1) * 128],
                                          rhs=vtok[b0:b0 + 64, s4, hh * 128:(hh + 1) * 128], start=True, stop=True),
                 reads=[t_kdtok[s4], t_vtok[s4]], writes=[tpU], inc=True)
            k.op("dve", lambda e: e.scalar_tensor_tensor(out=self.S32[:, l, hh, :], in0=self.S32[:, l, hh, :], scalar=ebl[:, hh, c:c + 1],
                                                         in1=pU[:, 0:128], op0=ALU.mult, op1=ALU.add),
                 reads=[self.t_S32[l][hh], t_ebl[hh], tpU], writes=[self.t_S32[l][hh]])
            k.op("act", lambda e: e.activation(out=self.Sb[:, l, hh, :], in_=self.S32[:, l, hh, :], func=AF.Copy),
                 reads=[self.t_S32[l][hh]], writes=[self.t_Sb[l][hh]])
        for c in range(8):
            for hh in range(4):
                chunk(c, hh)

        def onorm(hh, pO, tpO, n, acol0, scol0):
            o, to = O[hh % 2], tO[hh % 2]
            k.op("dve", lambda e: e.tensor_copy(out=o[:, 0:n], in_=pO[:, 0:n]), reads=[tpO], writes=[to])
            sq, tsq = self.nxt("sq")
            k.op("act", lambda e: e.activation(out=sq[:, 0:n], in_=o[:, 0:n], func=AF.Square), reads=[to], writes=[tsq])
            pn, tpn = self.psum(6, 8)
            k.op("pe", lambda e: e.matmul(pn[:, 0:n], lhsT=self.onesb[:], rhs=sq[:, 0:n], start=True, stop=True),
                 reads=[tsq, self.t_const], writes=[tpn])
            rs, trs = self.rstd_from(pn, tpn, n, 128)
            k.op("dve", lambda e: e.tensor_tensor(out=o[:, 0:n], in0=o[:, 0:n], in1=rs[:, 0:n], op=ALU.mult), reads=[to, trs], writes=[to])
            k.op("dve", lambda e: e.scalar_tensor_tensor(out=aout[:, hh, acol0:acol0 + n], in0=o[:, 0:n],
                                                         scalar=self.hcol[:, 8 + l * 4 + hh:9 + l * 4 + hh],
                                                         in1=sgh[:, hh, scol0:scol0 + n], op0=ALU.mult, op1=ALU.mult),
                 reads=[to, self.t_const, t_sgh[hh]], writes=[t_a[2 if acol0 >= NPH else ci]])
        for hh in range(4):
            onorm(hh, po[hh][0], po[hh][1], 512, c0, 0)
        if samp:
            pso, tpso = self.ps[5], self.t_ps[5]

            def sample(j):
                sj, tsj, ssj = Sj[j % 2], t_Sj[j % 2], self.s_hg[j % 2]
                vm, tvm = vmask[j % 2], t_vm[j % 2]
                k.dma("sp", ssj, lambda e: e.dma_start(out=sj[:, :, :], in_=self.shg[l, j].rearrange("h k v -> k h v")), writes=[tsj])
                k.op("dve", lambda e: e.tensor_scalar(out=vm[0:NS, :], in0=v_s[0:NS, :], scalar1=self.ident[0:NS, j:j + 1], scalar2=None,
                                                      op0=ALU.mult), reads=[t_vs, self.t_const], writes=[tvm])
                pv, tpv = self.psum(6, 8)
                k.op("pe", lambda e: e.matmul(pv[:, :], lhsT=self.onesf[0:NS, :], rhs=vm[0:NS, :], start=True, stop=True),
                     reads=[tvm, self.t_const], writes=[tpv])
                for hh in range(4):
                    k.op("dve", lambda e, hh=hh: e.tensor_scalar(out=sj[:, hh, :], in0=sj[:, hh, :], scalar1=f_s[:, hh, j:j + 1],
                                                                 scalar2=None, op0=ALU.mult), reads=[tsj, t_fs], writes=[tsj])
                    k.op("dve", lambda e, hh=hh: e.scalar_tensor_tensor(out=sj[:, hh, :], in0=pv[:, hh * 128:(hh + 1) * 128],
                                                                        scalar=kk_s[:, hh, j:j + 1], in1=sj[:, hh, :],
                                                                        op0=ALU.mult, op1=ALU.add),
                         reads=[tpv, t_ks, tsj], writes=[tsj])
                k.dma("sp", ssj, lambda e: e.dma_start(out=self.hgs_out[l, j].rearrange("h k v -> k h v"), in_=sj[:, :, :]), reads=[tsj])
                for hh in range(4):
                    k.op("pe", lambda e, hh=hh: e.matmul(pso[:, hh * NS + j:hh * NS + j + 1], lhsT=sj[:, hh, :], rhs=q_s[:, hh, j:j + 1],
                                                         start=True, stop=True), reads=[tsj, t_qs], writes=[tpso], inc=(hh == 3))
            for j in range(NS):
                sample(j)
            for hh in range(4):
                onorm(hh, self.ps[5][:, hh * NS:(hh + 1) * NS], tpso, NS, NPH, 512)

    PW = [1, 2, 3, 4, 5, 6, 7, 8]

    def tab(self, l, m, which):
        return self.s5tab[:, l, self.PW.index(m), which, :]

    def setup_s5(self):
        k = self.k
        tc, sc = self.t_const, self.s_const
        nc, st = self.nc, self.st

        def sbt(name, shape):
            return st.enter_context(nc.sbuf_tensor(name, list(shape), F32))
        k.dma("sp", sc, lambda e: e.dma_start(out=self.pmask[:], in_=self.c_pmask[:, :]), writes=[tc])
        k.dma("sp", sc, lambda e: e.dma_start(out=self.bdmask[:], in_=self.c_bdmask[:, :]), writes=[tc])
        k.op("dve", lambda e: e.memset(self.halfpi[:], float(np.pi / 2)), writes=[tc])
        k.op("dve", lambda e: e.memset(self.Hprev[:], 0.0), writes=[tc])
        tg = T("s5stage")
        sg_ = k.dsem("s5stage")
        k.dma("sp", sg_, lambda e: e.dma_start(out=self.gstage[0:8, :], in_=self.s5_d.rearrange("l (c p) -> (l c) p", p=P)),
              reads=[tc], writes=[tc, tg])
        k.dma("sp", sg_, lambda e: e.dma_start(out=self.gstage[8:24, :], in_=self.glu_b.rearrange("l (c p) -> (l c) p", p=P)),
              writes=[tg])
        ps, tp = self.psum()
        k.op("pe", lambda e: e.transpose(ps[:, 0:24], self.gstage[0:24, :], self.ident[0:24, 0:24]), reads=[tc, tg], writes=[tp])
        k.op("dve", lambda e: e.tensor_copy(out=self.scol[:], in_=ps[:, 0:24]), reads=[tp], writes=[tc])
        self.arena.reset()
        W = [self.arena.f32([32]) for i in range(12)]
        lam_st = self.arena.f32([256])
        self.s5raw = self.arena.f32([DEPTH * 8 * 2, 32])
        self.s5unit = self.arena.f32([DEPTH * 7 * 2, 32])
        self.s5r8 = self.arena.f32([DEPTH, 32])

        def dv(fn):
            k.op("dve", fn, reads=[tc], writes=[tc])

        def av(fn):
            k.op("act", fn, reads=[tc], writes=[tc])

        def cmul(orr, oi, ar_, ai, br, bi, t0, t1):
            dv(lambda e: e.tensor_tensor(out=t0[:], in0=ar_, in1=br, op=ALU.mult))
            dv(lambda e: e.tensor_tensor(out=t1[:], in0=ai, in1=bi, op=ALU.mult))
            dv(lambda e: e.tensor_tensor(out=orr, in0=t0[:], in1=t1[:], op=ALU.subtract))
            dv(lambda e: e.tensor_tensor(out=t0[:], in0=ar_, in1=bi, op=ALU.mult))
            dv(lambda e: e.tensor_tensor(out=t1[:], in0=ai, in1=br, op=ALU.mult))
            dv(lambda e: e.tensor_tensor(out=oi, in0=t0[:], in1=t1[:], op=ALU.add))
        for l in range(DEPTH):
            def layer(l):
                LR, LI, DT, ER, CS, SN, T0, T1, T2, T3, PRn, PIn = W
                tl = T("lamst")
                sl_ = k.dsem(f"lamst{l}")
                for i, src in enumerate((self.lam_re, self.lam_re, self.lam_im, self.lam_im)):
                    k.dma("sp", sl_, lambda e, i=i, src=src: e.dma_start(out=lam_st[0:32, i * 64:(i + 1) * 64], in_=src[l, :, :]),
                          reads=[tc] if i == 0 else [], writes=[tc, tl] if i == 0 else [tl])
                k.dma("sp", sl_, lambda e: e.dma_start(out=DT[:], in_=self.log_dt[l:l + 1, :].partition_broadcast(P)), writes=[tl])
                ps, tp = self.psum()
                k.op("pe", lambda e: e.transpose(ps[:, 0:32], lam_st[0:32, 0:128], self.ident[0:32, 0:32]), reads=[tc, tl], writes=[tp], inc=False)
                k.op("pe", lambda e: e.transpose(ps[:, 32:64], lam_st[0:32, 128:256], self.ident[0:32, 0:32]), reads=[tc, tl], writes=[tp])
                k.op("dve", lambda e: e.tensor_copy(out=LR[:], in_=ps[:, 0:32]), reads=[tp], writes=[tc])
                k.op("dve", lambda e: e.tensor_copy(out=LI[:], in_=ps[:, 32:64]), reads=[tp], writes=[tc])
                k.op("act", lambda e: e.activation(out=DT[:], in_=DT[:], func=AF.Exp), reads=[tc, tl], writes=[tc])
                dv(lambda e: e.tensor_tensor(out=T0[:], in0=LR[:], in1=DT[:], op=ALU.mult))
                av(lambda e: e.activation(out=ER[:], in_=T0[:], func=AF.Exp))
                dv(lambda e: e.tensor_tensor(out=T0[:], in0=LI[:], in1=DT[:], op=ALU.mult))
                av(lambda e: e.activation(out=SN[:], in_=T0[:], func=AF.Sin, scale=1.0 / 16))
                av(lambda e: e.activation(out=CS[:], in_=T0[:], func=AF.Sin, scale=1.0 / 16, bias=self.halfpi[:, 0:1]))
                for _ in range(4):
                    dv(lambda e: e.tensor_tensor(out=T0[:], in0=CS[:], in1=CS[:], op=ALU.mult))
                    dv(lambda e: e.tensor_tensor(out=T1[:], in0=SN[:], in1=SN[:], op=ALU.mult))
                    dv(lambda e: e.scalar_tensor_tensor(out=SN[:], in0=CS[:], scalar=2.0, in1=SN[:], op0=ALU.mult, op1=ALU.mult))
                    dv(lambda e: e.tensor_tensor(out=CS[:], in0=T0[:], in1=T1[:], op=ALU.subtract))
                pr = {}
                pi = {}

                def store(m, prt, pit):
                    dv(lambda e: e.tensor_copy(out=self.tab(l, m, 0), in_=prt))
                    dv(lambda e: e.tensor_scalar(out=self.tab(l, m, 1), in0=pit, scalar1=self.pmask[:, 2:3], scalar2=None, op0=ALU.mult))
                def raw(m, c):
                    return self.s5raw[:, (l * 8 + self.PW.index(m)) * 2 + c, :]
                dv(lambda e: e.tensor_tensor(out=raw(1, 0), in0=ER[:], in1=CS[:], op=ALU.mult))
                dv(lambda e: e.tensor_tensor(out=raw(1, 1), in0=ER[:], in1=SN[:], op=ALU.mult))
                for m in (2, 3, 4, 5, 6, 7, 8):
                    cmul(raw(m, 0), raw(m, 1), raw(m - 1, 0), raw(m - 1, 1), raw(1, 0), raw(1, 1), T0, T1)
                for m in self.PW:
                    store(m, raw(m, 0), raw(m, 1))
                def unit(kk, c):
                    return self.s5unit[:, (l * 7 + kk) * 2 + c, :]
                dv(lambda e: e.tensor_tensor(out=T0[:], in0=LR[:], in1=DT[:], op=ALU.mult))
                av(lambda e: e.activation(out=self.s5r8[:, l, :], in_=T0[:], func=AF.Exp, scale=8.0))
                dv(lambda e: e.reciprocal(out=T1[:], in_=self.s5r8[:, l, :]))
                dv(lambda e: e.tensor_tensor(out=unit(0, 0), in0=raw(8, 0), in1=T1[:], op=ALU.mult))
                dv(lambda e: e.tensor_tensor(out=unit(0, 1), in0=raw(8, 1), in1=T1[:], op=ALU.mult))
                for kk in range(1, 7):
                    dv(lambda e, kk=kk: e.tensor_tensor(out=T0[:], in0=unit(kk - 1, 0), in1=unit(kk - 1, 0), op=ALU.mult))
                    dv(lambda e, kk=kk: e.tensor_tensor(out=T1[:], in0=unit(kk - 1, 1), in1=unit(kk - 1, 1), op=ALU.mult))
                    dv(lambda e, kk=kk: e.tensor_tensor(out=unit(kk, 0), in0=T0[:], in1=T1[:], op=ALU.subtract))
                    dv(lambda e, kk=kk: e.scalar_tensor_tensor(out=unit(kk, 1), in0=unit(kk - 1, 0), scalar=2.0, in1=unit(kk - 1, 1),
                                                               op0=ALU.mult, op1=ALU.mult))
                dv(lambda e: e.tensor_scalar(out=T2[:], in0=raw(1, 0), scalar1=-1.0, scalar2=None, op0=ALU.add))
                dv(lambda e: e.tensor_tensor(out=T0[:], in0=LR[:], in1=LR[:], op=ALU.mult))
                dv(lambda e: e.tensor_tensor(out=T1[:], in0=LI[:], in1=LI[:], op=ALU.mult))
                dv(lambda e: e.tensor_tensor(out=T3[:], in0=T0[:], in1=T1[:], op=ALU.add))
                dv(lambda e: e.reciprocal(out=T3[:], in_=T3[:]))
                dv(lambda e: e.tensor_tensor(out=T0[:], in0=T2[:], in1=LR[:], op=ALU.mult))
                dv(lambda e: e.tensor_tensor(out=T1[:], in0=raw(1, 1), in1=LI[:], op=ALU.mult))
                dv(lambda e: e.tensor_tensor(out=T0[:], in0=T0[:], in1=T1[:], op=ALU.add))
                dv(lambda e: e.tensor_tensor(out=self.s5coef[:, l, 0, :], in0=T0[:], in1=T3[:], op=ALU.mult))
                dv(lambda e: e.tensor_tensor(out=T0[:], in0=raw(1, 1), in1=LR[:], op=ALU.mult))
                dv(lambda e: e.tensor_tensor(out=T1[:], in0=T2[:], in1=LI[:], op=ALU.mult))
                dv(lambda e: e.tensor_tensor(out=T0[:], in0=T0[:], in1=T1[:], op=ALU.subtract))
                dv(lambda e: e.tensor_tensor(out=T0[:], in0=T0[:], in1=T3[:], op=ALU.mult))
                dv(lambda e: e.tensor_scalar(out=self.s5coef[:, l, 1, :], in0=T0[:], scalar1=self.pmask[:, 2:3], scalar2=None, op0=ALU.mult))
            layer(l)
        self.s5_pre = []
        for l in range(DEPTH):
            def preload(l):
                f32 = self.arena.f32
                bS = f32([32, 16]); bX = f32([32, 16]); Cn1 = f32([4, 128]); Cn2 = f32([4, 128])
                t_b = T(f"bSX{l}"); t_Cn = T(f"Cn{l}")
                sb_, sc_ = k.dsem(f"s5b{l}"), k.dsem(f"s5c{l}")
                bre = self.b_re[l].rearrange("g p h -> p g h")
                bim = self.b_im[l].rearrange("g p h -> p g h")
                for (dst, top, bot) in ((bS, bre, bim), (bX, bim, bre)):
                    k.dma("sp", sb_, lambda e, dst=dst, top=top: e.dma_start(out=dst[0:64, :, :], in_=top), writes=[t_b])
                    k.dma("sp", sb_, lambda e, dst=dst, bot=bot: e.dma_start(out=dst[64:128, :, :], in_=bot), writes=[t_b])
                cre = self.c_re[l].rearrange("(c g) h p -> (g h) c p", c=4)
                cim = self.c_im[l].rearrange("(c g) h p -> (g h) c p", c=4)
                for (dst, left, right) in ((Cn1, cre, cim), (Cn2, cim, cre)):
                    k.dma("sp", sc_, lambda e, dst=dst, left=left: e.dma_start(out=dst[:, :, 0:64], in_=left), writes=[t_Cn])
                    k.dma("sp", sc_, lambda e, dst=dst, right=right: e.dma_start(out=dst[:, :, 64:128], in_=right), writes=[t_Cn])
                self.s5_pre.append((bS, bX, Cn1, Cn2, t_b, t_Cn))
            preload(l)
        self.s5_setup_mark = self.arena.off
        for l in range(DEPTH):
            def consts_layer(l):
                k.barrier()
                self.arena.off = self.s5_setup_mark
                CT4 = self.arena.bf16([4, 5120]); t_CT = T("CTs")
                self.s5_consts(l, CT4, t_CT, self.s5_pre[l])
                k.dma("sp", self.s_s5in[2], lambda e: e.dma_start(out=self.s5scr[l].rearrange("c p f -> p c f"), in_=CT4), reads=[t_CT],
                      writes=self.t_scr[l])
            consts_layer(l)
        for l in range(DEPTH):
            def rot_layer(l):
                k.barrier()
                self.arena.off = self.s5_setup_mark
                TAB = self.arena.f32([3, 32 * 128])
                tA = self.arena.f32([32, 64])
                t_T = T("rotT")
                Tc = TAB[:, 0, :].rearrange("p (g n) -> p g n", g=32)
                Ts = TAB[:, 1, :].rearrange("p (g n) -> p g n", g=32)
                Rr = TAB[:, 2, :].rearrange("p (g n) -> p g n", g=32)

                def dv(fn):
                    k.op("dve", fn, reads=[tc, t_T], writes=[t_T])
                dv(lambda e: e.memset(Tc[:, :, 0:1], 1.0))
                dv(lambda e: e.memset(Ts[:, :, 0:1], 0.0))
                for kk in range(7):
                    def lvl(kk):
                        L = 1 << kk
                        ck = self.s5unit[:, (l * 7 + kk) * 2 + 0, :].unsqueeze(2).to_broadcast([P, 32, L])
                        sk = self.s5unit[:, (l * 7 + kk) * 2 + 1, :].unsqueeze(2).to_broadcast([P, 32, L])
                        dv(lambda e: e.tensor_tensor(out=Tc[:, :, L:2 * L], in0=Tc[:, :, 0:L], in1=ck, op=ALU.mult))
                        dv(lambda e: e.tensor_tensor(out=tA[:, :, 0:L], in0=Ts[:, :, 0:L], in1=sk, op=ALU.mult))
                        dv(lambda e: e.tensor_tensor(out=Tc[:, :, L:2 * L], in0=Tc[:, :, L:2 * L], in1=tA[:, :, 0:L], op=ALU.subtract))
                        dv(lambda e: e.tensor_tensor(out=Ts[:, :, L:2 * L], in0=Ts[:, :, 0:L], in1=ck, op=ALU.mult))
                        dv(lambda e: e.tensor_tensor(out=tA[:, :, 0:L], in0=Tc[:, :, 0:L], in1=sk, op=ALU.mult))
                        dv(lambda e: e.tensor_tensor(out=Ts[:, :, L:2 * L], in0=Ts[:, :, L:2 * L], in1=tA[:, :, 0:L], op=ALU.add))
                    lvl(kk)
                dv(lambda e: e.tensor_scalar(out=TAB[:, 1, :], in0=TAB[:, 1, :], scalar1=self.pmask[:, 3:4], scalar2=None, op0=ALU.mult))
                dv(lambda e: e.tensor_copy(out=Rr, in_=self.s5r8[:, l, :].unsqueeze(2).to_broadcast([P, 32, 128])))
                dv(lambda e: e.memset(Rr[:, :, 0:1], 0.0))
                k.dma("sp", self.s_s5in[2], lambda e: e.dma_start(out=self.s5rot[l].rearrange("p a g n -> p a (g n)"), in_=TAB), reads=[t_T],
                      writes=[self.t_rot[l]])
            rot_layer(l)

    def s5(self, l, h, cout, t_c):
        k = self.k
        ar = self.arena
        tc = self.t_const
        NCM = NPH + NS
        NB = 128
        samp = (h == 0)
        uP = ar.bf16([4, 8 * NB])
        us = ar.bf16([4, NS])
        z = ar.bf16([4, NCM])
        t_uP = [T(f"uP{i}") for i in range(4)]
        t_us = T("us")
        t_z = [[T(f"z{c4}_{ci}") for ci in range(3)] for c4 in range(4)]
        svu, tsu = self.wblock(l, 3072)

        def uproj(c4, ci):
            ps, tp = self.psum()
            self.proj_fm(svu, tsu, c4, ci, ci * 512, 512, ps, tp)
            k.op("act", lambda e: e.activation(
                out=uP[:, c4, :].rearrange("p (t n) -> p t n", t=8)[:, :, ci * 64:(ci + 1) * 64],
                in_=ps[:, 0:512].rearrange("p (n t) -> p t n", t=8), func=AF.Copy), reads=[tp], writes=[t_uP[c4]])
        for c4 in range(4):
            for ci in range(2):
                uproj(c4, ci)
        if samp:
            psU, tpsU = self.psum()
            for c4 in range(4):
                self.proj_fm(svu, tsu, c4, 2, NPH, NS, psU[:, c4 * NS:(c4 + 1) * NS], tpsU)
            k.op("act", lambda e: e.activation(out=us[:, :, :], in_=psU[:, 0:4 * NS].rearrange("p (a b) -> p a b", a=4), func=AF.Copy),
                 reads=[tpsU], writes=[t_us])
        hp_s = self.Hprev[:, l, 0, :]
        hp_x = self.Hprev[:, l, 1, :]
        hcs = ar.f32([32]); hcx = ar.f32([32]); htmp = ar.f32([32])
        t_hc = T("hc")
        if h == 1:
            k.op("dve", lambda e: e.tensor_tensor(out=hcs, in0=hp_s, in1=self.tab(l, 8, 0), op=ALU.mult), reads=[tc], writes=[t_hc])
            k.op("dve", lambda e: e.tensor_tensor(out=htmp, in0=hp_x, in1=self.tab(l, 8, 1), op=ALU.mult), reads=[tc], writes=[t_hc])
            k.op("dve", lambda e: e.tensor_tensor(out=hcs, in0=hcs, in1=htmp, op=ALU.add), reads=[t_hc], writes=[t_hc])
        H0s_all = H0x_all = Hns_all = None
        t_H0 = T("H0")
        if samp:
            H0s_all = ar.f32([NS, 32]); H0x_all = ar.f32([NS, 32]); Hns_all = ar.f32([NS, 32])
            Hst = [ar.f32([4, 128]) for _ in range(2)]
            t_Hst = T("Hst")
            for (dst, left, right) in ((Hst[0], self.sre, self.sim), (Hst[1], self.sim, self.sre)):
                k.dma("sp", self.s_s5in[2], lambda e, dst=dst, left=left: e.dma_start(
                    out=dst[:, :, 0:64], in_=left[l].rearrange("(a j4) g p -> (j4 g) a p", j4=4)), writes=[t_Hst])
                k.dma("sp", self.s_s5in[2], lambda e, dst=dst, right=right: e.dma_start(
                    out=dst[:, :, 64:128], in_=right[l].rearrange("(a j4) g p -> (j4 g) a p", j4=4)), writes=[t_Hst])
            for si, dsta in ((0, H0s_all), (1, H0x_all)):
                ps, tp = self.psum()
                for a in range(4):
                    k.op("pe", lambda e, a=a, si=si, ps=ps: e.transpose(ps[:, a * 128:(a + 1) * 128], Hst[si][:, a, :], self.ident[:]),
                         reads=[t_Hst, tc], writes=[tp], inc=(a == 3))
                k.op("dve", lambda e, ps=ps, dsta=dsta: e.tensor_copy(out=dsta.rearrange("p j g -> p (j g)"), in_=ps[:, :]),
                     reads=[tp], writes=[t_H0])
        mark = ar.off
        self._pt = {}
        k.barrier()
        args = (uP, us, z, t_uP, t_us, t_z, hcs, hcx, t_hc, H0s_all, H0x_all, Hns_all, t_H0)
        ar.off = mark
        self.s5_pass("A", l, h, 0, *args)
        for c4 in range(4):
            if c4 + 1 < 4:
                ar.off = mark
                self.s5_pass("A", l, h, c4 + 1, *args)
            ar.off = mark
            self.s5_pass("B", l, h, c4, *args)
        k.barrier()
        ar.off = mark
        if samp:
            psN, tpsN = self.psum()
            for a in range(4):
                k.op("pe", lambda e, a=a: e.transpose(psN[:, a * 128:(a + 1) * 128],
                                                      Hns_all.rearrange("p j g -> p (j g)")[:, a * 128:(a + 1) * 128], self.ident[:]),
                     reads=[t_H0, tc], writes=[tpsN], inc=(a == 3))
            k.op("dve", lambda e: e.tensor_copy(out=Hst[0].rearrange("p a c -> p (a c)"), in_=psN[:, :]), reads=[tpsN, t_Hst], writes=[t_Hst])
            so = self.s_s5s
            k.dma("sp", so, lambda e: e.dma_start(out=self.res_out[l].rearrange("(a j4) g p -> (j4 g) a p", j4=4), in_=Hst[0][:, :, 0:64]),
                  reads=[t_Hst])
            k.dma("sp", so, lambda e: e.dma_start(out=self.ims_out[l].rearrange("(a j4) g p -> (j4 g) a p", j4=4), in_=Hst[0][:, :, 64:128]),
                  reads=[t_Hst])
        k.op("dve", lambda e: e.tensor_copy(out=self.Hprev[:, l, :, :], in_=self.Hnew[:, :, :]), reads=[tc, self.t_Hnew], writes=[tc])
        if h == 1:
            psF, tpsF = self.psum()
            k.op("pe", lambda e: e.transpose(psF[0:32, 0:128], self.Hnew[:, 0, :], self.ident[:]), reads=[self.t_Hnew, tc], writes=[tpsF])
            hst = ar.f32([128]); t_hst = T("hst")
            k.op("dve", lambda e: e.tensor_copy(out=hst[0:32, :], in_=psF[0:32, 0:128]), reads=[tpsF], writes=[t_hst])
            k.dma("sp", self.s_s5p, lambda e: e.dma_start(out=self.rep_out[l, :, :], in_=hst[0:32, 0:64]), reads=[t_hst])
            k.dma("sp", self.s_s5p, lambda e: e.dma_start(out=self.imp_out[l, :, :], in_=hst[0:32, 64:128]), reads=[t_hst])
        gw = self.glu_w[l].rearrange("(kc kp) f -> kp kc f", kp=P)
        slA, tsA, ssA = self.wslot()
        svA = slA[:, 0:4 * 512].rearrange("p (kc f) -> p kc f", kc=4)
        k.dma("pool", ssA, lambda e: e.dma_start(out=svA, in_=gw[:, :, 0:512]), writes=[tsA])
        slB, tsB, ssB = self.wslot()
        svB = slB[:, 0:4 * 512].rearrange("p (kc f) -> p kc f", kc=4)
        k.dma("pool", ssB, lambda e: e.dma_start(out=svB, in_=gw[:, :, 512:1024]), writes=[tsB])

        def glu(oc, ci, c0, n):
            pA, tpA = self.psum()
            pB, tpB = self.psum()
            for kc in range(4):
                k.op("pe", lambda e, kc=kc: e.matmul(pA[:, 0:n], lhsT=svA[:, kc, oc * 128:(oc + 1) * 128], rhs=z[:, kc, c0:c0 + n],
                                                     start=(kc == 0), stop=(kc == 3)), reads=[tsA, t_z[kc][ci]], writes=[tpA], inc=(kc == 3))
            for kc in range(4):
                k.op("pe", lambda e, kc=kc: e.matmul(pB[:, 0:n], lhsT=svB[:, kc, oc * 128:(oc + 1) * 128], rhs=z[:, kc, c0:c0 + n],
                                                     start=(kc == 0), stop=(kc == 3)), reads=[tsB, t_z[kc][ci]], writes=[tpB], inc=(kc == 3))
            sg, tsg = self.nxt("sg")
            k.op("act", lambda e: e.activation(out=sg[:, 0:n], in_=pB[:, 0:n], func=AF.Sigmoid,
                                               bias=self.scol[:, 8 + l * 8 + 4 + oc:8 + l * 8 + 5 + oc]),
                 reads=[tpB, tc], writes=[tsg])
            k.op("dve", lambda e: e.scalar_tensor_tensor(out=cout[:, oc, c0:c0 + n], in0=pA[:, 0:n],
                                                         scalar=self.scol[:, 8 + l * 8 + oc:8 + l * 8 + 1 + oc],
                                                         in1=sg[:, 0:n], op0=ALU.add, op1=ALU.mult),
                 reads=[tpA, tsg, tc], writes=[t_c[ci]])
        for oc in range(4):
            for (ci, c0, n) in self.cts(h):
                glu(oc, ci, c0, n)

    def pt(self, name):
        t = self._pt.get(name)
        if t is None:
            t = self._pt[name] = T(name)
        return t

    def ct_views(self, CT):
        LSe = CT[:, 0:1024].rearrange("p (a b) -> p a b", a=8)
        LSo = CT[:, 1024:2048].rearrange("p (a b) -> p a b", a=8)
        CAb = CT[:, 2048:4096].rearrange("p (a b) -> p a b", a=8)
        KBb = CT[:, 4096:5120].rearrange("p (a b) -> p a b", a=8)
        return LSe, LSo, CAb, KBb

    def s5_consts(self, l, CT4, t_CT, pre):
        k = self.k
        ar = self.arena
        tc = self.t_const
        f32 = ar.f32
        bS, bX, Cn1, Cn2, t_b, t_Cn = pre
        Bs = f32([32, 16]); Bx = f32([32, 16]); S0 = f32([32, 16]); X0 = f32([32, 16])
        Xs = [f32([32, 16]) for _ in range(2)]; t_Xs = [T("Xs0"), T("Xs1")]
        t1 = f32([32, 16]); t2 = f32([32, 16])
        t_B = T("BsBx"); t_SX = T("S0X0"); t_t12 = T("t12")

        def bc(ap32):
            return ap32.unsqueeze(2).to_broadcast([P, 32, 16])
        C1 = bc(self.s5coef[:, l, 0, :])
        C2 = bc(self.s5coef[:, l, 1, :])

        def dv(fn, reads, writes):
            k.op("dve", fn, reads=reads, writes=writes)

        def flat(a):
            return a.rearrange("p g h -> p (g h)")
        dv(lambda e: e.tensor_tensor(out=Bs[:, :, :], in0=bS[:, :, :], in1=C1, op=ALU.mult), [t_b, tc], [t_B])
        dv(lambda e: e.tensor_tensor(out=t1[:, :, :], in0=bX[:, :, :], in1=C2, op=ALU.mult), [t_b, tc], [t_t12])
        dv(lambda e: e.tensor_tensor(out=Bs[:, :, :], in0=Bs[:, :, :], in1=t1[:, :, :], op=ALU.add), [t_B, t_t12], [t_B])
        dv(lambda e: e.tensor_tensor(out=Bx[:, :, :], in0=bX[:, :, :], in1=C1, op=ALU.mult), [t_b, tc], [t_B])
        dv(lambda e: e.tensor_tensor(out=t1[:, :, :], in0=bS[:, :, :], in1=C2, op=ALU.mult), [t_b, tc], [t_t12])
        dv(lambda e: e.tensor_tensor(out=Bx[:, :, :], in0=Bx[:, :, :], in1=t1[:, :, :], op=ALU.subtract), [t_B, t_t12], [t_B])
        for (src, dst, col) in ((Cn1, S0, 3), (Cn2, X0, 2)):
            def cstack(src, dst, col):
                psC, tpsC = self.psum()
                for c4 in range(4):
                    k.op("pe", lambda e, c4=c4: e.transpose(psC[:, c4 * 128:(c4 + 1) * 128], src[:, c4, :], self.ident[:]),
                         reads=[t_Cn, tc], writes=[tpsC], inc=(c4 == 3))
                dv(lambda e: e.tensor_scalar(out=flat(dst), in0=psC[:, :], scalar1=self.pmask[:, col:col + 1], scalar2=None, op0=ALU.mult),
                   [tpsC, tc], [t_SX])
            cstack(src, dst, col)
        for m in range(8):
            def power(m):
                xs, txs = Xs[m % 2], t_Xs[m % 2]
                if m == 0:
                    dv(lambda e: e.tensor_copy(out=xs[:, :, :], in_=Bs[:, :, :]), [t_B], [txs])
                else:
                    dv(lambda e: e.tensor_tensor(out=xs[:, :, :], in0=Bs[:, :, :], in1=bc(self.tab(l, m, 0)), op=ALU.mult), [t_B, tc], [txs])
                    dv(lambda e: e.tensor_tensor(out=t1[:, :, :], in0=Bx[:, :, :], in1=bc(self.tab(l, m, 1)), op=ALU.mult), [t_B, tc], [t_t12])
                    dv(lambda e: e.tensor_tensor(out=xs[:, :, :], in0=xs[:, :, :], in1=t1[:, :, :], op=ALU.add), [txs, t_t12], [txs])
                xs2 = flat(xs)
                s02 = flat(S0)
                psT, tpT = self.psum()
                psK, tpK = self.psum()
                for c4 in range(4):
                    k.op("pe", lambda e, c4=c4: e.transpose(psT[:, c4 * 128:(c4 + 1) * 128], xs2[:, c4 * 128:(c4 + 1) * 128], self.ident[:]),
                         reads=[txs, tc], writes=[tpT], inc=(c4 == 3))
                for c4 in range(4):
                    k.op("pe", lambda e, c4=c4: e.matmul(psK[:, c4 * 128:(c4 + 1) * 128], lhsT=xs2[:, c4 * 128:(c4 + 1) * 128],
                                                         rhs=s02[:, c4 * 128:(c4 + 1) * 128], start=True, stop=True),
                         reads=[txs, t_SX], writes=[tpK], inc=(c4 == 3))
                s8 = 7 - m
                psT3 = psT[:, :].rearrange("p (c f) -> p c f", c=4)
                psK3 = psK[:, :].rearrange("p (c f) -> p c f", c=4)
                dv(lambda e: e.tensor_scalar(out=CT4[:, :, s8 * 128:(s8 + 1) * 128], in0=psT3, scalar1=self.pmask[:, 0:1], scalar2=None,
                                             op0=ALU.mult), [tpT, tc], [t_CT])
                dv(lambda e: e.tensor_scalar(out=CT4[:, :, 1024 + s8 * 128:1024 + (s8 + 1) * 128], in0=psT3, scalar1=self.pmask[:, 1:2],
                                             scalar2=None, op0=ALU.mult), [tpT, tc], [t_CT])
                bd3 = self.bdmask[:, :].unsqueeze(1).to_broadcast([P, 4, 128])
                if m == 0:
                    t2v = flat(t2).rearrange("p (c f) -> p c f", c=4)
                    dv(lambda e: e.tensor_tensor(out=t2v, in0=psK3, in1=bd3, op=ALU.mult), [tpK, tc], [t_t12])
                    for c4 in range(4):
                        dv(lambda e, c4=c4: e.scalar_tensor_tensor(out=CT4[:, c4, 4096:4096 + 128], in0=self.ident[:, :],
                                                                   scalar=self.scol[:, l * 4 + c4:l * 4 + c4 + 1], in1=t2v[:, c4, :],
                                                                   op0=ALU.mult, op1=ALU.add), [t_t12, tc], [t_CT])
                else:
                    dv(lambda e: e.tensor_tensor(out=CT4[:, :, 4096 + m * 128:4096 + (m + 1) * 128], in0=psK3, in1=bd3, op=ALU.mult),
                       [tpK, tc], [t_CT])
            power(m)
        k.op("dve", lambda e: e.memset(CT4[:, :, 2048:4096], 0.0), writes=[t_CT])
        for m in range(1, 9):
            def capow(m):
                r = m - 1
                dv(lambda e: e.tensor_tensor(out=t1[:, :, :], in0=S0[:, :, :], in1=bc(self.tab(l, m, 0)), op=ALU.mult), [t_SX, tc], [t_t12])
                dv(lambda e: e.tensor_tensor(out=t2[:, :, :], in0=X0[:, :, :], in1=bc(self.tab(l, m, 1)), op=ALU.mult), [t_SX, tc], [t_t12])
                cav = CT4[:, :, 2048 + r * 256:2048 + (r + 1) * 256].rearrange("p c (gp two f) -> p c gp two f", two=2, f=32)
                t1v = t1.rearrange("p (c gp two) h -> p c gp two h", c=4, two=2)
                t2v_ = t2.rearrange("p (c gp two) h -> p c gp two h", c=4, two=2)
                dv(lambda e: e.tensor_tensor(out=cav[:, :, :, 0, 0:16], in0=t1v[:, :, :, 0, :], in1=t2v_[:, :, :, 0, :], op=ALU.subtract),
                   [t_t12], [t_CT])
                dv(lambda e: e.tensor_tensor(out=cav[:, :, :, 1, 16:32], in0=t1v[:, :, :, 1, :], in1=t2v_[:, :, :, 1, :], op=ALU.subtract),
                   [t_t12], [t_CT])
            capow(m)

    def s5_pass(self, part, l, h, c4, uP, us, z, t_uP, t_us, t_z, hcs, hcx, t_hc, H0s_all, H0x_all, Hns_all, t_H0):
        k = self.k
        ar = self.arena
        tc = self.t_const
        NB = 128
        samp = (h == 0)
        g0 = c4 * 8
        f32 = ar.f32
        par = c4 % 2
        LSb = ar.bf16([2048]); CKb = ar.bf16([3072])
        LSe = LSb[:, 0:1024].rearrange("p (a b) -> p a b", a=8)
        LSo = LSb[:, 1024:2048].rearrange("p (a b) -> p a b", a=8)
        CAb = CKb[:, 0:2048].rearrange("p (a b) -> p a b", a=8)
        KBb = CKb[:, 2048:3072].rearrange("p (a b) -> p a b", a=8)
        t_LS = self.pt("LSb"); t_CA = t_KB = self.pt("CKb")
        Hb2 = [ar.bf16([8, NB + 2]) for _ in range(2)]
        Hb = Hb2[par]; t_Hb = [self.pt(f"Hb{par}_{i}") for i in range(8)]
        HS = ar.f32([4, 128]); HX = ar.f32([4, 128]); WS = ar.f32([4, 128]); WX = ar.f32([4, 128]); TMP = ar.f32([4, 128])
        TBb = ar.f32([3, 512])
        t_HS, t_HX, t_HX2, t_WS, t_WX, t_TMP, t_TB = (self.pt(n_) for n_ in ("HS", "HX", "HX2", "WS", "WX", "TMPr", "TBb"))
        hl = ar.f32([4, 4]); t_hl = self.pt("hl")
        if samp:
            H0s = H0s_all[:, :, g0:g0 + 8]; H0x = H0x_all[:, :, g0:g0 + 8]; Hns = Hns_all[:, :, g0:g0 + 8]
            H0b2 = [ar.bf16([8, NS]) for _ in range(2)]
            H0b = H0b2[par]; t_H0b = self.pt(f"H0b{par}")
            tmpx = f32([NS, 8]); t_tmpx = self.pt("tmpx")
            hls = f32([8, NS]); t_hls = self.pt("hls")

        def dv(fn, reads, writes):
            k.op("dve", fn, reads=reads, writes=writes)
        if part == "A":
            k.dma("sp", self.s_s5in[0], lambda e: e.dma_start(out=LSb, in_=self.s5scr[l, c4][:, 0:2048]), reads=[self.t_scr[l][c4]],
                  writes=[t_LS])
            if samp:
                k.op("act", lambda e: e.activation(out=H0b[:, :, :], in_=H0s.rearrange("p j g -> p g j"), func=AF.Copy),
                     reads=[t_H0], writes=[t_H0b])
        else:
            k.dma("pool", self.s_s5ck, lambda e: e.dma_start(out=CKb, in_=self.s5scr[l, c4][:, 2048:5120]), reads=[self.t_scr[l][c4]],
                  writes=[t_CA])

        def lsw(g8, s8):
            q = g8 // 2
            src = LSe if g8 % 2 == 0 else LSo
            return src[32 * q:32 * q + 32, s8, :]
        if part == "A":
            for gb in range(2):
                def gbatch(gb):
                    pss_ = [self.psum(), self.psum()]
                    for gl in range(4):
                        g8 = gb * 4 + gl
                        q = g8 // 2
                        ps, tp = pss_[gl // 2]
                        for s8 in range(8):
                            k.op("pe", lambda e, gl=gl, g8=g8, q=q, s8=s8, ps=ps: e.matmul(
                                ps[:, (gl % 2) * 128:(gl % 2 + 1) * 128], lhsT=lsw(g8, s8), rhs=uP[32 * q:32 * q + 32, c4, s8 * NB:(s8 + 1) * NB],
                                start=(s8 == 0), stop=(s8 == 7), tile_position=(32 * q, 0)),
                                reads=[t_LS, t_uP[c4]], writes=[tp], inc=(s8 == 7))
                    gq = g0 + gb * 4
                    k.dma("sp", self.s_s5in[1], lambda e: e.dma_start(out=TBb.rearrange("p a (g n) -> p a g n", g=4),
                                                                      in_=self.s5rot[l][:, :, gq:gq + 4, :]),
                          reads=[self.t_rot[l]], writes=[t_TB])
                    for half_ in range(2):
                        ps, tp = pss_[half_]
                        k.op("dve", lambda e, ps=ps, half_=half_: e.tensor_copy(out=HS[:, half_ * 2:half_ * 2 + 2, :],
                                                                                  in_=ps[:, 0:256].rearrange("p (g n) -> p g n", g=2)),
                             reads=[tp], writes=[t_HS])
                    if h == 1:
                        k.op("dve", lambda e: e.tensor_tensor(out=HS[:, :, 0:1], in0=HS[:, :, 0:1],
                                                              in1=hcs[:, gq:gq + 4].unsqueeze(2), op=ALU.add),
                             reads=[t_HS, t_hc], writes=[t_HS])
                    k.dma("sp", self.s_s5x[0], lambda e: e.dma_start(out=HX[0:64, :, :], in_=HS[64:128, :, :]), reads=[t_HS], writes=[t_HX])
                    k.dma("sp", self.s_s5x[1], lambda e: e.dma_start(out=HX[64:128, :, :], in_=HS[0:64, :, :]), reads=[t_HS], writes=[t_HX2])
                    fl = lambda a: a.rearrange("p g n -> p (g n)")
                    Tc, TsS, Rr = TBb[:, 0, :], TBb[:, 1, :], TBb[:, 2, :]
                    tHX = [t_HX, t_HX2]

                    def dvb(fn, reads, writes):
                        k.op("dve", fn, reads=reads, writes=writes)
                    dvb(lambda e: e.tensor_tensor(out=fl(WS), in0=fl(HS), in1=Tc, op=ALU.mult), [t_HS, t_TB], [t_WS])
                    dvb(lambda e: e.tensor_tensor(out=fl(TMP), in0=fl(HX), in1=TsS, op=ALU.mult), tHX + [t_TB], [t_TMP])
                    dvb(lambda e: e.tensor_tensor(out=fl(WS), in0=fl(WS), in1=fl(TMP), op=ALU.add), [t_WS, t_TMP], [t_WS])
                    dvb(lambda e: e.tensor_tensor(out=fl(WX), in0=fl(HX), in1=Tc, op=ALU.mult), tHX + [t_TB], [t_WX])
                    dvb(lambda e: e.tensor_tensor(out=fl(TMP), in0=fl(HS), in1=TsS, op=ALU.mult), [t_HS, t_TB], [t_TMP])
                    dvb(lambda e: e.tensor_tensor(out=fl(WX), in0=fl(WX), in1=fl(TMP), op=ALU.subtract), [t_WX, t_TMP], [t_WX])
                    dvb(lambda e: e.tensor_tensor_scan(out=fl(HS), data0=Rr, data1=fl(WS), initial=0.0, op0=ALU.mult, op1=ALU.add),
                        [t_WS, t_TB, t_HS], [t_HS])
                    dvb(lambda e: e.tensor_tensor_scan(out=fl(HX), data0=Rr, data1=fl(WX), initial=0.0, op0=ALU.mult, op1=ALU.add),
                        [t_WX, t_TB] + tHX, tHX)
                    dvb(lambda e: e.tensor_tensor(out=fl(WS), in0=fl(HS), in1=Tc, op=ALU.mult), [t_HS, t_TB], [t_WS])
                    dvb(lambda e: e.tensor_tensor(out=fl(TMP), in0=fl(HX), in1=TsS, op=ALU.mult), tHX + [t_TB], [t_TMP])
                    g8a = gb * 4
                    dvb(lambda e: e.tensor_tensor(out=Hb[:, g8a:g8a + 4, 1:NB + 1], in0=WS[:, :, :], in1=TMP[:, :, :], op=ALU.subtract),
                        [t_WS, t_TMP], t_Hb[g8a:g8a + 4])
                    dvb(lambda e: e.tensor_tensor(out=self.Hnew[:, 0, gq:gq + 4], in0=WS[:, :, NB - 1], in1=TMP[:, :, NB - 1], op=ALU.subtract),
                        [t_WS, t_TMP], [self.t_Hnew])
                    Tc3 = Tc.rearrange("p (g n) -> p g n", g=4)
                    Ts3 = TsS.rearrange("p (g n) -> p g n", g=4)
                    dvb(lambda e: e.tensor_tensor(out=hl[:, 0, :], in0=HX[:, :, NB - 1], in1=Tc3[:, :, NB - 1], op=ALU.mult), tHX + [t_TB], [t_hl])
                    dvb(lambda e: e.tensor_tensor(out=hl[:, 1, :], in0=HS[:, :, NB - 1], in1=Ts3[:, :, NB - 1], op=ALU.mult), [t_HS, t_TB], [t_hl])
                    dvb(lambda e: e.tensor_tensor(out=self.Hnew[:, 1, gq:gq + 4], in0=hl[:, 0, :], in1=hl[:, 1, :], op=ALU.add),
                        [t_hl], [self.t_Hnew])
                    k.op("act", lambda e: e.activation(out=Hb[:, g8a:g8a + 4, 0:1], in_=self.Hprev[:, l, 0, gq:gq + 4].unsqueeze(2), func=AF.Copy),
                         reads=[tc], writes=t_Hb[g8a:g8a + 4])
                gbatch(gb)
            if samp:
                for q in range(4):
                    def sq_(q):
                        psH, tpsH = self.psum()
                        for e2 in range(2):
                            g8 = 2 * q + e2
                            k.op("pe", lambda e, g8=g8, e2=e2: e.matmul(psH[:, e2 * NS:(e2 + 1) * NS], lhsT=lsw(g8, 7), rhs=us[32 * q:32 * q + 32, c4, :],
                                                                      start=True, stop=True, tile_position=(32 * q, 0)),
                                 reads=[t_LS, t_us], writes=[tpsH], inc=(e2 == 1))
                        dv(lambda e: e.tensor_copy(out=hls[:, 2 * q:2 * q + 2, :], in_=psH[:, 0:2 * NS].rearrange("p (g j) -> p g j", g=2)),
                           [tpsH], [t_hls])
                    sq_(q)
                A1b = self.tab(l, 1, 0)[:, g0:g0 + 8].unsqueeze(1).to_broadcast([P, NS, 8])
                A2b = self.tab(l, 1, 1)[:, g0:g0 + 8].unsqueeze(1).to_broadcast([P, NS, 8])
                dv(lambda e: e.tensor_tensor(out=Hns, in0=H0s, in1=A1b, op=ALU.mult), [t_H0, tc], [t_H0])
                dv(lambda e: e.tensor_tensor(out=tmpx[:, :, :], in0=H0x, in1=A2b, op=ALU.mult), [t_H0, tc], [t_tmpx])
                dv(lambda e: e.tensor_tensor(out=Hns, in0=Hns, in1=tmpx[:, :, :], op=ALU.add), [t_H0, t_tmpx], [t_H0])
                dv(lambda e: e.tensor_tensor(out=Hns, in0=Hns, in1=hls.rearrange("p g j -> p j g"), op=ALU.add),
                   [t_H0, t_hls], [t_H0])
            return
        if self.stop and self.stop.get('s5_stop') == 2:
            return
        t_Hb_all = list(t_Hb)

        def inter(psr, r, rhs_of, last_stop=True):
            for g8 in range(8):
                q = g8 // 2
                if g8 % 2 == 0:
                    o = psr[32 * q:32 * q + 16]
                    w = CAb[:, r, g8 * 32:g8 * 32 + 16]
                else:
                    o = psr[32 * q:32 * q + 32]
                    w = CAb[:, r, g8 * 32:g8 * 32 + 32]
                yield g8, o, w, q
        for bank in range(2):
            def ybank(bank):
                ps, tp = self.psum()
                for rr in range(4):
                    r = bank * 4 + rr
                    psr = ps[:, rr * NB:(rr + 1) * NB]
                    for j in range(r + 1):
                        k.op("pe", lambda e, j=j, r=r, psr=psr: e.matmul(psr, lhsT=KBb[:, j, :], rhs=uP[:, c4, (r - j) * NB:(r - j + 1) * NB],
                                                                         start=(j == 0), stop=False),
                             reads=[t_KB, t_uP[c4]], writes=[tp], inc=False)
                    for g8, o, w, q in inter(psr, r, None):
                        k.op("pe", lambda e, g8=g8, o=o, w=w, q=q: e.matmul(o, lhsT=w, rhs=Hb[:, g8, 0:NB], start=False, stop=(g8 % 2 == 1),
                                                                            tile_position=(0, 32 * q)),
                             reads=[t_CA, t_Hb[g8]], writes=[tp], inc=(g8 == 7))
                k.op("act", lambda e: e.activation(
                    out=z[:, c4, 0:NPH].rearrange("p (n r) -> p r n", r=8)[:, bank * 4:bank * 4 + 4, :],
                    in_=ps[:, :].rearrange("p (r n) -> p r n", r=4), func=AF.Gelu_apprx_tanh),
                    reads=[tp], writes=[t_z[c4][0], t_z[c4][1]])
            ybank(bank)
        if samp:
            ps, tp = self.psum()
            psr = ps[:, 0:NS]
            k.op("pe", lambda e: e.matmul(psr, lhsT=KBb[:, 0, :], rhs=us[:, c4, :], start=True, stop=False),
                 reads=[t_KB, t_us], writes=[tp], inc=False)
            for g8, o, w, q in inter(psr, 0, None):
                k.op("pe", lambda e, g8=g8, o=o, w=w, q=q: e.matmul(o, lhsT=w, rhs=H0b[:, g8, :], start=False, stop=(g8 % 2 == 1),
                                                                    tile_position=(0, 32 * q)),
                     reads=[t_CA, t_H0b], writes=[tp], inc=(g8 == 7))
            k.op("act", lambda e: e.activation(out=z[:, c4, NPH:NPH + NS], in_=ps[:, 0:NS], func=AF.Gelu_apprx_tanh),
                 reads=[tp], writes=[t_z[c4][2]])

    def merge(self, l, h, branches, nxt=None):
        k = self.k
        ar = self.arena
        NCM = NPH + NS
        cts = self.cts(h)
        merged = ar.bf16([DC, NCM])
        t_m = [[T(f"m{dc}_{c}") for c in range(3)] for dc in range(DC)]
        ybuf = ar.f32([DC, NCM])
        t_y = [T(f"my{c}") for c in range(3)]
        acc = [ar.f32([512]) for _ in range(2)]; t_acc = [T("acc0"), T("acc1")]
        sgf = [ar.f32([512]) for _ in range(2)]; t_sgf = [T("sgf0"), T("sgf1")]
        cnt = [0, 0]
        wv = self.w_in[l].rearrange("(kc kp) f -> kp kc f", kp=P)

        def one_dc(dc):
            slA, tsA, ssA = self.wslot()
            gA = slA[:, 0:8 * 3 * 128].rearrange("p (kc n f) -> p kc n f", kc=8, n=3)
            for n in range(3):
                col = 3584 + n * D + dc * 128
                k.dma("pool", ssA, lambda e, n=n, col=col: e.dma_start(out=gA[:, :, n, :], in_=wv[:, :, col:col + 128]),
                      writes=[tsA] if n == 0 else [])
            tsA.w = (ssA, ssA.count)
            slB, tsB, ssB = self.wslot()
            wB = slB[:, 0:4 * 3 * 128].rearrange("p (kc n f) -> p kc n f", kc=4, n=3)
            for n in range(3):
                src = self.w_branch[l, n].rearrange("(kc kp) d -> kp kc d", kp=P)
                k.dma("pool", ssB, lambda e, n=n, src=src: e.dma_start(out=wB[:, :, n, :], in_=src[:, :, dc * 128:(dc + 1) * 128]),
                      writes=[tsB] if n == 0 else [])
            tsB.w = (ssB, ssB.count)

            def one_ct(ci, c0, nn):
                a, ta = acc[cnt[0] % 2], t_acc[cnt[0] % 2]
                cnt[0] += 1
                for n in range(3):
                    def one_n(n):
                        br, t_br = branches[n]
                        pg, tpg = self.psum()
                        pb, tpb = self.psum()
                        for kc in range(8):
                            k.op("pe", lambda e, kc=kc: e.matmul(pg[:, 0:nn], lhsT=gA[:, kc, n, :], rhs=self.hT[:, kc, c0:c0 + nn],
                                                                 start=(kc == 0), stop=(kc == 7)),
                                 reads=[tsA, self.t_h[ci]], writes=[tpg], inc=(kc == 7))
                        for kc in range(4):
                            k.op("pe", lambda e, kc=kc: e.matmul(pb[:, 0:nn], lhsT=wB[:, kc, n, :], rhs=br[:, kc, c0:c0 + nn],
                                                                 start=(kc == 0), stop=(kc == 3)),
                                 reads=[tsB, t_br[ci]], writes=[tpb], inc=(kc == 3))
                        sg_, tsg = sgf[cnt[1] % 2], t_sgf[cnt[1] % 2]
                        cnt[1] += 1
                        k.op("act", lambda e: e.activation(out=sg_[:, 0:nn], in_=pg[:, 0:nn], func=AF.Sigmoid), reads=[tpg], writes=[tsg])
                        if n == 0:
                            k.op("dve", lambda e: e.tensor_tensor(out=a[:, 0:nn], in0=pb[:, 0:nn], in1=sg_[:, 0:nn], op=ALU.mult),
                                 reads=[tpb, tsg], writes=[ta])
                        else:
                            k.op("dve", lambda e: e.tensor_tensor(out=sg_[:, 0:nn], in0=pb[:, 0:nn], in1=sg_[:, 0:nn], op=ALU.mult),
                                 reads=[tpb, tsg], writes=[tsg])
                            if n == 1:
                                k.op("dve", lambda e: e.tensor_tensor(out=a[:, 0:nn], in0=a[:, 0:nn], in1=sg_[:, 0:nn], op=ALU.add),
                                     reads=[ta, tsg], writes=[ta])
                            else:
                                k.op("dve", lambda e: e.tensor_tensor(out=merged[:, dc, c0:c0 + nn], in0=a[:, 0:nn], in1=sg_[:, 0:nn], op=ALU.add),
                                     reads=[ta, tsg], writes=[t_m[dc][ci]])
                    one_n(n)
            for (ci, c0, nn) in cts:
                one_ct(ci, c0, nn)
        for dc in range(DC):
            one_dc(dc)
        pss = {ci: (self.ps[5 + ci], self.t_ps[5 + ci]) for (ci, _, _) in cts}
        wo = self.w_out[l].rearrange("(kc kp) d -> kp kc d", kp=P)
        def load_wo(ob):
            slot, tsl, ssl = self.wslot()
            sv = slot[:, 0:8 * 512].rearrange("p (kc f) -> p kc f", kc=8)
            k.dma("pool", ssl, lambda e: e.dma_start(out=sv, in_=wo[:, :, ob * 512:(ob + 1) * 512]), writes=[tsl])
            return sv, tsl

        def out_tile(sv, tsl, ob, j, ci, c0, nn):
            oc = ob * 4 + j
            po, tpo = self.psum(0, 5)
            for kc in range(8):
                k.op("pe", lambda e, kc=kc: e.matmul(po[:, 0:nn], lhsT=sv[:, kc, j * 128:(j + 1) * 128], rhs=merged[:, kc, c0:c0 + nn],
                                                     start=(kc == 0), stop=(kc == 7)),
                     reads=[tsl, t_m[kc][ci]], writes=[tpo], inc=(kc == 7))
            k.op("act", lambda e: e.activation(out=ybuf[:, oc, c0:c0 + nn], in_=po[:, 0:nn], func=AF.Copy), reads=[tpo], writes=[t_y[ci]])
            sq, tsq = self.nxt("sq")
            k.op("act", lambda e: e.activation(out=sq[:, 0:nn], in_=ybuf[:, oc, c0:c0 + nn], func=AF.Square), reads=[t_y[ci]], writes=[tsq])
            pS, tpS = pss[ci]
            k.op("pe", lambda e: e.matmul(pS[:, 0:nn], lhsT=self.onesb[:], rhs=sq[:, 0:nn], start=(oc == 0), stop=(oc == DC - 1)),
                 reads=[tsq, self.t_const], writes=[tpS], inc=True)
        sv0, tsl0 = load_wo(0)
        for j in range(4):
            for (ci, c0, nn) in cts:
                out_tile(sv0, tsl0, 0, j, ci, c0, nn)
        sv1, tsl1 = load_wo(1)
        for ti, (ci, c0, nn) in enumerate(cts):
            for j in range(4):
                out_tile(sv1, tsl1, 1, j, ci, c0, nn)
            self.post_norm(l, 3, ybuf, t_y, pss, [(ci, c0, nn)], nxt, do_barrier=(ti == len(cts) - 1))

    def dump(self, name, ap, t, rows=P):
        if not self.dbg:
            return
        k = self.k
        n = int(np.prod(ap.shape[1:]))
        off = self.dbg_off
        self.dbg_off += n
        assert self.dbg_off <= self.dbg
        self.dbg_map[name] = (off, rows, tuple(ap.shape[1:]))
        dst = self.dbgst[0:rows, 0:n]
        if len(ap.shape) == 3:
            dst = dst.rearrange("p (a b) -> p a b", a=ap.shape[1])
        k.op("dve", lambda e: e.tensor_copy(out=dst, in_=ap), reads=[t], writes=[self.t_dbgst])
        k.dma("sp", self.s_dbg, lambda e: e.dma_start(out=self.dbg_out[0:rows, off:off + n], in_=self.dbgst[0:rows, 0:n]),
              reads=[self.t_dbgst])

    def wblock(self, l, col0, ncols=512):
        k = self.k
        slot, tsl, ssl = self.wslot()
        sv = slot[:, 0:8 * ncols].rearrange("p (kc f) -> p kc f", kc=8)
        wv = self.w_in[l].rearrange("(kc kp) f -> kp kc f", kp=P)
        k.dma("pool", ssl, lambda e: e.dma_start(out=sv, in_=wv[:, :, col0:col0 + ncols]), writes=[tsl])
        return sv, tsl

    def proj_fm(self, sv, tsl, j, ci, c0, n, ps, tp):
        k = self.k
        for kc in range(8):
            k.op("pe", lambda e, kc=kc: e.matmul(ps[:, 0:n], lhsT=sv[:, kc, j * 128:(j + 1) * 128], rhs=self.hT[:, kc, c0:c0 + n],
                                                 start=(kc == 0), stop=(kc == 7)),
                 reads=[tsl, self.t_h[ci]], writes=[tp], inc=(kc == 7))

    def proj_tm(self, sv, tsl, ci, t0, m, ps, tp):
        k = self.k
        for kc in range(8):
            k.op("pe", lambda e, kc=kc: e.matmul(ps[0:m, :], lhsT=self.hT[:, kc, t0:t0 + m], rhs=sv[:, kc, :],
                                                 start=(kc == 0), stop=(kc == 7)),
                 reads=[tsl, self.t_h[ci]], writes=[tp], inc=(kc == 7))

    def gmlp(self, l, h, bout, t_bout):
        k = self.k
        ar = self.arena
        NCM = NPH + NS
        vtok = ar.bf16([8, 512])
        u = ar.bf16([4, NCM])
        gt = [ar.f32([512]) for _ in range(2)]
        t_gt = [T("gt0"), T("gt1")]
        vs32 = ar.f32([512])
        t_vs = T("vs32")
        tmps = ar.f32([4, NS])
        t_tmps = T("tmps")
        t_vtok = [T(f"vtok{i}") for i in range(8)]
        t_u = [[T(f"u{j}_{c}") for c in range(3)] for j in range(4)]
        cts = self.cts(h)
        bsr = ar.f32([512])
        t_bsr = T("bsr")
        k.dma("sp", self.s_bsr, lambda e: e.dma_start(out=bsr[0:1, :].rearrange("p (g t) -> p g t", g=4), in_=self.gmlp_bs[l:l + 1, :, :]),
              writes=[t_bsr])
        sv, tsl = self.wblock(l, 2560)
        slices = [(s, s * 128, 128) for s in range(8)] + ([(8, 1024, NS)] if h == 0 else [])

        def do_slice(s, t0, m):
            ci = 2 if s == 8 else s // 4
            ps, tp = self.psum()
            self.proj_tm(sv, tsl, ci, t0, m, ps, tp)
            g, tg = gt[s % 2], t_gt[s % 2]
            st6, tst = self.nxt("st6")
            k.op("act", lambda e: e.activation(out=g[0:m, :], in_=ps[0:m, :], func=AF.Gelu_apprx_tanh), reads=[tp], writes=[tg])
            k.op("dve", lambda e: e.bn_stats(out=st6[0:m, 0:6], in_=g[0:m, :]), reads=[tg], writes=[tst])
            k.op("dve", lambda e: e.bn_aggr(out=st6[0:m, 6:8], in_=st6[0:m, 0:6]), reads=[tst], writes=[tst])
            k.op("act", lambda e: e.activation(out=st6[0:m, 7:8], in_=st6[0:m, 7:8], func=AF.Sqrt, bias=self.epsc[0:m, 0:1]),
                 reads=[tst, self.t_const], writes=[tst])
            k.op("dve", lambda e: e.reciprocal(out=st6[0:m, 7:8], in_=st6[0:m, 7:8]), reads=[tst], writes=[tst])
            k.op("dve", lambda e: e.tensor_scalar(out=g[0:m, :], in0=g[0:m, :], scalar1=st6[0:m, 6:7], scalar2=st6[0:m, 7:8],
                                                  op0=ALU.subtract, op1=ALU.mult), reads=[tg, tst], writes=[tg])
            k.op("dve", lambda e: e.tensor_tensor(out=g[0:m, :], in0=g[0:m, :], in1=self.ngB[0:m, l, :], op=ALU.mult),
                 reads=[tg, self.t_const], writes=[tg])
            if s < 8:
                k.op("dve", lambda e: e.tensor_tensor(out=vtok[0:m, s, :], in0=g[0:m, :], in1=self.nbB[0:m, l, :], op=ALU.add),
                     reads=[tg, self.t_const], writes=[t_vtok[s]])
            else:
                k.op("dve", lambda e: e.tensor_tensor(out=vs32[0:m, :], in0=g[0:m, :], in1=self.nbB[0:m, l, :], op=ALU.add),
                     reads=[tg, self.t_const], writes=[t_vs])
                k.dma("sp", self.s_vs, lambda e: e.dma_start(out=self.vs_out[l, :, :], in_=vs32[0:NS, :]), reads=[t_vs])
        for (s, t0, m) in slices:
            do_slice(s, t0, m)
        sv2, tsl2 = self.wblock(l, 2048)

        def do_u(j, ci, c0, n):
            ps, tp = self.psum()
            self.proj_fm(sv2, tsl2, j, ci, c0, n, ps, tp)
            k.op("act", lambda e: e.activation(out=u[:, j, c0:c0 + n], in_=ps[:, 0:n], func=AF.Gelu_apprx_tanh),
                 reads=[tp], writes=[t_u[j][ci]])
        for j in range(4):
            for (ci, c0, n) in cts:
                do_u(j, ci, c0, n)

        def do_mix(g, ci, c0):
            ps, tp = self.psum()
            for s4 in range(4):
                sl = ci * 4 + s4
                k.op("pe", lambda e, s4=s4, sl=sl: e.matmul(ps[:, s4 * 128:(s4 + 1) * 128], lhsT=vtok[:, sl, g * 128:(g + 1) * 128],
                                                            rhs=self.wmT[:, l, g, :], start=True, stop=False),
                     reads=[t_vtok[sl], self.t_const], writes=[tp], inc=False)
                k.op("pe", lambda e, s4=s4: e.matmul(ps[:, s4 * 128:(s4 + 1) * 128], lhsT=self.onesf[0:1, :],
                                                     rhs=bsr[0:1, g * 128:(g + 1) * 128], start=False, stop=True),
                     reads=[self.t_const, t_bsr], writes=[tp], inc=(s4 == 3))
            k.op("dve", lambda e: e.tensor_tensor(out=bout[:, g, c0:c0 + 512], in0=ps[:, :], in1=u[:, g, c0:c0 + 512], op=ALU.mult),
                 reads=[tp, t_u[g][ci]], writes=[t_bout[ci]])
        for g in range(4):
            for ci in range(2):
                do_mix(g, ci, ci * 512)
        if h == 0:
            ps, tp = self.psum()
            for g in range(4):
                k.op("pe", lambda e, g=g: e.transpose(ps[:, g * NS:(g + 1) * NS], vs32[0:NS, g * 128:(g + 1) * 128],
                                                      self.ident[0:NS, 0:NS]),
                     reads=[t_vs, self.t_const], writes=[tp], inc=(g == 3))
            for g in range(4):
                k.op("dve", lambda e, g=g: e.tensor_scalar(out=tmps[:, g, :], in0=ps[:, g * NS:(g + 1) * NS],
                                                           scalar1=self.wsc[:, l, g:g + 1], scalar2=self.bsc[:, l, g:g + 1],
                                                           op0=ALU.mult, op1=ALU.add),
                     reads=[tp, self.t_const], writes=[t_tmps])
            k.op("dve", lambda e: e.tensor_tensor(out=bout[:, :, NPH:NPH + NS], in0=tmps[:, :, :], in1=u[:, :, NPH:NPH + NS],
                                                  op=ALU.mult),
                 reads=[t_tmps] + [t_u[j][2] for j in range(4)], writes=[t_bout[2]])

    def mixer(self, l, h, pre=True, nxt=None):
        k = self.k
        if pre:
            k.barrier()
        ar = self.arena
        ar.reset()
        NCM = NPH + NS
        aout = ar.bf16([4, NCM])
        bout = ar.bf16([4, NCM])
        cout = ar.bf16([4, NCM])
        t_a = [T(f"aout{c}") for c in range(3)]
        t_b = [T(f"bout{c}") for c in range(3)]
        t_c = [T(f"cout{c}") for c in range(3)]
        if pre:
            self.prenorm(l, 2, h)
        mark = ar.off
        self.gmlp(l, h, bout, t_b)
        if self.dbg and self.stop and self.stop.get("phase") == "gmlp":
            for j in range(4):
                self.dump(f"bout{j}", bout[:, j, :], t_b[0] if False else t_b[2] if h == 0 else t_b[1])
            return
        k.barrier()
        ar.off = mark
        if not (self.stop and self.stop.get("skip_s5")):
            self.s5(l, h, cout, t_c)
            if self.dbg and self.stop and self.stop.get("phase") == "s5":
                for j in range(4):
                    self.dump(f"cout{j}", cout[:, j, :], t_c[2 if h == 0 else 1])
                return
            k.barrier()
            ar.off = mark
        self._pt = {}
        for ci in range(2):
            ar.off = mark
            self.hgrn_tile(l, h, ci, aout, t_a)
        k.barrier()
        ar.off = mark
        if h == 1:
            k.dma("sp", self.s_hgp, lambda e: e.dma_start(out=self.hgp_out[l].rearrange("h k v -> k h v"), in_=self.S32[:, l, :, :]),
                  reads=self.t_S32[l])
        if self.dbg and self.stop and self.stop.get("phase") == "hgrn":
            for j in range(4):
                self.dump(f"aout{j}", aout[:, j, :], t_a[2 if h == 0 else 1])
            return
        self.merge(l, h, [(aout, t_a), (bout, t_b), (cout, t_c)], nxt=nxt)

    def g_ap(self, l, j, dc):
        o = (l * 6 + j) * 8 + dc
        return self.gcol[:, o:o + 1]

    def load_x(self, h):
        k = self.k
        self.arena.reset()
        stage = [self.arena.f32([D]) for _ in range(4)]
        t_st = self.t_stage_in
        s_st = self.s_stage_in
        for (ci, c0, n) in self.cts(h):
            if n == 512:
                for s in range(4):
                    r0 = h * NPH + c0 + s * 128
                    k.dma("sp", s_st[s], lambda e, s=s, r0=r0: e.dma_start(out=stage[s], in_=self.xp[r0:r0 + 128, :]),
                          writes=[t_st[s]])
                for dc in range(DC):
                    ps, tp = self.psum()
                    for s in range(4):
                        k.op("pe", lambda e, s=s, dc=dc, ps=ps: e.transpose(
                            ps[:, s * 128:(s + 1) * 128], stage[s][:, dc * 128:(dc + 1) * 128], self.ident[:]),
                            reads=[t_st[s], self.t_const], writes=[tp], inc=(s == 3))
                    en = "act" if dc % 2 == 0 else "dve"
                    if en == "act":
                        k.op("act", lambda e, dc=dc, ps=ps, c0=c0: e.activation(
                            out=self.xT[:, dc, c0:c0 + 512], in_=ps[:], func=AF.Copy), reads=[tp], writes=[self.t_x[ci]])
                    else:
                        k.op("dve", lambda e, dc=dc, ps=ps, c0=c0: e.tensor_copy(
                            out=self.xT[:, dc, c0:c0 + 512], in_=ps[:]), reads=[tp], writes=[self.t_x[ci]])
            else:
                k.dma("sp", s_st[0], lambda e: e.dma_start(out=stage[0][0:NS, :], in_=self.xs[:, :]), writes=[t_st[0]])
                ps, tp = self.psum()
                for dc in range(DC):
                    k.op("pe", lambda e, dc=dc, ps=ps: e.transpose(
                        ps[:, dc * NS:(dc + 1) * NS], stage[0][0:NS, dc * 128:(dc + 1) * 128], self.ident[0:NS, 0:NS]),
                        reads=[t_st[0], self.t_const], writes=[tp], inc=(dc == DC - 1))
                k.op("dve", lambda e, ps=ps, c0=c0: e.tensor_copy(
                    out=self.xT[:, :, c0:c0 + NS], in_=ps[:, 0:DC * NS].rearrange("p (a b) -> p a b", a=DC)),
                    reads=[tp], writes=[self.t_x[ci]])

    def store_x(self, h):
        k = self.k
        self.arena.reset()
        stage = [self.arena.f32([D]) for _ in range(2)]
        t_st = self.t_stage_out
        s_st = self.s_stage_out
        si = 0
        for (ci, c0, n) in self.cts(h):
            nsl = 4 if n == 512 else 1
            rows = 128 if n == 512 else NS
            for s in range(nsl):
                stg, tst, sst = stage[si % 2], t_st[si % 2], s_st[si % 2]
                si += 1
                for half in range(2):
                    ps, tp = self.psum()
                    for q in range(4):
                        dc = half * 4 + q
                        k.op("pe", lambda e, ps=ps, q=q, dc=dc, s=s, c0=c0, rows=rows: e.transpose(
                            ps[0:rows, q * 128:(q + 1) * 128], self.xT[:, dc, c0 + s * 128:c0 + s * 128 + rows], self.ident[:]),
                            reads=[self.t_x[ci], self.t_const], writes=[tp], inc=(q == 3))
                    if half == 0:
                        k.op("act", lambda e, ps=ps, stg=stg, rows=rows: e.activation(
                            out=stg[0:rows, 0:512], in_=ps[0:rows, :], func=AF.Copy), reads=[tp], writes=[tst])
                    else:
                        k.op("dve", lambda e, ps=ps, stg=stg, rows=rows: e.tensor_copy(
                            out=stg[0:rows, 512:1024], in_=ps[0:rows, :]), reads=[tp], writes=[tst])
                if n == 512:
                    r0 = h * NPH + c0 + s * 128
                    k.dma("sp", sst, lambda e, stg=stg, r0=r0: e.dma_start(out=self.yp[r0:r0 + 128, :], in_=stg[:, :]),
                          reads=[tst])
                else:
                    k.dma("sp", sst, lambda e, stg=stg: e.dma_start(out=self.ys[:, :], in_=stg[0:NS, :]), reads=[tst])

    def rstd_from(self, ps, tp, n, width):
        k = self.k
        rs, trs = self.nxt("rs")
        k.op("act", lambda e: e.activation(out=rs[:, 0:n], in_=ps[:, 0:n], func=AF.Ln, scale=1.0 / width, bias=self.epsc[:, 0:1]),
             reads=[tp, self.t_const], writes=[trs])
        k.op("act", lambda e: e.activation(out=rs[:, 0:n], in_=rs[:, 0:n], func=AF.Exp, scale=-0.5), reads=[trs], writes=[trs])
        return rs, trs

    def prenorm(self, l, j, h):
        for (ci, c0, n) in self.cts(h):
            self.prenorm_tile(l, j, ci, c0, n)

    def prenorm_tile(self, l, j, ci, c0, n):
        k = self.k
        ps, tp = self.ps[5 + ci], self.t_ps[5 + ci]
        for dc in range(DC):
            sq, tsq = self.nxt("sq")
            k.op("act", lambda e, sq=sq, dc=dc: e.activation(out=sq[:, 0:n], in_=self.xT[:, dc, c0:c0 + n], func=AF.Square),
                 reads=[self.t_x[ci]], writes=[tsq])
            k.op("pe", lambda e, sq=sq, dc=dc: e.matmul(ps[:, 0:n], lhsT=self.onesb[:], rhs=sq[:, 0:n], start=(dc == 0), stop=(dc == DC - 1)),
                 reads=[tsq, self.t_const], writes=[tp], inc=True)
        rs, trs = self.rstd_from(ps, tp, n, D)
        for dc in range(DC):
            k.op("dve", lambda e, dc=dc: e.scalar_tensor_tensor(
                out=self.hT[:, dc, c0:c0 + n], in0=self.xT[:, dc, c0:c0 + n], scalar=self.g_ap(l, j, dc),
                in1=rs[:, 0:n], op0=ALU.mult, op1=ALU.mult),
                reads=[self.t_x[ci], trs, self.t_const], writes=[self.t_h[ci]])

    def post_norm(self, l, jpost, ybuf, t_y, pss, cts, nxt, do_barrier=True):
        k = self.k
        for (ci, c0, n) in cts:
            def post(ci, c0, n):
                pS, tpS = pss[ci]
                rs, trs = self.rstd_from(pS, tpS, n, D)
                for dc in range(DC):
                    k.op("dve", lambda e, dc=dc: e.tensor_tensor(out=ybuf[:, dc, c0:c0 + n], in0=ybuf[:, dc, c0:c0 + n], in1=rs[:, 0:n],
                                                                 op=ALU.mult), reads=[t_y[ci], trs], writes=[t_y[ci]])
                    k.op("dve", lambda e, dc=dc: e.scalar_tensor_tensor(out=self.xT[:, dc, c0:c0 + n], in0=ybuf[:, dc, c0:c0 + n],
                                                                        scalar=self.g_ap(l, jpost, dc), in1=self.xT[:, dc, c0:c0 + n],
                                                                        op0=ALU.mult, op1=ALU.add),
                         reads=[t_y[ci], self.t_const, self.t_x[ci]], writes=[self.t_x[ci]])
                if nxt is not None:
                    self.prenorm_tile(nxt[0], nxt[1], ci, c0, n)
            post(ci, c0, n)
        if not do_barrier:
            return
        if nxt is not None:
            k.barrier(exclude=("pe", "pool"))
        else:
            k.barrier()

    def ffn(self, l, i, h, pre=True, nxt=None):
        k = self.k
        cts = self.cts(h)
        NCM = NPH + NS
        self.arena.reset()
        act = self.arena.bf16([FC, NCM])
        ybuf = self.arena.f32([DC, NCM])
        t_act = [[T(f"act{fc}_{c}") for c in range(3)] for fc in range(FC)]
        t_y = [T(f"y{c}") for c in range(3)]
        jpre, jpost = (0, 1) if i == 0 else (4, 5)
        if pre:
            self.prenorm(l, jpre, h)
        wg = self.w_gate[l, i].rearrange("(kc kp) f -> kp kc f", kp=P)
        wu = self.w_up[l, i].rearrange("(kc kp) f -> kp kc f", kp=P)
        wd = self.w_down[l, i].rearrange("(fc fp) d -> fp fc d", fp=P)
        for b in range(11 if self.stop is None else self.stop.get('gu', 11)):
            slot, tsl, ssl = self.wslot()
            sv = slot[:, 0:8 * 2 * 256].rearrange("p (kc g f) -> p kc g f", kc=8, g=2)
            f0 = b * 256
            k.dma("pool", ssl, lambda e, sv=sv, f0=f0: e.dma_start(out=sv[:, :, 0, :], in_=wg[:, :, f0:f0 + 256]), writes=[tsl])
            k.dma("pool", ssl, lambda e, sv=sv, f0=f0: e.dma_start(out=sv[:, :, 1, :], in_=wu[:, :, f0:f0 + 256]), writes=[])
            tsl.w = (ssl, ssl.count)
            for jf in range(2):
                fc = 2 * b + jf
                for (ci, c0, n) in cts:
                    psg, tpg = self.psum(0, 6)
                    psu, tpu = self.psum(0, 6)
                    for kc in range(8):
                        k.op("pe", lambda e, sv=sv, kc=kc, jf=jf, psg=psg, c0=c0, n=n: e.matmul(
                            psg[:, 0:n], lhsT=sv[:, kc, 0, jf * 128:(jf + 1) * 128], rhs=self.hT[:, kc, c0:c0 + n],
                            start=(kc == 0), stop=(kc == 7)), reads=[tsl, self.t_h[ci]], writes=[tpg], inc=(kc == 7))
                    for kc in range(8):
                        k.op("pe", lambda e, sv=sv, kc=kc, jf=jf, psu=psu, c0=c0, n=n: e.matmul(
                            psu[:, 0:n], lhsT=sv[:, kc, 1, jf * 128:(jf + 1) * 128], rhs=self.hT[:, kc, c0:c0 + n],
                            start=(kc == 0), stop=(kc == 7)), reads=[tsl, self.t_h[ci]], writes=[tpu], inc=(kc == 7))
                    sg, tsg = self.nxt("sg")
                    k.op("act", lambda e, sg=sg, psg=psg, n=n: e.activation(out=sg[:, 0:n], in_=psg[:, 0:n], func=AF.Silu),
                         reads=[tpg], writes=[tsg])
                    k.op("dve", lambda e, sg=sg, psu=psu, fc=fc, c0=c0, n=n: e.tensor_tensor(
                        out=act[:, fc, c0:c0 + n], in0=psu[:, 0:n], in1=sg[:, 0:n], op=ALU.mult),
                        reads=[tpu, tsg], writes=[t_act[fc][ci]])
        pss = {ci: (self.ps[5 + ci], self.t_ps[5 + ci]) for (ci, _, _) in cts}

        def load_down(dc):
            slot, tsl, ssl = self.wslot()
            sv = slot[:, 0:22 * 128].rearrange("p (fc d) -> p fc d", fc=22)
            k.dma("pool", ssl, lambda e: e.dma_start(out=sv[:, :, :], in_=wd[:, :, dc * 128:(dc + 1) * 128]), writes=[tsl])
            return sv, tsl

        def down_tile(sv, tsl, dc, ci, c0, n):
            psd, tpd = self.psum(0, 5)
            for fc in range(FC):
                k.op("pe", lambda e, fc=fc: e.matmul(psd[:, 0:n], lhsT=sv[:, fc, :], rhs=act[:, fc, c0:c0 + n],
                                                     start=(fc == 0), stop=(fc == FC - 1)),
                     reads=[tsl, t_act[fc][ci]], writes=[tpd], inc=(fc == FC - 1))
            k.op("act", lambda e: e.activation(out=ybuf[:, dc, c0:c0 + n], in_=psd[:, 0:n], func=AF.Copy), reads=[tpd], writes=[t_y[ci]])
            sq, tsq = self.nxt("sq")
            k.op("act", lambda e: e.activation(out=sq[:, 0:n], in_=ybuf[:, dc, c0:c0 + n], func=AF.Square), reads=[t_y[ci]], writes=[tsq])
            pS, tpS = pss[ci]
            k.op("pe", lambda e: e.matmul(pS[:, 0:n], lhsT=self.onesb[:], rhs=sq[:, 0:n], start=(dc == 0), stop=(dc == DC - 1)),
                 reads=[tsq, self.t_const], writes=[tpS], inc=True)
        for dc in range(DC - 2):
            sv, tsl = load_down(dc)
            for (ci, c0, n) in cts:
                down_tile(sv, tsl, dc, ci, c0, n)
        last = [(dc,) + load_down(dc) for dc in (DC - 2, DC - 1)]
        for ti, (ci, c0, n) in enumerate(cts):
            for (dc, sv, tsl) in last:
                down_tile(sv, tsl, dc, ci, c0, n)
            self.post_norm(l, jpost, ybuf, t_y, pss, [(ci, c0, n)], nxt, do_barrier=(ti == len(cts) - 1))

    def body(self):
        k = self.k
        nc, st = self.nc, self.st
        self.epsc = st.enter_context(nc.sbuf_tensor("epsc", [P, 1], F32))
        k.op("dve", lambda e: e.memset(self.epsc[:], EPS), writes=[self.t_const])
        self.setup_consts()
        self.setup_gmlp()
        self.setup_hgrn()
        self.setup_s5()
        k.barrier()
        halves = [0, 1] if self.stop is None else self.stop.get("halves", [0, 1])
        for h in halves:
            self.load_x(h)
            k.barrier()
            for l in range(DEPTH):
                if self.stop is not None and l >= self.stop.get("layers", DEPTH):
                    break
                if self.stop is not None and self.stop.get("phase") == "io":
                    continue
                if self.stop is not None and self.stop.get("phase") == "norm":
                    self.arena.reset()
                    self.prenorm(l, 0, h)
                    continue
                chain = False
                self.ffn(l, 0, h, pre=(l == 0 or not chain), nxt=(l, 2) if chain else None)
                if self.stop is not None and self.stop.get("phase") == "ffn1":
                    continue
                self.mixer(l, h, pre=not chain, nxt=(l, 4) if chain else None)
                if self.stop is not None and self.stop.get("phase") in ("gmlp", "s5", "hgrn", "mixer"):
                    continue
                self.ffn(l, 1, h, pre=not chain, nxt=(l + 1, 0) if (chain and l + 1 < DEPTH) else None)
            k.barrier()
            self.store_x(h)
            k.barrier()
        eng = k.engs["sp"]
        waits = {}
        for so in self.out_sems:
            if so.count:
                k._need(eng, waits, (so, so.count))
        eng.ops.append((None, list(waits.values()), None))


_CACHE = {}


def get_prog(stop=None):
    key = repr(stop)
    if key not in _CACHE:
        _CACHE[key] = Prog(stop=stop)
    return _CACHE[key]


def make_in_maps(inputs):
    c = host_consts()
    maps = []
    f = lambda a: np.ascontiguousarray(np.asarray(a, dtype=np.float32))
    shared = {n: f(inputs[n]) for n in ("norm_g", "ffn_w_gate", "ffn_w_up", "ffn_w_down", "w_in", "gmlp_ws", "gmlp_bs",
                                        "gmlp_norm_g", "gmlp_norm_b", "hgrn_lb_logits", "hgrn_norm_g", "s5_lam_re", "s5_lam_im", "s5_log_dt",
                                        "s5_b_re", "s5_b_im", "s5_c_re", "s5_c_im", "s5_d", "s5_glu_w", "s5_glu_b", "w_branch", "w_out")}
    xp = f(inputs["x_prompt"])
    xs = f(inputs["x_sample"])
    for core in range(NCORES):
        m = dict(shared)
        m["xp"] = xp[core]
        m["xs"] = xs[core * NS:(core + 1) * NS, 0, :]
        m["c_ident"] = c["ident"]
        m["c_mask128"] = c["mask128"]
        m["c_scanmask"] = c["scanmask"]
        m["c_tri2"] = c["tri2"]
        m["c_pmask"] = c["pmask"]
        m["c_bdmask"] = c["bdmask"]
        m["sre"] = f(inputs["state_s5_re"])[:, core * NS:(core + 1) * NS]
        m["sim"] = f(inputs["state_s5_im"])[:, core * NS:(core + 1) * NS]
        m["shg"] = f(inputs["state_hgrn"])[:, core * NS:(core + 1) * NS]
        maps.append(m)
    return maps


def assemble(r):
    yp = np.stack([r[c]["yp"] for c in range(NCORES)], axis=0)
    ys = np.concatenate([r[c]["ys"] for c in range(NCORES)], axis=0)[:, None, :]
    hgp = np.stack([r[c]["hgp"] for c in range(NCORES)], axis=1)
    rep = np.stack([r[c]["rep"] for c in range(NCORES)], axis=1)
    imp = np.stack([r[c]["imp"] for c in range(NCORES)], axis=1)
    hgs = np.concatenate([r[c]["hgs"] for c in range(NCORES)], axis=1)
    res = np.concatenate([r[c]["res"] for c in range(NCORES)], axis=1)
    ims = np.concatenate([r[c]["ims"] for c in range(NCORES)], axis=1)
    vs = np.concatenate([r[c]["vs"] for c in range(NCORES)], axis=1)[:, :, None, :]
    return tuple(np.ascontiguousarray(a, dtype=np.float32) for a in (yp, ys, hgp, rep, imp, hgs, res, ims, vs))


def kernel(**inputs):
    prog = get_prog()
    res = run_bass_kernel_spmd(prog.nc, make_in_maps(inputs), core_ids=list(range(NCORES)))
    return assemble(res.results)
```
